# Optimizing a Trainium2 kernel written in Bass

```python
import jax, jax.numpy as jnp
from jax import lax
import numpy as np

D_MODEL = 1024
BATCH = 16
SEQ = 2048
DEPTH = 1

HEAD_DIM = 64
RWKV_HEADS = 16
RWKV_W = RWKV_HEADS * HEAD_DIM
ATT_HEADS = 16
ATT_W = ATT_HEADS * HEAD_DIM
MIX_W = RWKV_W + ATT_W
DECAY_RANK = 64
ICLR_RANK = 64
SHIFT_COLS = 3 * RWKV_W + DECAY_RANK + ICLR_RANK
RWKV_GATE0 = SHIFT_COLS
ATT0 = RWKV_GATE0 + RWKV_W
ATT_GATE0 = ATT0 + 3 * ATT_W
N_IN = ATT_GATE0 + ATT_W
DILATED_PATTERNS = ((128, 1), (512, 4), (2048, 16))
BLK = 128
RMS_EPS = 1e-5
GN_EPS = 64e-5

kernel_name = 'hybrid_rwkv7_dilated_attn'


def rms_norm(x, w):
    xf = x.astype(jnp.float32)
    y = xf * lax.rsqrt(jnp.mean(xf * xf, axis=-1, keepdims=True) + RMS_EPS)
    return y * w.astype(jnp.float32)


def rwkv7_time_mix(p_shift, gate, w0, w_up, a0, a_up, k_k, k_a, r_k, ln_x_w, ln_x_b):
    B, S, _ = p_shift.shape
    r, k, v, w_lo, a_lo = jnp.split(
        p_shift, [RWKV_W, 2 * RWKV_W, 3 * RWKV_W, 3 * RWKV_W + DECAY_RANK], axis=-1)
    w = -jax.nn.softplus(-(w0 + jnp.tanh(w_lo) @ w_up)) - 0.5
    decay = jnp.exp(-jnp.exp(w))
    a = jax.nn.sigmoid(a0 + a_lo @ a_up)
    heads = lambda t: t.reshape(B, S, RWKV_HEADS, HEAD_DIM)
    kk = heads(k * k_k)
    kk = kk / jnp.maximum(jnp.linalg.norm(kk, axis=-1, keepdims=True), 1e-12)
    k = k * (1.0 + (a - 1.0) * k_a)
    r_h, k_h, v_h, w_h, a_h = heads(r), heads(k), heads(v), heads(decay), heads(a)

    def step(state, inp):
        r_t, w_t, k_t, v_t, av_t, bv_t = inp
        sa = jnp.einsum('bhvk,bhk->bhv', state, av_t)
        state = (state * w_t[:, :, None, :] + sa[..., None] * bv_t[:, :, None, :]
                 + v_t[..., None] * k_t[:, :, None, :])
        return state, jnp.einsum('bhvk,bhk->bhv', state, r_t)

    xs = tuple(t.astype(jnp.float32).transpose(1, 0, 2, 3)
               for t in (r_h, w_h, k_h, v_h, -kk, kk * a_h))
    s0 = jnp.zeros((B, RWKV_HEADS, HEAD_DIM, HEAD_DIM), jnp.float32)
    _, y = lax.scan(step, s0, xs)
    y = y.transpose(1, 0, 2, 3)
    mu = jnp.mean(y, axis=-1, keepdims=True)
    var = jnp.mean(jnp.square(y - mu), axis=-1, keepdims=True)
    y = ((y - mu) * lax.rsqrt(var + GN_EPS)).reshape(B, S, RWKV_W) * ln_x_w + ln_x_b
    bonus = jnp.sum(r_h * k_h * r_k, axis=-1, keepdims=True) * v_h
    return (y + bonus.reshape(B, S, RWKV_W)) * jax.nn.silu(gate)


def dilated_window_attention(q, k, v, window, dilation):
    B, S, H, E = q.shape
    L = S // dilation
    steps = window // dilation
    nb = -(-L // BLK)
    Lp = nb * BLK

    def to_sub(t):
        t = t.reshape(B, L, dilation, H, E).transpose(0, 2, 3, 1, 4)
        t = jnp.pad(t, ((0, 0), (0, 0), (0, 0), (0, Lp - L), (0, 0)))
        return t.reshape(B, dilation, H, nb, BLK, E)

    def band(t):
        prev = jnp.pad(t, ((0, 0), (0, 0), (0, 0), (1, 0), (0, 0), (0, 0)))[:, :, :, :-1]
        return jnp.concatenate([prev, t], axis=4)

    qb = to_sub(q)
    kc, vc = band(to_sub(k)), band(to_sub(v))
    s = jnp.einsum('bdhnqe,bdhnke->bdhnqk', qb, kc) * (E ** -0.5)
    qi = jnp.arange(nb)[:, None, None] * BLK + jnp.arange(BLK)[None, :, None]
    ki = (jnp.arange(nb)[:, None, None] - 1) * BLK + jnp.arange(2 * BLK)[None, None, :]
    dist = qi - ki
    valid = (dist >= 0) & (dist <= steps) & (ki >= 0)
    s = jnp.where(valid, s, -jnp.inf)
    m = jnp.max(s, axis=-1, keepdims=True)
    pexp = jnp.exp(s - m)
    l = jnp.sum(pexp, axis=-1, keepdims=True)
    o = jnp.einsum('bdhnqk,bdhnke->bdhnqe', pexp, vc) / l
    lse = (m + jnp.log(l))[..., 0]
    o = o.reshape(B, dilation, H, Lp, E)[:, :, :, :L].transpose(0, 3, 1, 2, 4).reshape(B, S, H, E)
    lse = lse.reshape(B, dilation, H, Lp)[..., :L].transpose(0, 3, 1, 2).reshape(B, S, H)
    return o, lse


def mixed_dilated_attention(p_att, gate):
    B, S, _ = p_att.shape
    q, k, v = (t.reshape(B, S, ATT_HEADS, HEAD_DIM) for t in jnp.split(p_att, 3, axis=-1))
    outs, lses = [], []
    for window, dilation in DILATED_PATTERNS:
        o, lse = dilated_window_attention(q, k, v, window, dilation)
        outs.append(o)
        lses.append(lse)
    wts = jax.nn.softmax(jnp.stack(lses, axis=0), axis=0)
    o = jnp.sum(wts[..., None] * jnp.stack(outs, axis=0), axis=0)
    return o.reshape(B, S, ATT_W) * jax.nn.silu(gate)


def setup_inputs(seed: int = 0) -> dict:
    key = jax.random.key(seed)
    ks = jax.random.split(key, 15)
    nrm = lambda k, shape: jax.random.normal(k, shape, jnp.float32)
    return {
        'x': nrm(ks[0], (BATCH, SEQ, D_MODEL)),
        'norm_w': 1.0 + 0.02 * nrm(ks[1], (D_MODEL,)),
        'w_in': nrm(ks[2], (D_MODEL, N_IN)) * D_MODEL ** -0.5,
        'mu_shift': jax.random.uniform(ks[3], (SHIFT_COLS,), jnp.float32),
        'w0': jax.random.uniform(ks[4], (RWKV_W,), jnp.float32, minval=-6.0, maxval=-1.0),
        'w_up': 0.1 * nrm(ks[5], (DECAY_RANK, RWKV_W)),
        'a0': 0.1 * nrm(ks[6], (RWKV_W,)),
        'a_up': 0.1 * nrm(ks[7], (ICLR_RANK, RWKV_W)),
        'k_k': 0.85 + 0.02 * nrm(ks[8], (RWKV_W,)),
        'k_a': 1.0 + 0.02 * nrm(ks[9], (RWKV_W,)),
        'r_k': 0.1 * nrm(ks[10], (RWKV_HEADS, HEAD_DIM)),
        'ln_x_w': 1.0 + 0.02 * nrm(ks[11], (RWKV_W,)),
        'ln_x_b': 0.02 * nrm(ks[12], (RWKV_W,)),
        'w_out': nrm(ks[13], (MIX_W, D_MODEL)) * MIX_W ** -0.5,
        'final_norm_w': 1.0 + 0.02 * nrm(ks[14], (D_MODEL,)),
    }


def reference(x, norm_w, w_in, mu_shift, w0, w_up, a0, a_up, k_k, k_a, r_k,
              ln_x_w, ln_x_b, w_out, final_norm_w):
    h = x.astype(jnp.float32)
    for _ in range(DEPTH):
        xn = rms_norm(h, norm_w)
        p = jnp.einsum('bsd,dn->bsn', xn, w_in.astype(jnp.float32))
        p_tm = p[..., :SHIFT_COLS]
        prev = jnp.pad(p_tm, ((0, 0), (1, 0), (0, 0)))[:, :-1]
        p_tm = p_tm + (prev - p_tm) * mu_shift
        y_rwkv = rwkv7_time_mix(p_tm, p[..., RWKV_GATE0:ATT0], w0, w_up, a0, a_up,
                                k_k, k_a, r_k, ln_x_w, ln_x_b)
        y_att = mixed_dilated_attention(p[..., ATT0:ATT_GATE0], p[..., ATT_GATE0:])
        mix = jnp.concatenate([y_rwkv, y_att], axis=-1)
        h = h + jnp.einsum('bsm,md->bsd', mix, w_out.astype(jnp.float32))
    return rms_norm(h, final_norm_w).astype(x.dtype)
```

```python
import math
from contextlib import ExitStack

import numpy as np
import concourse.bass as bass
import concourse.mybir as mybir
from concourse.bass_utils import run_bass_kernel_spmd

F32 = mybir.dt.float32
BF16 = mybir.dt.bfloat16
AF = mybir.ActivationFunctionType
ALU = mybir.AluOpType
AX = mybir.AxisListType

NCORES = 8
NB = 2
SEQ = 2048
D = 1024
NIN = 8320
SHIFT_COLS = 3200
GATE0 = 3200
ATT0 = 4224
ATTG0 = 7296
C0 = math.exp(-0.5)
RMS_EPS = 1e-5
GN_EPS = 64e-5
PATTERNS = (1, 4, 16)


class Sched:
    def __init__(self, nc, es, ndma=12):
        self.nc = nc
        self.eng = {"pe": nc.tensor, "dve": nc.vector, "act": nc.scalar, "pool": nc.gpsimd, "sp": nc.sync}
        self.sem = {e: es.enter_context(nc.semaphore("sem_" + e)) for e in ("pe", "dve", "act", "pool")}
        self.cnt = {e: 0 for e in self.sem}
        self.dsem = [es.enter_context(nc.semaphore(f"dsem{i}")) for i in range(ndma)]
        self.dcnt = [0] * ndma
        self.dnext = 0
        self.waited = {e: {} for e in self.eng}
        self.lastw = {}
        self.readers = {}
        self.children = {}
        self.nwaits = 0
        self.nops = 0

    def _semobj(self, sk):
        return self.sem[sk[1]] if sk[0] == "e" else self.dsem[sk[1]]

    def _related(self, k):
        out = [k]
        if "/" in k:
            out.append(k.split("/")[0])
        else:
            out.extend(self.children.get(k, ()))
        return out

    def _wait(self, e, sk, val):
        if self.waited[e].get(sk, 0) < val:
            self.eng[e].wait_ge(self._semobj(sk), val)
            self.waited[e][sk] = val
            self.nwaits += 1

    stopped = False

    def op(self, e, fn, reads=(), writes=(), dma=False, inc=True):
        if self.stopped:
            return None
        deps = {}

        def add(ev):
            sk, val = ev
            if e == "pe" and sk == ("e", "pe"):
                return
            if deps.get(sk, 0) < val:
                deps[sk] = val

        for k0 in reads:
            for k in self._related(k0):
                if k in self.lastw:
                    add(self.lastw[k])
        for k0 in writes:
            for k in self._related(k0):
                if k in self.lastw:
                    add(self.lastw[k])
                for ev in self.readers.get(k, {}).items():
                    add(ev)
        for sk, val in deps.items():
            self._wait(e, sk, val)
        if dma:
            idx = self.dnext
            self.dnext = (self.dnext + 1) % len(self.dsem)
            if self.dcnt[idx] > 0:
                self._wait(e, ("d", idx), self.dcnt[idx])
            inst = fn(self.eng[e])
            inst.then_inc(self.dsem[idx], 16)
            self.dcnt[idx] += 16
            ev = (("d", idx), self.dcnt[idx])
        else:
            inst = fn(self.eng[e])
            if inc:
                inst.then_inc(self.sem[e], 1)
                self.cnt[e] += 1
                ev = (("e", e), self.cnt[e])
            else:
                ev = (("e", e), self.cnt[e] + 1)
        self.nops += 1
        for k in writes:
            if "/" in k:
                self.children.setdefault(k.split("/")[0], set()).add(k)
            self.lastw[k] = ev
            self.readers[k] = {}
        for k in reads:
            if "/" in k:
                self.children.setdefault(k.split("/")[0], set()).add(k)
            r = self.readers.setdefault(k, {})
            if r.get(ev[0], 0) < ev[1]:
                r[ev[0]] = ev[1]
        return inst

    def barrier(self):
        if self.stopped:
            return
        for e in self.eng:
            for x in self.sem:
                if self.cnt[x] > 0:
                    self._wait(e, ("e", x), self.cnt[x])
            for i in range(len(self.dsem)):
                if self.dcnt[i] > 0:
                    self._wait(e, ("d", i), self.dcnt[i])


def bcast(ap, dims):
    return bass.AP(ap.tensor, ap.offset, [list(ap.ap[0])] + [list(d) for d in dims])


class _Stop(Exception):
    pass


def build(stop=None):
    nc = bass.Bass("TRN2", target_bir_lowering=False)
    dt = lambda n, s, k="ExternalInput", d=F32: nc.dram_tensor(n, s, d, kind=k).ap()
    x = dt("x", [NB, SEQ, D])
    w_in = dt("w_in", [D, NIN])
    w_out = dt("w_out", [2 * D, D])
    lora_up = dt("lora_up", [128, 1024])
    chv_d = dt("chv", [128, 7 * 8])
    mu_d = dt("mu", [128, 25])
    normw_d = dt("normw", [128, 8])
    fnw_d = dt("fnw", [128, D])
    cst_d = dt("cst", [128, 3 * 128])
    msk_d = dt("msk", [128, 128 + 64 + 256])
    rst_d = dt("rst", [128, 512])
    y = dt("y", [NB, SEQ, D], "ExternalOutput")
    mixs = dt("mixs", [NB, 16, 128, SEQ], "Internal", BF16)
    dbg = dt("dbg", [128, 1024], "ExternalOutput") if stop else None

    es = ExitStack()
    with es:
        S = Sched(nc, es)
        _names = {}

        def sb(n, s, d=F32, st=es):
            k = _names.get(n, 0)
            _names[n] = k + 1
            return st.enter_context(nc.sbuf_tensor(n if k == 0 else f"{n}_{k}", s, d))
        ps = [es.enter_context(nc.psum_tensor(f"ps{i}", [128, 512], F32)) for i in range(8)]
        psb = [p[:, :].bitcast(BF16) for p in ps]

        xnT = sb("xnT", [128, 8, SEQ], BF16)
        chv = sb("chv_s", [128, 7, 8])
        muT = sb("mu_s", [128, 25])
        omu = sb("omu_s", [128, 25])
        omk = sb("omk_s", [128, 8])
        normw = sb("normw_s", [128, 8])
        normw_bc = sb("normw_bc", [128, 8, 128])
        fnw = sb("fnw_s", [128, D])
        cstf = sb("cstf", [128, 3 * 128])
        cst = sb("cst_s", [128, 3, 128], BF16)
        mskf = sb("mskf", [128, 448])
        maskA = sb("maskA", [64, 128], BF16)
        maskN = sb("maskN", [64, 64], BF16)
        maskT = sb("maskT", [128, 2, 2, 128], BF16)
        identP = sb("identP", [64, 64], BF16)
        rst = sb("rst_s", [128, 512])
        stg = [sb(f"stg{i}", [128, 8, 128]) for i in range(2)]
        Wb = [[sb(f"Wb{u}_{i}", [128, 8, 128], BF16) for i in range(4)] for u in range(2)]
        Wl = sb("Wl", [128, 8, 128], BF16)
        upf = sb("upf", [128, 1024])
        upb = sb("upb", [128, 1024], BF16)
        lora_bf = sb("lora_bf", [128, SEQ], BF16)
        WO = sb("WO", [128, 16, D], BF16)

        ident = cst[:, 0, :]
        blockones = cst[:, 1, :]
        ones = cst[:, 2, :]

        def dma(out, in_, reads, writes):
            return S.op("sp", lambda e: e.dma_start(out=out, in_=in_), reads, writes, dma=True)

        def mm(out, lhsT, rhs, reads, writes, start=True, stop=True, inc=True):
            return S.op("pe", lambda e: e.matmul(out, lhsT=lhsT, rhs=rhs, start=start, stop=stop),
                        reads, writes, inc=inc)

        def tr(out, in_, idn, reads, writes, inc=True):
            return S.op("pe", lambda e: e.transpose(out, in_, idn), reads, writes, inc=inc)

        def act(out, in_, func, reads, writes, scale=1.0, bias=None):
            if bias is None:
                return S.op("act", lambda e: e.activation(out=out, in_=in_, func=func, scale=scale), reads, writes)
            return S.op("act", lambda e: e.activation(out=out, in_=in_, func=func, scale=scale, bias=bias),
                        reads, writes)

        def tt(out, in0, in1, op, reads, writes, eng="dve"):
            return S.op(eng, lambda e: e.tensor_tensor(out=out, in0=in0, in1=in1, op=op), reads, writes)

        def ts(out, in0, s1, op0, reads, writes, s2=None, op1=None, eng="dve"):
            if s2 is None:
                return S.op(eng, lambda e: e.tensor_scalar(out=out, in0=in0, scalar1=s1, scalar2=None, op0=op0),
                            reads, writes)
            return S.op(eng, lambda e: e.tensor_scalar(out=out, in0=in0, scalar1=s1, scalar2=s2, op0=op0, op1=op1),
                        reads, writes)

        def stt(out, in0, scalar, in1, op0, op1, reads, writes):
            return S.op("dve", lambda e: e.scalar_tensor_tensor(out=out, in0=in0, scalar=scalar, in1=in1,
                                                                 op0=op0, op1=op1), reads, writes)

        def cp(out, in_, reads, writes, eng="dve"):
            if eng == "act":
                return act(out, in_, AF.Copy, reads, writes)
            return S.op(eng, lambda e: e.tensor_copy(out=out, in_=in_), reads, writes)

        def memset(ap, val, writes, eng="dve"):
            return S.op(eng, lambda e: e.memset(ap, val), (), writes)

        def c64(t):
            return t[:, :].rearrange("p (c t) -> p c t", t=64)

        def red(out, in_, reads, writes):
            return S.op("dve", lambda e: e.tensor_reduce(out=out, in_=in_, axis=AX.X, op=ALU.add), reads, writes)

        dma(chv[:, :, :], chv_d.rearrange("p (a b) -> p a b", b=8), (), ["chv"])
        dma(muT[:, :], mu_d[:, :], (), ["mu"])
        dma(normw[:, :], normw_d[:, :], (), ["normw"])
        dma(fnw[:, :], fnw_d[:, :], (), ["fnw"])
        dma(cstf[:, :], cst_d[:, :], (), ["cstf"])
        dma(mskf[:, :], msk_d[:, :], (), ["mskf"])
        dma(rst[:, :], rst_d[:, :], (), ["rst"])
        dma(upf[:, :], lora_up[:, :], (), ["upf"])
        cp(cst[:, :, :], cstf[:, :].rearrange("p (a b) -> p a b", b=128), ["cstf"], ["cst"])
        cp(maskA[:, :], mskf[0:64, 0:128], ["mskf"], ["maskA"])
        cp(maskN[:, :], mskf[0:64, 128:192], ["mskf"], ["maskN"])
        for j in range(2):
            cp(maskT[:, j, :, :], mskf[:, 192:448].rearrange("p (a b) -> p a b", b=128), ["mskf"], ["maskT"])
        cp(identP[:, :], cstf[0:64, 0:64], ["cstf"], ["identP"])
        cp(upb[:, :], upf[:, :], ["upf"], ["upb"])
        ts(omu[:, :], muT[:, :], -1.0, ALU.mult, ["mu"], ["omu"], s2=1.0, op1=ALU.add)
        ts(omk[:, :], chv[:, 3, :], -1.0, ALU.mult, ["chv"], ["omk"], s2=1.0, op1=ALU.add)
        cp(normw_bc[:, :, :], bcast(normw[:, :], [[1, 8], [0, 128]]), ["normw"], ["normw_bc"])

        W0, A0, KKS, KA, RK, LNW, LNB = range(7)

        stg_i = [0]

        def load_wtile(col0, dst, key):
            i = stg_i[0] % 2
            stg_i[0] += 1
            src = w_in[:, col0:col0 + 128].rearrange("(kc p) c -> p kc c", p=128)
            for h in range(2):
                dma(stg[i][:, h * 4:h * 4 + 4, :], src[:, h * 4:h * 4 + 4, :], (), [f"stg{i}/{h}"])
            tt(dst[:, :, :], stg[i][:, :, :], normw_bc[:, :, :], ALU.mult, [f"stg{i}", "normw_bc"], [key], eng="pool")

        for mt in range(16):
            i = stg_i[0] % 2
            stg_i[0] += 1
            dma(stg[i][:, :, :].rearrange("p a b -> p (a b)"), w_out[mt * 128:(mt + 1) * 128, :], (), [f"stg{i}"])
            cp(WO[:, mt, :], stg[i][:, :, :].rearrange("p a b -> p (a b)"), [f"stg{i}"], ["WO"], eng="pool")

        def proj_block(W, wkey, psi, t0):
            for kc in range(8):
                mm(ps[psi][:, :], W[:, kc, :], xnT[:, kc, t0:t0 + 512], [wkey, "xnT"], [f"ps{psi}"],
                   start=(kc == 0), stop=(kc == 7), inc=(kc == 7))

        def shift_evac(psi, mcol, prevlast, pkey, out, okey, tmp, tkey):
            p = ps[psi]
            act(tmp[:, :], p[:, :], AF.Copy, [f"ps{psi}", "omu"], [tkey], scale=omu[:, mcol:mcol + 1])
            stt(out[:, 1:512], p[:, 0:511], muT[:, mcol:mcol + 1], tmp[:, 1:512], ALU.mult, ALU.add,
                [f"ps{psi}", "mu", tkey], [okey])
            stt(out[:, 0:1], prevlast, muT[:, mcol:mcol + 1], tmp[:, 0:1], ALU.mult, ALU.add,
                [pkey, "mu", tkey], [okey])
            cp(prevlast, p[:, 511:512], [f"ps{psi}"], [pkey], eng="act")

        def checkpoint(name, src_ap, key, n):
            if stop != name:
                return
            dtile = upf
            memset(dtile[:, :], 0.0, ["upf"])
            n = min(n, 1024)
            cp(dtile[0:src_ap.shape[0], 0:n], src_ap[:, 0:n], [key], ["upf"])
            dma(dbg[:, :], dtile[:, :], ["upf"], ["dbg"])
            S.barrier()
            S.stopped = True

        for b in range(NB):
          try:
              with ExitStack() as s1:
                  xt = [sb(f"xt{i}", [128, D], F32, s1) for i in range(2)]
                  sqt = sb("sqt", [128, D], F32, s1)
                  xnb = sb("xnb", [128, D], BF16, s1)
                  ssA = sb("ssA", [128, 2], F32, s1)
                  for tti in range(16):
                      xi = tti % 2
                      dma(xt[xi][:, :], x[b, tti * 128:(tti + 1) * 128, :], (), [f"xt{xi}"])
                      act(sqt[:, :], xt[xi][:, :], AF.Square, [f"xt{xi}"], ["sqt"])
                      red(ssA[:, 0:1], sqt[:, :], ["sqt"], ["ssA"])
                      act(ssA[:, 1:2], ssA[:, 0:1], AF.Ln, ["ssA"], ["ssA1"], scale=1.0 / D, bias=RMS_EPS)
                      act(ssA[:, 1:2], ssA[:, 1:2], AF.Exp, ["ssA1"], ["ssA1"], scale=-0.5)
                      ts(xnb[:, :], xt[xi][:, :], ssA[:, 1:2], ALU.mult, [f"xt{xi}", "ssA1"], ["xnb"])
                      for kc in range(8):
                          tr(psb[0][:, kc * 128:(kc + 1) * 128], xnb[:, kc * 128:(kc + 1) * 128], ident,
                             ["xnb", "cst"], ["ps0"], inc=(kc == 7))
                      cp(xnT[:, :, tti * 128:(tti + 1) * 128], psb[0][:, :].rearrange("p (a b) -> p a b", b=128),
                         ["ps0"], ["xnT"], eng="act")
                  S.barrier()
              checkpoint("A", xnT[:, 3, :], "xnT", SEQ)

              with ExitStack() as s1:
                  lo = sb("lo", [128, 512], F32, s1)
                  lotmp = sb("lotmp", [128, 512], F32, s1)
                  plast = sb("plastl", [128, 1], F32, s1)
                  load_wtile(3072, Wl, "Wl")
                  memset(plast[:, :], 0.0, ["plastl"])
                  for tb in range(4):
                      t0 = tb * 512
                      proj_block(Wl, "Wl", 0, t0)
                      shift_evac(0, 24, plast[:, 0:1], "plastl", lo, "lo", lotmp, "lotmp")
                      act(lora_bf[0:64, t0:t0 + 512], lo[0:64, :], AF.Tanh, ["lo"], ["lora_bf"])
                      act(lora_bf[64:128, t0:t0 + 512], lo[64:128, :], AF.Copy, ["lo"], ["lora_bf"])
                  S.barrier()
              checkpoint("L", lora_bf[:, :], "lora_bf", SEQ)

              units = [("r", hp) for hp in range(8)] + [("a", hp) for hp in range(8)]

              def unit_cols(u):
                  kind, hp = u
                  if kind == "r":
                      return [hp * 128, 1024 + hp * 128, 2048 + hp * 128, GATE0 + hp * 128]
                  return [ATT0 + hp * 128, ATT0 + 1024 + hp * 128, ATT0 + 2048 + hp * 128, ATTG0 + hp * 128]

              def prefetch(ui):
                  if ui >= len(units):
                      return
                  for i, c in enumerate(unit_cols(units[ui])):
                      load_wtile(c, Wb[ui % 2][i], f"Wb{ui % 2}_{i}")

              prefetch(0)

              with ExitStack() as s1:
                  f = lambda n: sb(n, [128, 512], F32, s1)
                  R, K, TV, SIGW, AA, CS, EW, EWI, KK, RN, Bt, BON, FIN = [f(n) for n in
                      ("R", "K", "TV", "SIGW", "AA", "CS", "EW", "EWI", "KK", "RN", "Bt", "BON", "FIN")]
                  vT = sb("vT", [128, 512], BF16, s1)
                  sg = sb("sg", [128, 512], BF16, s1)
                  SQ = sb("SQ", [128, 512], BF16, s1)
                  T1 = sb("T1", [128, 512], BF16, s1)
                  mixo = sb("mixo", [128, 512], BF16, s1)
                  WCt = sb("WCt", [128, 8], F32, s1)
                  AR = sb("AR", [128, 8, 2, 64], BF16, s1)
                  BK = sb("BK", [128, 2, 512], BF16, s1)
                  BKh = sb("BKh", [128, 2, 512], BF16, s1)
                  AbT = [sb(f"AbT{j}", [64, 8, 128], BF16, s1) for j in range(2)]
                  AkT = [sb(f"AkT{j}", [64, 8, 128], BF16, s1) for j in range(2)]
                  PM = [sb(f"PM{j}", [64, 8, 128], BF16, s1) for j in range(2)]
                  NN = [sb(f"NN{j}", [64, 8, 64], BF16, s1) for j in range(2)]
                  tokB = sb("tokB", [64, 8, 128], BF16, s1)
                  tokK = sb("tokK", [64, 8, 128], BF16, s1)
                  tokV = sb("tokV", [64, 8, 128], BF16, s1)
                  X2all = sb("X2all", [64, 8, 128], F32, s1)
                  Y2all = sb("Y2all", [64, 8, 128], F32, s1)
                  KV = sb("KV", [128, 8, 64], F32, s1)
                  Ysb = sb("Ysb", [64, 8, 128], F32, s1)
                  ARs = sb("ARs", [128, 8, 2, 64], BF16, s1)
                  HT = sb("HT", [128, 64], F32, s1)
                  Xsb = sb("Xsb", [64, 128], BF16, s1)
                  Usb = sb("Usb", [64, 128], BF16, s1)
                  H32 = sb("H32", [128, 64], F32, s1)
                  Hbf2 = sb("Hbf2", [128, 2, 64], BF16, s1)
                  plr = sb("plr", [128, 3], F32, s1)
                  YSQ = sb("YSQ", [64, 1024], F32, s1)
                  ynb = sb("ynb", [64, 8, 128], BF16, s1)
                  st = sb("st", [64, 6, 16], F32, s1)

                  for ui in range(8):
                      hp = units[ui][1]
                      prefetch(ui + 1)
                      Wr, Wk, Wv, Wg = Wb[ui % 2]
                      wk = [f"Wb{ui % 2}_{i}" for i in range(4)]
                      memset(H32[:, :], 0.0, ["H32"])
                      memset(Hbf2[:, :, :], 0.0, ["Hbf2"], eng="pool")
                      memset(plr[:, :], 0.0, ["plr0", "plr1", "plr2"])
                      cv = lambda v: chv[:, v, hp:hp + 1]
                      for tb in range(4):
                          t0 = tb * 512
                          proj_block(Wr, wk[0], 0, t0)
                          proj_block(Wk, wk[1], 1, t0)
                          proj_block(Wv, wk[2], 2, t0)
                          proj_block(Wg, wk[3], 3, t0)
                          mm(ps[4][:, :], upb[0:64, hp * 128:(hp + 1) * 128], lora_bf[0:64, t0:t0 + 512],
                             ["upb", "lora_bf"], ["ps4"])
                          mm(ps[5][:, :], upb[64:128, hp * 128:(hp + 1) * 128], lora_bf[64:128, t0:t0 + 512],
                             ["upb", "lora_bf"], ["ps5"])
                          shift_evac(0, hp, plr[:, 0:1], "plr0", R, "R", R, "R")
                          shift_evac(1, 8 + hp, plr[:, 1:2], "plr1", K, "K", K, "K")
                          shift_evac(2, 16 + hp, plr[:, 2:3], "plr2", vT, "vT", TV, "TV")
                          act(sg[:, :], ps[3][:, :], AF.Silu, ["ps3"], ["sg"])
                          act(SIGW[:, :], ps[4][:, :], AF.Sigmoid, ["ps4", "chv"], ["SIGW"], bias=cv(W0))
                          act(AA[:, :], ps[5][:, :], AF.Sigmoid, ["ps5", "chv"], ["AA"], bias=cv(A0))
                          S.op("dve", lambda e: e.tensor_tensor_scan(out=CS[:, :], data0=rst[:, :], data1=SIGW[:, :],
                                                                     initial=0.0, op0=ALU.mult, op1=ALU.add),
                               ["rst", "SIGW"], ["CS"])
                          tt(SIGW[:, :], CS[:, :], SIGW[:, :], ALU.subtract, ["CS", "SIGW"], ["SIGW"])
                          act(EW[:, :], CS[:, :], AF.Exp, ["CS"], ["EW"], scale=-C0)
                          act(EWI[:, :], CS[:, :], AF.Exp, ["CS"], ["EWI"], scale=C0)
                          act(SIGW[:, :], SIGW[:, :], AF.Exp, ["SIGW"], ["SIGW"], scale=-C0)
                          cp(WCt[:, :], EW[:, 63:512:64], ["EW"], ["WCt"])
                          tt(CS[:, :].rearrange("p (c t) -> p c t", t=64), EWI[:, :].rearrange("p (c t) -> p c t", t=64),
                             bcast(WCt[:, :], [[1, 8], [0, 64]]), ALU.mult, ["EWI", "WCt"], ["CS"])
                          ts(KK[:, :], K[:, :], cv(KKS), ALU.mult, ["K", "chv"], ["KK"])
                          act(SQ[:, :], KK[:, :], AF.Square, ["KK"], ["SQ"])
                          mm(ps[6][:, :], blockones, SQ[:, :], ["cst", "SQ"], ["ps6"])
                          act(RN[:, :], ps[6][:, :], AF.Ln, ["ps6"], ["RN"])
                          act(RN[:, :], RN[:, :], AF.Exp, ["RN"], ["RN"], scale=-0.5)
                          tt(KK[:, :], KK[:, :], RN[:, :], ALU.mult, ["KK", "RN"], ["KK"])
                          tt(Bt[:, :], KK[:, :], AA[:, :], ALU.mult, ["KK", "AA"], ["Bt"])
                          ts(AA[:, :], AA[:, :], cv(KA), ALU.mult, ["AA", "chv", "omk"], ["AA"],
                             s2=omk[:, hp:hp + 1], op1=ALU.add)
                          tt(K[:, :], K[:, :], AA[:, :], ALU.mult, ["K", "AA"], ["K"])
                          stt(T1[:, :], R[:, :], cv(RK), K[:, :], ALU.mult, ALU.mult, ["R", "K", "chv"], ["T1"])
                          mm(ps[7][:, :], blockones, T1[:, :], ["cst", "T1"], ["ps7"])
                          tt(BON[:, :], ps[7][:, :], vT[:, :], ALU.mult, ["ps7", "vT"], ["BON"])
                          tt(AR[:, :, 1, :], c64(R), c64(EW), ALU.mult, ["R", "EW"], ["AR"])
                          stt(AR[:, :, 0, :], c64(KK), -1.0, c64(SIGW), ALU.mult, ALU.mult, ["KK", "SIGW"], ["AR"])
                          cp(ARs[:, :, 0, :], AR[:, :, 1, :], ["AR"], ["ARs"], eng="pool")
                          cp(ARs[:, :, 1, :], AR[:, :, 0, :], ["AR"], ["ARs"], eng="pool")
                          tt(BK[:, 0, :], Bt[:, :], EWI[:, :], ALU.mult, ["Bt", "EWI"], ["BK"])
                          tt(BK[:, 1, :], K[:, :], EWI[:, :], ALU.mult, ["K", "EWI"], ["BK"])
                          tt(BKh[:, 0, :], Bt[:, :], CS[:, :], ALU.mult, ["Bt", "CS"], ["BKh"])
                          tt(BKh[:, 1, :], K[:, :], CS[:, :], ALU.mult, ["K", "CS"], ["BKh"])

                          if ui == 0 and tb == 0:
                              checkpoint("S1", BKh[:, :, :].rearrange("p a b -> p (a b)"), "BKh", 1024)
                          for j in range(2):
                              kp = slice(j * 64, j * 64 + 64)
                              for c in range(8):
                                  cs_ = slice(c * 64, c * 64 + 64)
                                  bk = c // 4
                                  col = (c % 4) * 128
                                  mm(ps[0 + bk][0:64, col:col + 128], BK[kp, 0, cs_], AR[kp, c, :, :].rearrange("p a b -> p (a b)"),
                                     ["BK", "AR"], [f"ps{bk}"], inc=(c % 4 == 3))
                              for c in range(8):
                                  cs_ = slice(c * 64, c * 64 + 64)
                                  bk = c // 4
                                  col = (c % 4) * 128
                                  mm(ps[2 + bk][0:64, col:col + 128], BK[kp, 1, cs_], AR[kp, c, :, :].rearrange("p a b -> p (a b)"),
                                     ["BK", "AR"], [f"ps{2 + bk}"], inc=(c % 4 == 3))
                              for c in range(8):
                                  cs_ = slice(c * 64, c * 64 + 64)
                                  mm(ps[4][0:64, cs_], AR[kp, c, 0, :], BK[kp, 0, cs_], ["BK", "AR"], ["ps4"],
                                     inc=(c == 7))
                              mA = bcast(maskA[:, :], [[0, 4], [1, 128]])
                              for bk in range(2):
                                  tt(AbT[j][:, bk * 4:bk * 4 + 4, :], ps[bk][0:64, :].rearrange("p (c t) -> p c t", t=128),
                                     mA, ALU.mult, [f"ps{bk}", "maskA"], [f"AbT{j}"])
                                  tt(AkT[j][:, bk * 4:bk * 4 + 4, :],
                                     ps[2 + bk][0:64, :].rearrange("p (c t) -> p c t", t=128),
                                     mA, ALU.mult, [f"ps{2 + bk}", "maskA"], [f"AkT{j}"])
                              tt(NN[j][:, :, :], ps[4][0:64, :].rearrange("p (c t) -> p c t", t=64),
                                 bcast(maskN[:, :], [[0, 8], [1, 64]]), ALU.mult, ["ps4", "maskN"], [f"NN{j}"])
                              cp(PM[j][:, :, 0:64], bcast(identP[:, :], [[0, 8], [1, 64]]), ["identP"], [f"PM{j}"],
                                 eng="pool")
                              cp(PM[j][:, :, 64:128], AbT[j][:, :, 0:64], [f"AbT{j}"], [f"PM{j}"], eng="pool")
                              for lvl in range(6):
                                  last = lvl == 5
                                  for c in range(8):
                                      bk = c // 4
                                      col = (c % 4) * 128
                                      if last:
                                          mm(ps[5 + bk][0:64, col:col + 64], NN[j][:, c, :], PM[j][:, c, 0:64],
                                             [f"NN{j}", f"PM{j}"], [f"ps{5 + bk}"], inc=(c % 4 == 3))
                                      else:
                                          mm(ps[5 + bk][0:64, col:col + 128], NN[j][:, c, :], PM[j][:, c, :],
                                             [f"NN{j}", f"PM{j}"], [f"ps{5 + bk}"], inc=(c % 4 == 3))
                                  if not last:
                                      for c in range(8):
                                          mm(ps[7][0:64, c * 64:c * 64 + 64], PM[j][:, c, 64:128], NN[j][:, c, :],
                                             [f"NN{j}", f"PM{j}"], ["ps7"], inc=(c == 7))
                                  for bk in range(2):
                                      pv = ps[5 + bk][0:64, :].rearrange("p (c t) -> p c t", t=128)
                                      tt(PM[j][:, bk * 4:bk * 4 + 4, 0:64], pv[:, :, 0:64],
                                         PM[j][:, bk * 4:bk * 4 + 4, 0:64], ALU.add,
                                         [f"ps{5 + bk}", f"PM{j}"], [f"PM{j}"])
                                      if not last:
                                          cp(PM[j][:, bk * 4:bk * 4 + 4, 64:128], pv[:, :, 64:128],
                                             [f"ps{5 + bk}"], [f"PM{j}"], eng="act")
                                  if not last:
                                      cp(NN[j][:, :, :], ps[7][0:64, :].rearrange("p (c t) -> p c t", t=64),
                                         ["ps7"], [f"NN{j}"], eng="act")
                          for qi, (src, dst, dkey, skey) in enumerate(((BKh[:, 0, :], tokB, "tokB", "BKh"),
                                                                        (BKh[:, 1, :], tokK, "tokK", "BKh"),
                                                                        (vT[:, :], tokV, "tokV", "vT"))):
                              for c in range(8):
                                  tr(psb[qi][0:64, c * 128:(c + 1) * 128], src[:, c * 64:(c + 1) * 64], ident,
                                     [skey, "cst"], [f"ps{qi}"], inc=(c == 7))
                              cp(dst[:, :, :], psb[qi][0:64, :].rearrange("p (c t) -> p c t", t=128), [f"ps{qi}"], [dkey],
                                 eng=("act" if qi != 1 else "dve"))

                          if ui == 0 and tb == 0:
                              checkpoint("S2", PM[1][:, :, :].rearrange("p a b -> p (a b)"), "PM1", 1024)
                          for off, dst, dkey, banks in ((0, X2all, "X2all", (0, 1)), (64, Y2all, "Y2all", (2, 3))):
                              for c in range(8):
                                  bk = banks[c // 4]
                                  col = (c % 4) * 128
                                  for j in range(2):
                                      jc = slice(j * 64, j * 64 + 64)
                                      mm(ps[bk][0:64, col + j * 64:col + j * 64 + 64], AkT[j][:, c, off:off + 64],
                                         tokV[:, c, jc], [f"AkT{j}", "tokV"], [f"ps{bk}"], inc=(c % 4 == 3 and j == 1))
                              for h in range(2):
                                  cp(dst[:, h * 4:h * 4 + 4, :], ps[banks[h]][0:64, :].rearrange("p (c t) -> p c t", t=128),
                                     [f"ps{banks[h]}"], [dkey], eng=("act" if h else "dve"))
                          for c in range(8):
                              cs_ = slice(c * 64, c * 64 + 64)
                              mm(ps[4][0:64, cs_], tokK[:, c, 0:64], tokV[:, c, 0:64], ["tokK", "tokV"], ["ps4"], inc=False)
                              mm(ps[4][64:128, cs_], tokK[:, c, 64:128], tokV[:, c, 64:128], ["tokK", "tokV"], ["ps4"],
                                 inc=(c == 7))
                          cp(KV[:, :, :], ps[4][:, :].rearrange("p (c v) -> p c v", v=64), ["ps4"], ["KV"], eng="act")

                          H2 = Hbf2[:, :, :].rearrange("p a b -> p (a b)")
                          for c in range(8):
                              cs_ = slice(c * 64, c * 64 + 64)
                              mm(ps[3][:, 0:128], AR[:, c, :, :].rearrange("p a b -> p (a b)"), H2, ["AR", "Hbf2"], ["ps3"])
                              mm(ps[5][:, 0:128], ARs[:, c, :, :].rearrange("p a b -> p (a b)"), H2, ["ARs", "Hbf2"], ["ps5"])
                              tt(Xsb[:, :], ps[3][0:64, 0:128], X2all[:, c, :], ALU.add, ["ps3", "X2all"], ["Xsb"])
                              for j in range(2):
                                  jc = slice(j * 64, j * 64 + 64)
                                  mm(ps[7][0:64, jc], PM[j][:, c, 0:64], Xsb[:, jc], [f"PM{j}", "Xsb"], ["ps7"], inc=(j == 1))
                              cp(Usb[:, :], ps[7][0:64, 0:128], ["ps7"], ["Usb"], eng="act")
                              stt(HT[:, :], H32[:, :], WCt[:, c:c + 1], KV[:, c, :], ALU.mult, ALU.add,
                                  ["H32", "WCt", "KV"], ["HT"])
                              mm(ps[4][0:64, 0:64], tokB[:, c, 0:64], Usb[:, 0:64], ["tokB", "Usb"], ["ps4"], inc=False)
                              mm(ps[4][64:128, 0:64], tokB[:, c, 64:128], Usb[:, 64:128], ["tokB", "Usb"], ["ps4"])
                              tt(H32[:, :], ps[4][:, 0:64], HT[:, :], ALU.add, ["ps4", "HT"], ["H32"])
                              cp(Hbf2[0:64, 0, :], H32[0:64, :], ["H32"], ["Hbf2"], eng="act")
                              cp(Hbf2[64:128, 1, :], H32[64:128, :], ["H32"], ["Hbf2"], eng="pool")
                              tt(Ysb[:, c, :], ps[5][0:64, 0:128], Y2all[:, c, :], ALU.add, ["ps5", "Y2all"], ["Ysb"])
                              for j in range(2):
                                  jc = slice(j * 64, j * 64 + 64)
                                  mm(ps[6][0:64, jc], AbT[j][:, c, 64:128], Usb[:, jc], [f"AbT{j}", "Usb"], ["ps6"],
                                     inc=(j == 1))
                              tt(Ysb[:, c, :], ps[6][0:64, 0:128], Ysb[:, c, :], ALU.add, ["ps6", "Ysb"], ["Ysb"])

                          if ui == 0 and tb == 0:
                              checkpoint("S3", H32[:, :], "H32", 64)
                          Yv = Ysb[:, :, :].rearrange("p c (j v) -> p (c j) v", v=64)
                          red(st[:, 0, :], Yv, ["Ysb"], ["st0"])
                          act(YSQ[:, :], Ysb[:, :, :].rearrange("p c t -> p (c t)"), AF.Square, ["Ysb"], ["YSQ"])
                          red(st[:, 1, :], YSQ[:, :].rearrange("p (g v) -> p g v", v=64), ["YSQ"], ["st1"])
                          ts(st[:, 2, :], st[:, 0, :], 1.0 / 64, ALU.mult, ["st0"], ["st2"])
                          tt(st[:, 3, :], st[:, 2, :], st[:, 2, :], ALU.mult, ["st2"], ["st3"])
                          stt(st[:, 4, :], st[:, 1, :], 1.0 / 64, st[:, 3, :], ALU.mult, ALU.subtract, ["st1", "st3"], ["st4"])
                          act(st[:, 5, :], st[:, 4, :], AF.Ln, ["st4"], ["st5"], bias=GN_EPS)
                          act(st[:, 5, :], st[:, 5, :], AF.Exp, ["st5"], ["st5"], scale=-0.5)
                          tt(Yv, Yv, bcast(st[:, 2, :], [[1, 16], [0, 64]]), ALU.subtract, ["Ysb", "st2"], ["Ysb"])
                          tt(ynb[:, :, :].rearrange("p c (j v) -> p (c j) v", v=64), Yv,
                             bcast(st[:, 5, :], [[1, 16], [0, 64]]), ALU.mult, ["Ysb", "st5"], ["ynb"])
                          for c in range(8):
                              for j in range(2):
                                  jc = slice(j * 64, j * 64 + 64)
                                  mm(ps[0][jc, c * 64:c * 64 + 64], ynb[:, c, jc], identP[:, :], ["ynb", "identP"], ["ps0"],
                                     inc=(c == 7 and j == 1))
                          ts(FIN[:, :], ps[0][:, :], cv(LNW), ALU.mult, ["ps0", "chv"], ["FIN"], s2=cv(LNB), op1=ALU.add)
                          tt(FIN[:, :], FIN[:, :], BON[:, :], ALU.add, ["FIN", "BON"], ["FIN"])
                          tt(mixo[:, :], FIN[:, :], sg[:, :], ALU.mult, ["FIN", "sg"], ["mixo"])
                          dma(mixs[b, hp, :, t0:t0 + 512], mixo[:, :], ["mixo"], [f"mixs{hp}"])
                          if ui == 0 and tb == 1:
                              checkpoint("S4", mixo[:, :], "mixo", 512)
                      if ui == 0:
                          checkpoint("U1", mixo[:, :], "mixo", 512)
                  S.barrier()

              with ExitStack() as s1:
                  qz = sb("qz", [128, 2, SEQ], BF16, s1)
                  memset(qz[:, :, :], 0.0, ["qz"], eng="pool")
                  kT = sb("kT", [128, SEQ], BF16, s1)
                  vTa = sb("vTa", [128, SEQ], BF16, s1)
                  sgT = sb("sgT", [128, SEQ], BF16, s1)
                  Vp = [sb(f"Vp{p}", [128, 16, 128], BF16, s1) for p in range(3)]
                  acc = sb("acc", [128, 2, SEQ], F32, s1)
                  RL = sb("RL", [128, SEQ], F32, s1)
                  mixa = sb("mixa", [128, SEQ], BF16, s1)
                  PT = [sb(f"PT{i}", [128, 2, 2, 128], BF16, s1) for i in range(2)]

                  def toks(d, i):
                      L = SEQ // d
                      g = i * 128
                      r, l0 = g // L, g % L
                      s0 = r + d * l0
                      return slice(s0, s0 + d * 127 + 1, d)

                  for ui in range(8, 16):
                      hp = units[ui][1]
                      prefetch(ui + 1)
                      Wq, Wk_, Wv_, Wg_ = Wb[ui % 2]
                      wk = [f"Wb{ui % 2}_{i}" for i in range(4)]
                      for tb in range(4):
                          t0 = tb * 512
                          proj_block(Wq, wk[0], 0, t0)
                          act(qz[0:64, 0, t0:t0 + 512], ps[0][0:64, :], AF.Copy, ["ps0"], ["qz"], scale=0.125)
                          act(qz[64:128, 1, t0:t0 + 512], ps[0][64:128, :], AF.Copy, ["ps0"], ["qz"], scale=0.125)
                          proj_block(Wk_, wk[1], 1, t0)
                          cp(kT[:, t0:t0 + 512], ps[1][:, :], ["ps1"], ["kT"])
                          proj_block(Wv_, wk[2], 2, t0)
                          act(vTa[:, t0:t0 + 512], ps[2][:, :], AF.Copy, ["ps2"], ["vTa"])
                          proj_block(Wg_, wk[3], 3, t0)
                          act(sgT[:, t0:t0 + 512], ps[3][:, :], AF.Silu, ["ps3"], ["sgT"])
                      for p, d in enumerate(PATTERNS):
                          for half in range(2):
                              pi = 4 + (p * 2 + half) % 2
                              for ii in range(8):
                                  i = half * 8 + ii
                                  tr(psb[pi][:, ii * 128:(ii + 1) * 128], vTa[:, toks(d, i)], ident, ["vTa", "cst"],
                                     [f"ps{pi}"], inc=(ii == 7))
                              cp(Vp[p][:, half * 8:half * 8 + 8, :], psb[pi][:, :].rearrange("p (a b) -> p a b", b=128),
                                 [f"ps{pi}"], [f"Vp{p}"], eng=("act" if half else "dve"))
                      blk = 0
                      for p, d in enumerate(PATTERNS):
                          L = SEQ // d
                          for i in range(16):
                              has_prev = ((i * 128) % L) != 0
                              kbs = ([(0, i - 1)] if has_prev else []) + [(1, i)]
                              si = blk % 2
                              blk += 1
                              sv = ps[si][:, :].rearrange("p (j k q) -> p j k q", j=2, k=2)
                              pt = PT[si]
                              for j in range(2):
                                  kp = slice(j * 64, j * 64 + 64)
                                  for (kbi, kt) in kbs:
                                      mm(sv[:, j, kbi, :], kT[:, toks(d, kt)], qz[:, j, toks(d, i)], ["kT", "qz"],
                                         [f"ps{si}"], inc=(j == 1 and kbi == 1))
                              k0 = kbs[0][0]
                              act(pt[:, :, k0:2, :], sv[:, :, k0:2, :], AF.Exp, [f"ps{si}"], [f"PT{si}"])
                              tt(pt[:, :, k0:2, :], pt[:, :, k0:2, :], maskT[:, :, k0:2, :], ALU.mult,
                                 [f"PT{si}", "maskT"], [f"PT{si}"], eng="pool")
                              ov = ps[2 + si][:, :].rearrange("p (r q) -> p r q", q=128)
                              for rgn in range(4):
                                  j = rgn % 2
                                  for n_, (kbi, kt) in enumerate(kbs):
                                      lhs = Vp[p][:, kt, :] if rgn < 2 else ones
                                      mm(ov[:, rgn, :], lhs, pt[:, j, kbi, :], [f"Vp{p}", "cst", f"PT{si}"], [f"ps{2 + si}"],
                                         start=(n_ == 0), stop=(n_ == len(kbs) - 1),
                                         inc=(rgn == 3 and n_ == len(kbs) - 1))
                              for j in range(2):
                                  kp = slice(j * 64, j * 64 + 64)
                                  src = ov[kp, j:4:2, :]
                                  dst = acc[kp, :, toks(d, i)]
                                  if p == 0:
                                      cp(dst, src, [f"ps{2 + si}"], [f"acc{j}"], eng=("act" if j else "dve"))
                                  else:
                                      tt(dst, src, dst, ALU.add, [f"ps{2 + si}", f"acc{j}"], [f"acc{j}"])
                      act(RL[:, :], acc[:, 1, :], AF.Ln, ["acc0", "acc1"], ["RL"])
                      act(RL[:, :], RL[:, :], AF.Exp, ["RL"], ["RL"], scale=-1.0)
                      tt(acc[:, 0, :], acc[:, 0, :], RL[:, :], ALU.mult, ["acc0", "acc1", "RL"], ["acc0", "acc1"])
                      tt(mixa[:, :], acc[:, 0, :], sgT[:, :], ALU.mult, ["acc0", "acc1", "sgT"], ["mixa"])
                      dma(mixs[b, 8 + hp, :, :], mixa[:, :], ["mixa"], [f"mixs{8 + hp}"])
                      if ui == 8:
                          checkpoint("T1", mixa[:, :], "mixa", SEQ)
                  S.barrier()

              with ExitStack() as s1:
                  xt = [sb(f"xo{i}", [128, D], F32, s1) for i in range(2)]
                  MT = [sb(f"MT{i}", [128, 16, 128], BF16, s1) for i in range(2)]
                  hT = sb("hT", [128, D], F32, s1)
                  sqo = sb("sqo", [128, D], F32, s1)
                  oT = [sb(f"oT{i}", [128, D], F32, s1) for i in range(2)]
                  sso = sb("sso", [128, 2], F32, s1)
                  allmix = [f"mixs{m}" for m in range(16)]
                  for tti in range(16):
                      xi = tti % 2
                      tsl = slice(tti * 128, (tti + 1) * 128)
                      dma(xt[xi][:, :], x[b, tsl, :], (), [f"xo{xi}"])
                      for q4 in range(4):
                          dma(MT[xi][:, q4 * 4:q4 * 4 + 4, :], mixs[b, q4 * 4:q4 * 4 + 4, :, tsl].rearrange("m p t -> p m t"),
                              allmix[q4 * 4:q4 * 4 + 4], [f"MT{xi}/{q4}"])
                      for half in range(2):
                          for mt in range(16):
                              mm(ps[6 + half][:, :], MT[xi][:, mt, :], WO[:, mt, half * 512:(half + 1) * 512],
                                 [f"MT{xi}", "WO"], [f"ps{6 + half}"], start=(mt == 0), stop=(mt == 15), inc=(mt == 15))
                          tt(hT[:, half * 512:(half + 1) * 512], ps[6 + half][:, :], xt[xi][:, half * 512:(half + 1) * 512],
                             ALU.add, [f"ps{6 + half}", f"xo{xi}"], ["hT"])
                      act(sqo[:, :], hT[:, :], AF.Square, ["hT"], ["sqo"])
                      red(sso[:, 0:1], sqo[:, :], ["sqo"], ["sso"])
                      act(sso[:, 1:2], sso[:, 0:1], AF.Ln, ["sso"], ["sso1"], scale=1.0 / D, bias=RMS_EPS)
                      act(sso[:, 1:2], sso[:, 1:2], AF.Exp, ["sso1"], ["sso1"], scale=-0.5)
                      stt(oT[xi][:, :], hT[:, :], sso[:, 1:2], fnw[:, :], ALU.mult, ALU.mult, ["hT", "sso1", "fnw"],
                          [f"oT{xi}"])
                      dma(y[b, tsl, :], oT[xi][:, :], [f"oT{xi}"], [f"y{b}_{tti}"])
                  S.barrier()
          except _Stop:
            break
        S.barrier()
        print(f"[kernel] ops={S.nops} waits={S.nwaits} cnt={S.cnt}")
    return nc


def _consts():
    ident = np.eye(128, dtype=np.float32)
    blockones = np.zeros((128, 128), np.float32)
    blockones[0:64, 0:64] = 1.0
    blockones[64:128, 64:128] = 1.0
    ones = np.ones((128, 128), np.float32)
    cst = np.concatenate([ident, blockones, ones], axis=1)
    s = np.arange(64)[:, None]
    t = np.arange(64)[None, :]
    maskA = np.concatenate([(s < t), (s <= t)], axis=1).astype(np.float32)
    maskN = (np.arange(64)[:, None] > np.arange(64)[None, :]).astype(np.float32)
    k = np.arange(128)[:, None]
    q = np.arange(128)[None, :]
    mprev = (k >= q).astype(np.float32)
    mdiag = (k <= q).astype(np.float32)
    msk = np.zeros((128, 448), np.float32)
    msk[0:64, 0:128] = maskA
    msk[0:64, 128:192] = maskN
    msk[:, 192:320] = mprev
    msk[:, 320:448] = mdiag
    rst = np.ones((128, 512), np.float32)
    rst[:, 0::64] = 0.0
    return cst, msk, rst


_NC_CACHE = {}


def kernel(x, norm_w, w_in, mu_shift, w0, w_up, a0, a_up, k_k, k_a, r_k, ln_x_w, ln_x_b, w_out, final_norm_w):
    f = lambda a: np.ascontiguousarray(np.asarray(a, dtype=np.float32))
    x = f(x)
    cst, msk, rst = _consts()
    pc = lambda v: f(v).reshape(8, 128).T
    chv = np.ascontiguousarray(np.stack([pc(w0), pc(a0), pc(k_k), pc(k_a), pc(np.asarray(r_k).reshape(-1)),
                                         pc(ln_x_w), pc(ln_x_b)], axis=1).reshape(128, 56))
    mu = np.ascontiguousarray(f(mu_shift).reshape(25, 128).T)
    normw = np.ascontiguousarray(f(norm_w).reshape(8, 128).T)
    fnw = np.ascontiguousarray(np.broadcast_to(f(final_norm_w)[None, :], (128, D)))
    lora_up = np.ascontiguousarray(np.concatenate([f(w_up), f(a_up)], axis=0))
    shared = {"w_in": f(w_in), "w_out": f(w_out), "lora_up": lora_up, "chv": chv, "mu": mu, "normw": normw,
              "fnw": fnw, "cst": cst, "msk": msk, "rst": rst}
    if "nc" not in _NC_CACHE:
        _NC_CACHE["nc"] = build()
    nc = _NC_CACHE["nc"]
    in_maps = [dict(shared, x=np.ascontiguousarray(x[c * NB:(c + 1) * NB])) for c in range(NCORES)]
    res = run_bass_kernel_spmd(nc, in_maps, core_ids=list(range(NCORES)))
    return np.concatenate([r["y"] for r in res.results], axis=0).astype(np.float32)
```

```python
import math
from contextlib import ExitStack

import numpy as np
import concourse.bass as bass
import concourse.mybir as mybir
from concourse.bass_utils import run_bass_kernel_spmd

F32 = mybir.dt.float32
BF16 = mybir.dt.bfloat16
AF = mybir.ActivationFunctionType
ALU = mybir.AluOpType
AX = mybir.AxisListType

NCORES = 8
NB = 2
SEQ = 2048
D = 1024
NIN = 8320
SHIFT_COLS = 3200
GATE0 = 3200
ATT0 = 4224
ATTG0 = 7296
C0 = math.exp(-0.5)
RMS_EPS = 1e-5
GN_EPS = 64e-5
PATTERNS = (1, 4, 16)


class Sched:
    def __init__(self, nc, es, ndma=12):
        self.nc = nc
        self.eng = {"pe": nc.tensor, "dve": nc.vector, "act": nc.scalar, "pool": nc.gpsimd, "sp": nc.sync}
        self.sem = {e: es.enter_context(nc.semaphore("sem_" + e)) for e in ("pe", "dve", "act", "pool")}
        self.cnt = {e: 0 for e in self.sem}
        self.dsem = [es.enter_context(nc.semaphore(f"dsem{i}")) for i in range(ndma)]
        self.dcnt = [0] * ndma
        self.dnext = 0
        self.waited = {e: {} for e in self.eng}
        self.lastw = {}
        self.readers = {}
        self.children = {}
        self.nwaits = 0
        self.nops = 0

    def _semobj(self, sk):
        return self.sem[sk[1]] if sk[0] == "e" else self.dsem[sk[1]]

    def _related(self, k):
        out = [k]
        if "/" in k:
            out.append(k.split("/")[0])
        else:
            out.extend(self.children.get(k, ()))
        return out

    def _wait(self, e, sk, val):
        if self.waited[e].get(sk, 0) < val:
            self.eng[e].wait_ge(self._semobj(sk), val)
            self.waited[e][sk] = val
            self.nwaits += 1

    stopped = False

    def op(self, e, fn, reads=(), writes=(), dma=False, inc=True):
        if self.stopped:
            return None
        deps = {}

        def add(ev):
            sk, val = ev
            if e == "pe" and sk == ("e", "pe"):
                return
            if deps.get(sk, 0) < val:
                deps[sk] = val

        for k0 in reads:
            for k in self._related(k0):
                if k in self.lastw:
                    add(self.lastw[k])
        for k0 in writes:
            for k in self._related(k0):
                if k in self.lastw:
                    add(self.lastw[k])
                for ev in self.readers.get(k, {}).items():
                    add(ev)
        for sk, val in deps.items():
            self._wait(e, sk, val)
        if dma:
            idx = self.dnext
            self.dnext = (self.dnext + 1) % len(self.dsem)
            if self.dcnt[idx] > 0:
                self._wait(e, ("d", idx), self.dcnt[idx])
            inst = fn(self.eng[e])
            inst.then_inc(self.dsem[idx], 16)
            self.dcnt[idx] += 16
            ev = (("d", idx), self.dcnt[idx])
        else:
            inst = fn(self.eng[e])
            if inc:
                inst.then_inc(self.sem[e], 1)
                self.cnt[e] += 1
                ev = (("e", e), self.cnt[e])
            else:
                ev = (("e", e), self.cnt[e] + 1)
        self.nops += 1
        for k in writes:
            if "/" in k:
                self.children.setdefault(k.split("/")[0], set()).add(k)
            self.lastw[k] = ev
            self.readers[k] = {}
        for k in reads:
            if "/" in k:
                self.children.setdefault(k.split("/")[0], set()).add(k)
            r = self.readers.setdefault(k, {})
            if r.get(ev[0], 0) < ev[1]:
                r[ev[0]] = ev[1]
        return inst

    def barrier(self):
        if self.stopped:
            return
        for e in self.eng:
            for x in self.sem:
                if self.cnt[x] > 0:
                    self._wait(e, ("e", x), self.cnt[x])
            for i in range(len(self.dsem)):
                if self.dcnt[i] > 0:
                    self._wait(e, ("d", i), self.dcnt[i])


def bcast(ap, dims):
    return bass.AP(ap.tensor, ap.offset, [list(ap.ap[0])] + [list(d) for d in dims])


class _Stop(Exception):
    pass


def build(stop=None):
    nc = bass.Bass("TRN2", target_bir_lowering=False)
    dt = lambda n, s, k="ExternalInput", d=F32: nc.dram_tensor(n, s, d, kind=k).ap()
    x = dt("x", [NB, SEQ, D])
    w_in = dt("w_in", [D, NIN])
    w_out = dt("w_out", [2 * D, D])
    lora_up = dt("lora_up", [128, 1024])
    chv_d = dt("chv", [128, 7 * 8])
    mu_d = dt("mu", [128, 25])
    normw_d = dt("normw", [128, 8])
    fnw_d = dt("fnw", [128, D])
    cst_d = dt("cst", [128, 3 * 128])
    msk_d = dt("msk", [128, 128 + 64 + 256])
    rst_d = dt("rst", [128, 512])
    y = dt("y", [NB, SEQ, D], "ExternalOutput")
    mixs = dt("mixs", [NB, 16, 128, SEQ], "Internal", BF16)
    dbg = dt("dbg", [128, 1024], "ExternalOutput") if stop else None

    es = ExitStack()
    with es:
        S = Sched(nc, es)
        _names = {}

        def sb(n, s, d=F32, st=es):
            k = _names.get(n, 0)
            _names[n] = k + 1
            return st.enter_context(nc.sbuf_tensor(n if k == 0 else f"{n}_{k}", s, d))
        ps = [es.enter_context(nc.psum_tensor(f"ps{i}", [128, 512], F32)) for i in range(8)]
        psb = [p[:, :].bitcast(BF16) for p in ps]

        xnT = sb("xnT", [128, 8, SEQ], BF16)
        chv = sb("chv_s", [128, 7, 8])
        muT = sb("mu_s", [128, 25])
        omu = sb("omu_s", [128, 25])
        omk = sb("omk_s", [128, 8])
        normw = sb("normw_s", [128, 8])
        normw_bc = sb("normw_bc", [128, 8, 128])
        fnw = sb("fnw_s", [128, D])
        cstf = sb("cstf", [128, 3 * 128])
        cst = sb("cst_s", [128, 3, 128], BF16)
        mskf = sb("mskf", [128, 448])
        maskA = sb("maskA", [64, 128], BF16)
        maskN = sb("maskN", [64, 64], BF16)
        maskT = sb("maskT", [128, 2, 2, 128], BF16)
        identP = sb("identP", [64, 64], BF16)
        rst = sb("rst_s", [128, 512])
        stg = [sb(f"stg{i}", [128, 8, 128]) for i in range(2)]
        Wb = [[sb(f"Wb{u}_{i}", [128, 8, 128], BF16) for i in range(4)] for u in range(2)]
        Wl = sb("Wl", [128, 8, 128], BF16)
        upf = sb("upf", [128, 1024])
        upb = sb("upb", [128, 1024], BF16)
        lora_bf = sb("lora_bf", [128, SEQ], BF16)
        WO = sb("WO", [128, 16, D], BF16)

        ident = cst[:, 0, :]
        blockones = cst[:, 1, :]
        ones = cst[:, 2, :]

        def dma(out, in_, reads, writes):
            return S.op("sp", lambda e: e.dma_start(out=out, in_=in_), reads, writes, dma=True)

        def mm(out, lhsT, rhs, reads, writes, start=True, stop=True, inc=True):
            return S.op("pe", lambda e: e.matmul(out, lhsT=lhsT, rhs=rhs, start=start, stop=stop),
                        reads, writes, inc=inc)

        def tr(out, in_, idn, reads, writes, inc=True):
            return S.op("pe", lambda e: e.transpose(out, in_, idn), reads, writes, inc=inc)

        def act(out, in_, func, reads, writes, scale=1.0, bias=None):
            if bias is None:
                return S.op("act", lambda e: e.activation(out=out, in_=in_, func=func, scale=scale), reads, writes)
            return S.op("act", lambda e: e.activation(out=out, in_=in_, func=func, scale=scale, bias=bias),
                        reads, writes)

        def tt(out, in0, in1, op, reads, writes, eng="dve"):
            return S.op(eng, lambda e: e.tensor_tensor(out=out, in0=in0, in1=in1, op=op), reads, writes)

        def ts(out, in0, s1, op0, reads, writes, s2=None, op1=None, eng="dve"):
            if s2 is None:
                return S.op(eng, lambda e: e.tensor_scalar(out=out, in0=in0, scalar1=s1, scalar2=None, op0=op0),
                            reads, writes)
            return S.op(eng, lambda e: e.tensor_scalar(out=out, in0=in0, scalar1=s1, scalar2=s2, op0=op0, op1=op1),
                        reads, writes)

        def stt(out, in0, scalar, in1, op0, op1, reads, writes):
            return S.op("dve", lambda e: e.scalar_tensor_tensor(out=out, in0=in0, scalar=scalar, in1=in1,
                                                                 op0=op0, op1=op1), reads, writes)

        def cp(out, in_, reads, writes, eng="dve"):
            if eng == "act":
                return act(out, in_, AF.Copy, reads, writes)
            return S.op(eng, lambda e: e.tensor_copy(out=out, in_=in_), reads, writes)

        def memset(ap, val, writes, eng="dve"):
            return S.op(eng, lambda e: e.memset(ap, val), (), writes)

        def c64(t):
            return t[:, :].rearrange("p (c t) -> p c t", t=64)

        def red(out, in_, reads, writes):
            return S.op("dve", lambda e: e.tensor_reduce(out=out, in_=in_, axis=AX.X, op=ALU.add), reads, writes)

        dma(chv[:, :, :], chv_d.rearrange("p (a b) -> p a b", b=8), (), ["chv"])
        dma(muT[:, :], mu_d[:, :], (), ["mu"])
        dma(normw[:, :], normw_d[:, :], (), ["normw"])
        dma(fnw[:, :], fnw_d[:, :], (), ["fnw"])
        dma(cstf[:, :], cst_d[:, :], (), ["cstf"])
        dma(mskf[:, :], msk_d[:, :], (), ["mskf"])
        dma(rst[:, :], rst_d[:, :], (), ["rst"])
        dma(upf[:, :], lora_up[:, :], (), ["upf"])
        cp(cst[:, :, :], cstf[:, :].rearrange("p (a b) -> p a b", b=128), ["cstf"], ["cst"])
        cp(maskA[:, :], mskf[0:64, 0:128], ["mskf"], ["maskA"])
        cp(maskN[:, :], mskf[0:64, 128:192], ["mskf"], ["maskN"])
        for j in range(2):
            cp(maskT[:, j, :, :], mskf[:, 192:448].rearrange("p (a b) -> p a b", b=128), ["mskf"], ["maskT"])
        cp(identP[:, :], cstf[0:64, 0:64], ["cstf"], ["identP"])
        cp(upb[:, :], upf[:, :], ["upf"], ["upb"])
        ts(omu[:, :], muT[:, :], -1.0, ALU.mult, ["mu"], ["omu"], s2=1.0, op1=ALU.add)
        ts(omk[:, :], chv[:, 3, :], -1.0, ALU.mult, ["chv"], ["omk"], s2=1.0, op1=ALU.add)
        cp(normw_bc[:, :, :], bcast(normw[:, :], [[1, 8], [0, 128]]), ["normw"], ["normw_bc"])

        W0, A0, KKS, KA, RK, LNW, LNB = range(7)

        stg_i = [0]

        def load_wtile(col0, dst, key):
            i = stg_i[0] % 2
            stg_i[0] += 1
            src = w_in[:, col0:col0 + 128].rearrange("(kc p) c -> p kc c", p=128)
            for h in range(2):
                dma(stg[i][:, h * 4:h * 4 + 4, :], src[:, h * 4:h * 4 + 4, :], (), [f"stg{i}/{h}"])
            tt(dst[:, :, :], stg[i][:, :, :], normw_bc[:, :, :], ALU.mult, [f"stg{i}", "normw_bc"], [key], eng="pool")

        for mt in range(16):
            i = stg_i[0] % 2
            stg_i[0] += 1
            dma(stg[i][:, :, :].rearrange("p a b -> p (a b)"), w_out[mt * 128:(mt + 1) * 128, :], (), [f"stg{i}"])
            cp(WO[:, mt, :], stg[i][:, :, :].rearrange("p a b -> p (a b)"), [f"stg{i}"], ["WO"], eng="pool")

        def proj_block(W, wkey, psi, t0):
            for kc in range(8):
                mm(ps[psi][:, :], W[:, kc, :], xnT[:, kc, t0:t0 + 512], [wkey, "xnT"], [f"ps{psi}"],
                   start=(kc == 0), stop=(kc == 7), inc=(kc == 7))

        def shift_evac(psi, mcol, prevlast, pkey, out, okey, tmp, tkey):
            p = ps[psi]
            act(tmp[:, :], p[:, :], AF.Copy, [f"ps{psi}", "omu"], [tkey], scale=omu[:, mcol:mcol + 1])
            stt(out[:, 1:512], p[:, 0:511], muT[:, mcol:mcol + 1], tmp[:, 1:512], ALU.mult, ALU.add,
                [f"ps{psi}", "mu", tkey], [okey])
            stt(out[:, 0:1], prevlast, muT[:, mcol:mcol + 1], tmp[:, 0:1], ALU.mult, ALU.add,
                [pkey, "mu", tkey], [okey])
            cp(prevlast, p[:, 511:512], [f"ps{psi}"], [pkey], eng="act")

        def checkpoint(name, src_ap, key, n):
            if stop != name:
                return
            dtile = upf
            memset(dtile[:, :], 0.0, ["upf"])
            n = min(n, 1024)
            cp(dtile[0:src_ap.shape[0], 0:n], src_ap[:, 0:n], [key], ["upf"])
            dma(dbg[:, :], dtile[:, :], ["upf"], ["dbg"])
            S.barrier()
            S.stopped = True

        for b in range(NB):
          try:
              with ExitStack() as s1:
                  xt = [sb(f"xt{i}", [128, D], F32, s1) for i in range(2)]
                  sqt = sb("sqt", [128, D], F32, s1)
                  xnb = sb("xnb", [128, D], BF16, s1)
                  ssA = sb("ssA", [128, 2], F32, s1)
                  for tti in range(16):
                      xi = tti % 2
                      dma(xt[xi][:, :], x[b, tti * 128:(tti + 1) * 128, :], (), [f"xt{xi}"])
                      act(sqt[:, :], xt[xi][:, :], AF.Square, [f"xt{xi}"], ["sqt"])
                      red(ssA[:, 0:1], sqt[:, :], ["sqt"], ["ssA"])
                      act(ssA[:, 1:2], ssA[:, 0:1], AF.Ln, ["ssA"], ["ssA1"], scale=1.0 / D, bias=RMS_EPS)
                      act(ssA[:, 1:2], ssA[:, 1:2], AF.Exp, ["ssA1"], ["ssA1"], scale=-0.5)
                      ts(xnb[:, :], xt[xi][:, :], ssA[:, 1:2], ALU.mult, [f"xt{xi}", "ssA1"], ["xnb"])
                      for kc in range(8):
                          tr(psb[0][:, kc * 128:(kc + 1) * 128], xnb[:, kc * 128:(kc + 1) * 128], ident,
                             ["xnb", "cst"], ["ps0"], inc=(kc == 7))
                      cp(xnT[:, :, tti * 128:(tti + 1) * 128], psb[0][:, :].rearrange("p (a b) -> p a b", b=128),
                         ["ps0"], ["xnT"], eng="act")
                  S.barrier()
              checkpoint("A", xnT[:, 3, :], "xnT", SEQ)

              with ExitStack() as s1:
                  lo = sb("lo", [128, 512], F32, s1)
                  lotmp = sb("lotmp", [128, 512], F32, s1)
                  plast = sb("plastl", [128, 1], F32, s1)
                  load_wtile(3072, Wl, "Wl")
                  memset(plast[:, :], 0.0, ["plastl"])
                  for tb in range(4):
                      t0 = tb * 512
                      proj_block(Wl, "Wl", 0, t0)
                      shift_evac(0, 24, plast[:, 0:1], "plastl", lo, "lo", lotmp, "lotmp")
                      act(lora_bf[0:64, t0:t0 + 512], lo[0:64, :], AF.Tanh, ["lo"], ["lora_bf"])
                      act(lora_bf[64:128, t0:t0 + 512], lo[64:128, :], AF.Copy, ["lo"], ["lora_bf"])
                  S.barrier()
              checkpoint("L", lora_bf[:, :], "lora_bf", SEQ)

              units = [("r", hp) for hp in range(8)] + [("a", hp) for hp in range(8)]

              def unit_cols(u):
                  kind, hp = u
                  if kind == "r":
                      return [hp * 128, 1024 + hp * 128, 2048 + hp * 128, GATE0 + hp * 128]
                  return [ATT0 + hp * 128, ATT0 + 1024 + hp * 128, ATT0 + 2048 + hp * 128, ATTG0 + hp * 128]

              def prefetch(ui):
                  if ui >= len(units):
                      return
                  for i, c in enumerate(unit_cols(units[ui])):
                      load_wtile(c, Wb[ui % 2][i], f"Wb{ui % 2}_{i}")

              prefetch(0)

              with ExitStack() as s1:
                  f = lambda n: sb(n, [128, 512], F32, s1)
                  R, K, TV, SIGW, AA, CS, EW, EWI, KK, RN, Bt, BON, FIN = [f(n) for n in
                      ("R", "K", "TV", "SIGW", "AA", "CS", "EW", "EWI", "KK", "RN", "Bt", "BON", "FIN")]
                  vT = sb("vT", [128, 512], BF16, s1)
                  sg = sb("sg", [128, 512], BF16, s1)
                  SQ = sb("SQ", [128, 512], BF16, s1)
                  T1 = sb("T1", [128, 512], BF16, s1)
                  mixo = sb("mixo", [128, 512], BF16, s1)
                  WCt = sb("WCt", [128, 8], F32, s1)
                  AR = sb("AR", [128, 8, 2, 64], BF16, s1)
                  BK = sb("BK", [128, 2, 512], BF16, s1)
                  BKh = sb("BKh", [128, 2, 512], BF16, s1)
                  AbT = [sb(f"AbT{j}", [64, 8, 128], BF16, s1) for j in range(2)]
                  AkT = [sb(f"AkT{j}", [64, 8, 128], BF16, s1) for j in range(2)]
                  PM = [sb(f"PM{j}", [64, 8, 128], BF16, s1) for j in range(2)]
                  NN = [sb(f"NN{j}", [64, 8, 64], BF16, s1) for j in range(2)]
                  tokB = sb("tokB", [64, 8, 128], BF16, s1)
                  tokK = sb("tokK", [64, 8, 128], BF16, s1)
                  tokV = sb("tokV", [64, 8, 128], BF16, s1)
                  X2all = sb("X2all", [64, 8, 128], F32, s1)
                  Y2all = sb("Y2all", [64, 8, 128], F32, s1)
                  KV = sb("KV", [128, 8, 64], F32, s1)
                  Ysb = sb("Ysb", [64, 8, 128], F32, s1)
                  ARs = sb("ARs", [128, 8, 2, 64], BF16, s1)
                  HT = sb("HT", [128, 64], F32, s1)
                  Xsb = sb("Xsb", [64, 128], BF16, s1)
                  Usb = sb("Usb", [64, 128], BF16, s1)
                  H32 = sb("H32", [128, 64], F32, s1)
                  Hbf2 = sb("Hbf2", [128, 2, 64], BF16, s1)
                  plr = sb("plr", [128, 3], F32, s1)
                  YSQ = sb("YSQ", [64, 1024], F32, s1)
                  ynb = sb("ynb", [64, 8, 128], BF16, s1)
                  st = sb("st", [64, 6, 16], F32, s1)

                  for ui in range(8):
                      hp = units[ui][1]
                      prefetch(ui + 1)
                      Wr, Wk, Wv, Wg = Wb[ui % 2]
                      wk = [f"Wb{ui % 2}_{i}" for i in range(4)]
                      memset(H32[:, :], 0.0, ["H32"])
                      memset(Hbf2[:, :, :], 0.0, ["Hbf2"], eng="pool")
                      memset(plr[:, :], 0.0, ["plr0", "plr1", "plr2"])
                      cv = lambda v: chv[:, v, hp:hp + 1]
                      for tb in range(4):
                          t0 = tb * 512
                          proj_block(Wr, wk[0], 0, t0)
                          proj_block(Wk, wk[1], 1, t0)
                          proj_block(Wv, wk[2], 2, t0)
                          proj_block(Wg, wk[3], 3, t0)
                          mm(ps[4][:, :], upb[0:64, hp * 128:(hp + 1) * 128], lora_bf[0:64, t0:t0 + 512],
                             ["upb", "lora_bf"], ["ps4"])
                          mm(ps[5][:, :], upb[64:128, hp * 128:(hp + 1) * 128], lora_bf[64:128, t0:t0 + 512],
                             ["upb", "lora_bf"], ["ps5"])
                          shift_evac(0, hp, plr[:, 0:1], "plr0", R, "R", R, "R")
                          shift_evac(1, 8 + hp, plr[:, 1:2], "plr1", K, "K", K, "K")
                          shift_evac(2, 16 + hp, plr[:, 2:3], "plr2", vT, "vT", TV, "TV")
                          act(sg[:, :], ps[3][:, :], AF.Silu, ["ps3"], ["sg"])
                          act(SIGW[:, :], ps[4][:, :], AF.Sigmoid, ["ps4", "chv"], ["SIGW"], bias=cv(W0))
                          act(AA[:, :], ps[5][:, :], AF.Sigmoid, ["ps5", "chv"], ["AA"], bias=cv(A0))
                          S.op("dve", lambda e: e.tensor_tensor_scan(out=CS[:, :], data0=rst[:, :], data1=SIGW[:, :],
                                                                     initial=0.0, op0=ALU.mult, op1=ALU.add),
                               ["rst", "SIGW"], ["CS"])
                          tt(SIGW[:, :], CS[:, :], SIGW[:, :], ALU.subtract, ["CS", "SIGW"], ["SIGW"])
                          act(EW[:, :], CS[:, :], AF.Exp, ["CS"], ["EW"], scale=-C0)
                          act(EWI[:, :], CS[:, :], AF.Exp, ["CS"], ["EWI"], scale=C0)
                          act(SIGW[:, :], SIGW[:, :], AF.Exp, ["SIGW"], ["SIGW"], scale=-C0)
                          cp(WCt[:, :], EW[:, 63:512:64], ["EW"], ["WCt"])
                          tt(CS[:, :].rearrange("p (c t) -> p c t", t=64), EWI[:, :].rearrange("p (c t) -> p c t", t=64),
                             bcast(WCt[:, :], [[1, 8], [0, 64]]), ALU.mult, ["EWI", "WCt"], ["CS"])
                          ts(KK[:, :], K[:, :], cv(KKS), ALU.mult, ["K", "chv"], ["KK"])
                          act(SQ[:, :], KK[:, :], AF.Square, ["KK"], ["SQ"])
                          mm(ps[6][:, :], blockones, SQ[:, :], ["cst", "SQ"], ["ps6"])
                          act(RN[:, :], ps[6][:, :], AF.Ln, ["ps6"], ["RN"])
                          act(RN[:, :], RN[:, :], AF.Exp, ["RN"], ["RN"], scale=-0.5)
                          tt(KK[:, :], KK[:, :], RN[:, :], ALU.mult, ["KK", "RN"], ["KK"])
                          tt(Bt[:, :], KK[:, :], AA[:, :], ALU.mult, ["KK", "AA"], ["Bt"])
                          ts(AA[:, :], AA[:, :], cv(KA), ALU.mult, ["AA", "chv", "omk"], ["AA"],
                             s2=omk[:, hp:hp + 1], op1=ALU.add)
                          tt(K[:, :], K[:, :], AA[:, :], ALU.mult, ["K", "AA"], ["K"])
                          stt(T1[:, :], R[:, :], cv(RK), K[:, :], ALU.mult, ALU.mult, ["R", "K", "chv"], ["T1"])
                          mm(ps[7][:, :], blockones, T1[:, :], ["cst", "T1"], ["ps7"])
                          tt(BON[:, :], ps[7][:, :], vT[:, :], ALU.mult, ["ps7", "vT"], ["BON"])
                          tt(AR[:, :, 1, :], c64(R), c64(EW), ALU.mult, ["R", "EW"], ["AR"])
                          stt(AR[:, :, 0, :], c64(KK), -1.0, c64(SIGW), ALU.mult, ALU.mult, ["KK", "SIGW"], ["AR"])
                          cp(ARs[:, :, 0, :], AR[:, :, 1, :], ["AR"], ["ARs"], eng="pool")
                          cp(ARs[:, :, 1, :], AR[:, :, 0, :], ["AR"], ["ARs"], eng="pool")
                          tt(BK[:, 0, :], Bt[:, :], EWI[:, :], ALU.mult, ["Bt", "EWI"], ["BK"], eng="pool")
                          tt(BK[:, 1, :], K[:, :], EWI[:, :], ALU.mult, ["K", "EWI"], ["BK"], eng="pool")
                          tt(BKh[:, 0, :], Bt[:, :], CS[:, :], ALU.mult, ["Bt", "CS"], ["BKh"], eng="pool")
                          tt(BKh[:, 1, :], K[:, :], CS[:, :], ALU.mult, ["K", "CS"], ["BKh"], eng="pool")

                          if ui == 0 and tb == 0:
                              checkpoint("S1", BKh[:, :, :].rearrange("p a b -> p (a b)"), "BKh", 1024)
                          for j in range(2):
                              kp = slice(j * 64, j * 64 + 64)
                              for c in range(8):
                                  cs_ = slice(c * 64, c * 64 + 64)
                                  bk = c // 4
                                  col = (c % 4) * 128
                                  mm(ps[0 + bk][0:64, col:col + 128], BK[kp, 0, cs_], AR[kp, c, :, :].rearrange("p a b -> p (a b)"),
                                     ["BK", "AR"], [f"ps{bk}"], inc=(c % 4 == 3))
                              for c in range(8):
                                  cs_ = slice(c * 64, c * 64 + 64)
                                  bk = c // 4
                                  col = (c % 4) * 128
                                  mm(ps[2 + bk][0:64, col:col + 128], BK[kp, 1, cs_], AR[kp, c, :, :].rearrange("p a b -> p (a b)"),
                                     ["BK", "AR"], [f"ps{2 + bk}"], inc=(c % 4 == 3))
                              for c in range(8):
                                  cs_ = slice(c * 64, c * 64 + 64)
                                  mm(ps[4][0:64, cs_], AR[kp, c, 0, :], BK[kp, 0, cs_], ["BK", "AR"], ["ps4"],
                                     inc=(c == 7))
                              mA = bcast(maskA[:, :], [[0, 4], [1, 128]])
                              for bk in range(2):
                                  tt(AbT[j][:, bk * 4:bk * 4 + 4, :], ps[bk][0:64, :].rearrange("p (c t) -> p c t", t=128),
                                     mA, ALU.mult, [f"ps{bk}", "maskA"], [f"AbT{j}"])
                                  tt(AkT[j][:, bk * 4:bk * 4 + 4, :],
                                     ps[2 + bk][0:64, :].rearrange("p (c t) -> p c t", t=128),
                                     mA, ALU.mult, [f"ps{2 + bk}", "maskA"], [f"AkT{j}"])
                              tt(NN[j][:, :, :], ps[4][0:64, :].rearrange("p (c t) -> p c t", t=64),
                                 bcast(maskN[:, :], [[0, 8], [1, 64]]), ALU.mult, ["ps4", "maskN"], [f"NN{j}"])
                              cp(PM[j][:, :, 0:64], bcast(identP[:, :], [[0, 8], [1, 64]]), ["identP"], [f"PM{j}"],
                                 eng="pool")
                              cp(PM[j][:, :, 64:128], AbT[j][:, :, 0:64], [f"AbT{j}"], [f"PM{j}"], eng="pool")
                          for lvl in range(6):
                              last = lvl == 5
                              for j in range(2):
                                  pq0, nnb = (5, 7) if j == 0 else (0, 2)
                                  for c in range(8):
                                      bk = c // 4
                                      col = (c % 4) * 128
                                      if last:
                                          mm(ps[pq0 + bk][0:64, col:col + 64], NN[j][:, c, :], PM[j][:, c, 0:64],
                                             [f"NN{j}", f"PM{j}"], [f"ps{pq0 + bk}"], inc=(c % 4 == 3))
                                      else:
                                          mm(ps[pq0 + bk][0:64, col:col + 128], NN[j][:, c, :], PM[j][:, c, :],
                                             [f"NN{j}", f"PM{j}"], [f"ps{pq0 + bk}"], inc=(c % 4 == 3))
                                  if not last:
                                      for c in range(8):
                                          mm(ps[nnb][0:64, c * 64:c * 64 + 64], PM[j][:, c, 64:128], NN[j][:, c, :],
                                             [f"NN{j}", f"PM{j}"], [f"ps{nnb}"], inc=(c == 7))
                                  for bk in range(2):
                                      pv = ps[pq0 + bk][0:64, :].rearrange("p (c t) -> p c t", t=128)
                                      tt(PM[j][:, bk * 4:bk * 4 + 4, 0:64], pv[:, :, 0:64],
                                         PM[j][:, bk * 4:bk * 4 + 4, 0:64], ALU.add,
                                         [f"ps{pq0 + bk}", f"PM{j}"], [f"PM{j}"])
                                      if not last:
                                          cp(PM[j][:, bk * 4:bk * 4 + 4, 64:128], pv[:, :, 64:128],
                                             [f"ps{pq0 + bk}"], [f"PM{j}"], eng="act")
                                  if not last:
                                      cp(NN[j][:, :, :], ps[nnb][0:64, :].rearrange("p (c t) -> p c t", t=64),
                                         [f"ps{nnb}"], [f"NN{j}"], eng="act")
                          for qi, (src, dst, dkey, skey) in enumerate(((BKh[:, 0, :], tokB, "tokB", "BKh"),
                                                                        (BKh[:, 1, :], tokK, "tokK", "BKh"),
                                                                        (vT[:, :], tokV, "tokV", "vT"))):
                              for c in range(8):
                                  tr(psb[qi][0:64, c * 128:(c + 1) * 128], src[:, c * 64:(c + 1) * 64], ident,
                                     [skey, "cst"], [f"ps{qi}"], inc=(c == 7))
                              cp(dst[:, :, :], psb[qi][0:64, :].rearrange("p (c t) -> p c t", t=128), [f"ps{qi}"], [dkey],
                                 eng=("act" if qi != 1 else "dve"))

                          if ui == 0 and tb == 0:
                              checkpoint("S2", PM[1][:, :, :].rearrange("p a b -> p (a b)"), "PM1", 1024)
                          for off, dst, dkey, banks in ((0, X2all, "X2all", (0, 1)), (64, Y2all, "Y2all", (2, 3))):
                              for c in range(8):
                                  bk = banks[c // 4]
                                  col = (c % 4) * 128
                                  for j in range(2):
                                      jc = slice(j * 64, j * 64 + 64)
                                      mm(ps[bk][0:64, col + j * 64:col + j * 64 + 64], AkT[j][:, c, off:off + 64],
                                         tokV[:, c, jc], [f"AkT{j}", "tokV"], [f"ps{bk}"], inc=(c % 4 == 3 and j == 1))
                              for h in range(2):
                                  cp(dst[:, h * 4:h * 4 + 4, :], ps[banks[h]][0:64, :].rearrange("p (c t) -> p c t", t=128),
                                     [f"ps{banks[h]}"], [dkey], eng=("act" if h else "dve"))
                          for c in range(8):
                              cs_ = slice(c * 64, c * 64 + 64)
                              mm(ps[4][0:64, cs_], tokK[:, c, 0:64], tokV[:, c, 0:64], ["tokK", "tokV"], ["ps4"], inc=False)
                              mm(ps[4][64:128, cs_], tokK[:, c, 64:128], tokV[:, c, 64:128], ["tokK", "tokV"], ["ps4"],
                                 inc=(c == 7))
                          cp(KV[:, :, :], ps[4][:, :].rearrange("p (c v) -> p c v", v=64), ["ps4"], ["KV"], eng="act")

                          H2 = Hbf2[:, :, :].rearrange("p a b -> p (a b)")
                          for c in range(8):
                              cs_ = slice(c * 64, c * 64 + 64)
                              mm(ps[3][:, 0:128], AR[:, c, :, :].rearrange("p a b -> p (a b)"), H2, ["AR", "Hbf2"], ["ps3"])
                              mm(ps[5][:, 0:128], ARs[:, c, :, :].rearrange("p a b -> p (a b)"), H2, ["ARs", "Hbf2"], ["ps5"])
                              tt(Xsb[:, :], ps[3][0:64, 0:128], X2all[:, c, :], ALU.add, ["ps3", "X2all"], ["Xsb"])
                              for j in range(2):
                                  jc = slice(j * 64, j * 64 + 64)
                                  mm(ps[7][0:64, jc], PM[j][:, c, 0:64], Xsb[:, jc], [f"PM{j}", "Xsb"], ["ps7"], inc=(j == 1))
                              cp(Usb[:, :], ps[7][0:64, 0:128], ["ps7"], ["Usb"], eng="act")
                              stt(HT[:, :], H32[:, :], WCt[:, c:c + 1], KV[:, c, :], ALU.mult, ALU.add,
                                  ["H32", "WCt", "KV"], ["HT"])
                              mm(ps[4][0:64, 0:64], tokB[:, c, 0:64], Usb[:, 0:64], ["tokB", "Usb"], ["ps4"], inc=False)
                              mm(ps[4][64:128, 0:64], tokB[:, c, 64:128], Usb[:, 64:128], ["tokB", "Usb"], ["ps4"])
                              tt(H32[:, :], ps[4][:, 0:64], HT[:, :], ALU.add, ["ps4", "HT"], ["H32"])
                              cp(Hbf2[0:64, 0, :], H32[0:64, :], ["H32"], ["Hbf2"], eng="act")
                              cp(Hbf2[64:128, 1, :], H32[64:128, :], ["H32"], ["Hbf2"], eng="pool")
                              tt(Ysb[:, c, :], ps[5][0:64, 0:128], Y2all[:, c, :], ALU.add, ["ps5", "Y2all"], ["Ysb"])
                              for j in range(2):
                                  jc = slice(j * 64, j * 64 + 64)
                                  mm(ps[6][0:64, jc], AbT[j][:, c, 64:128], Usb[:, jc], [f"AbT{j}", "Usb"], ["ps6"],
                                     inc=(j == 1))
                              tt(Ysb[:, c, :], ps[6][0:64, 0:128], Ysb[:, c, :], ALU.add, ["ps6", "Ysb"], ["Ysb"])

                          if ui == 0 and tb == 0:
                              checkpoint("S3", H32[:, :], "H32", 64)
                          Yv = Ysb[:, :, :].rearrange("p c (j v) -> p (c j) v", v=64)
                          red(st[:, 0, :], Yv, ["Ysb"], ["st0"])
                          act(YSQ[:, :], Ysb[:, :, :].rearrange("p c t -> p (c t)"), AF.Square, ["Ysb"], ["YSQ"])
                          red(st[:, 1, :], YSQ[:, :].rearrange("p (g v) -> p g v", v=64), ["YSQ"], ["st1"])
                          ts(st[:, 2, :], st[:, 0, :], 1.0 / 64, ALU.mult, ["st0"], ["st2"])
                          tt(st[:, 3, :], st[:, 2, :], st[:, 2, :], ALU.mult, ["st2"], ["st3"])
                          stt(st[:, 4, :], st[:, 1, :], 1.0 / 64, st[:, 3, :], ALU.mult, ALU.subtract, ["st1", "st3"], ["st4"])
                          act(st[:, 5, :], st[:, 4, :], AF.Ln, ["st4"], ["st5"], bias=GN_EPS)
                          act(st[:, 5, :], st[:, 5, :], AF.Exp, ["st5"], ["st5"], scale=-0.5)
                          tt(Yv, Yv, bcast(st[:, 2, :], [[1, 16], [0, 64]]), ALU.subtract, ["Ysb", "st2"], ["Ysb"])
                          tt(ynb[:, :, :].rearrange("p c (j v) -> p (c j) v", v=64), Yv,
                             bcast(st[:, 5, :], [[1, 16], [0, 64]]), ALU.mult, ["Ysb", "st5"], ["ynb"])
                          for c in range(8):
                              for j in range(2):
                                  jc = slice(j * 64, j * 64 + 64)
                                  mm(ps[0][jc, c * 64:c * 64 + 64], ynb[:, c, jc], identP[:, :], ["ynb", "identP"], ["ps0"],
                                     inc=(c == 7 and j == 1))
                          ts(FIN[:, :], ps[0][:, :], cv(LNW), ALU.mult, ["ps0", "chv"], ["FIN"], s2=cv(LNB), op1=ALU.add)
                          tt(FIN[:, :], FIN[:, :], BON[:, :], ALU.add, ["FIN", "BON"], ["FIN"])
                          tt(mixo[:, :], FIN[:, :], sg[:, :], ALU.mult, ["FIN", "sg"], ["mixo"])
                          dma(mixs[b, hp, :, t0:t0 + 512], mixo[:, :], ["mixo"], [f"mixs{hp}"])
                          if ui == 0 and tb == 1:
                              checkpoint("S4", mixo[:, :], "mixo", 512)
                      if ui == 0:
                          checkpoint("U1", mixo[:, :], "mixo", 512)
                  S.barrier()

              with ExitStack() as s1:
                  qz = sb("qz", [128, 2, SEQ], BF16, s1)
                  memset(qz[:, :, :], 0.0, ["qz"], eng="pool")
                  kT = sb("kT", [128, SEQ], BF16, s1)
                  vTa = sb("vTa", [128, SEQ], BF16, s1)
                  sgT = sb("sgT", [128, SEQ], BF16, s1)
                  Vp = [sb(f"Vp{p}", [128, 16, 128], BF16, s1) for p in range(3)]
                  acc = sb("acc", [128, 2, SEQ], F32, s1)
                  RL = sb("RL", [128, SEQ], F32, s1)
                  mixa = sb("mixa", [128, SEQ], BF16, s1)
                  PT = [sb(f"PT{i}", [128, 2, 2, 128], BF16, s1) for i in range(3)]

                  def toks(d, i):
                      L = SEQ // d
                      g = i * 128
                      r, l0 = g // L, g % L
                      s0 = r + d * l0
                      return slice(s0, s0 + d * 127 + 1, d)

                  for ui in range(8, 16):
                      hp = units[ui][1]
                      prefetch(ui + 1)
                      Wq, Wk_, Wv_, Wg_ = Wb[ui % 2]
                      wk = [f"Wb{ui % 2}_{i}" for i in range(4)]
                      for tb in range(4):
                          t0 = tb * 512
                          proj_block(Wq, wk[0], 0, t0)
                          act(qz[0:64, 0, t0:t0 + 512], ps[0][0:64, :], AF.Copy, ["ps0"], ["qz"], scale=0.125)
                          act(qz[64:128, 1, t0:t0 + 512], ps[0][64:128, :], AF.Copy, ["ps0"], ["qz"], scale=0.125)
                          proj_block(Wk_, wk[1], 1, t0)
                          cp(kT[:, t0:t0 + 512], ps[1][:, :], ["ps1"], ["kT"])
                          proj_block(Wv_, wk[2], 2, t0)
                          act(vTa[:, t0:t0 + 512], ps[2][:, :], AF.Copy, ["ps2"], ["vTa"])
                          proj_block(Wg_, wk[3], 3, t0)
                          act(sgT[:, t0:t0 + 512], ps[3][:, :], AF.Silu, ["ps3"], ["sgT"])
                      for p, d in enumerate(PATTERNS):
                          for half in range(2):
                              pi = 4 + (p * 2 + half) % 2
                              for ii in range(8):
                                  i = half * 8 + ii
                                  tr(psb[pi][:, ii * 128:(ii + 1) * 128], vTa[:, toks(d, i)], ident, ["vTa", "cst"],
                                     [f"ps{pi}"], inc=(ii == 7))
                              cp(Vp[p][:, half * 8:half * 8 + 8, :], psb[pi][:, :].rearrange("p (a b) -> p a b", b=128),
                                 [f"ps{pi}"], [f"Vp{p}"], eng=("act" if half else "dve"))
                      blocks = [(p, d, i) for p, d in enumerate(PATTERNS) for i in range(16)]
                      SB_, OB_ = (0, 1, 6), (2, 3, 7)
                      LOOK = 2

                      def kbs_of(d, i):
                          has_prev = ((i * 128) % (SEQ // d)) != 0
                          return ([(0, i - 1)] if has_prev else []) + [(1, i)]

                      def emit_scores(bi):
                          p, d, i = blocks[bi]
                          kbs = kbs_of(d, i)
                          sbk = SB_[bi % 3]
                          sv = ps[sbk][:, :].rearrange("p (j k q) -> p j k q", j=2, k=2)
                          pt = PT[bi % 3]
                          for j in range(2):
                              for (kbi, kt) in kbs:
                                  mm(sv[:, j, kbi, :], kT[:, toks(d, kt)], qz[:, j, toks(d, i)], ["kT", "qz"],
                                     [f"ps{sbk}"], inc=(j == 1 and kbi == 1))
                          k0 = kbs[0][0]
                          act(pt[:, :, k0:2, :], sv[:, :, k0:2, :], AF.Exp, [f"ps{sbk}"], [f"PT{bi % 3}"])
                          tt(pt[:, :, k0:2, :], pt[:, :, k0:2, :], maskT[:, :, k0:2, :], ALU.mult,
                             [f"PT{bi % 3}", "maskT"], [f"PT{bi % 3}"], eng="pool")

                      def emit_pv(bi):
                          p, d, i = blocks[bi]
                          kbs = kbs_of(d, i)
                          obk = OB_[bi % 3]
                          pt = PT[bi % 3]
                          ov = ps[obk][:, :].rearrange("p (r q) -> p r q", q=128)
                          for rgn in range(4):
                              j = rgn % 2
                              for n_, (kbi, kt) in enumerate(kbs):
                                  lhs = Vp[p][:, kt, :] if rgn < 2 else ones
                                  mm(ov[:, rgn, :], lhs, pt[:, j, kbi, :], [f"Vp{p}", "cst", f"PT{bi % 3}"], [f"ps{obk}"],
                                     start=(n_ == 0), stop=(n_ == len(kbs) - 1),
                                     inc=(rgn == 3 and n_ == len(kbs) - 1))
                          for j in range(2):
                              kp = slice(j * 64, j * 64 + 64)
                              src = ov[kp, j:4:2, :]
                              dst = acc[kp, :, toks(d, i)]
                              if p == 0:
                                  cp(dst, src, [f"ps{obk}"], [f"acc{j}"], eng=("act" if j else "dve"))
                              else:
                                  tt(dst, src, dst, ALU.add, [f"ps{obk}", f"acc{j}"], [f"acc{j}"])

                      for bi in range(min(LOOK, len(blocks))):
                          emit_scores(bi)
                      for bi in range(len(blocks)):
                          if bi + LOOK < len(blocks):
                              emit_scores(bi + LOOK)
                          emit_pv(bi)
                      act(RL[:, :], acc[:, 1, :], AF.Ln, ["acc0", "acc1"], ["RL"])
                      act(RL[:, :], RL[:, :], AF.Exp, ["RL"], ["RL"], scale=-1.0)
                      tt(acc[:, 0, :], acc[:, 0, :], RL[:, :], ALU.mult, ["acc0", "acc1", "RL"], ["acc0", "acc1"])
                      tt(mixa[:, :], acc[:, 0, :], sgT[:, :], ALU.mult, ["acc0", "acc1", "sgT"], ["mixa"])
                      dma(mixs[b, 8 + hp, :, :], mixa[:, :], ["mixa"], [f"mixs{8 + hp}"])
                      if ui == 8:
                          checkpoint("T1", mixa[:, :], "mixa", SEQ)
                  S.barrier()

              with ExitStack() as s1:
                  xt = [sb(f"xo{i}", [128, D], F32, s1) for i in range(2)]
                  MT = [sb(f"MT{i}", [128, 16, 128], BF16, s1) for i in range(2)]
                  hT = sb("hT", [128, D], F32, s1)
                  sqo = sb("sqo", [128, D], F32, s1)
                  oT = [sb(f"oT{i}", [128, D], F32, s1) for i in range(2)]
                  sso = sb("sso", [128, 2], F32, s1)
                  allmix = [f"mixs{m}" for m in range(16)]
                  for tti in range(16):
                      xi = tti % 2
                      tsl = slice(tti * 128, (tti + 1) * 128)
                      dma(xt[xi][:, :], x[b, tsl, :], (), [f"xo{xi}"])
                      for q4 in range(4):
                          dma(MT[xi][:, q4 * 4:q4 * 4 + 4, :], mixs[b, q4 * 4:q4 * 4 + 4, :, tsl].rearrange("m p t -> p m t"),
                              allmix[q4 * 4:q4 * 4 + 4], [f"MT{xi}/{q4}"])
                      for half in range(2):
                          for mt in range(16):
                              mm(ps[6 + half][:, :], MT[xi][:, mt, :], WO[:, mt, half * 512:(half + 1) * 512],
                                 [f"MT{xi}", "WO"], [f"ps{6 + half}"], start=(mt == 0), stop=(mt == 15), inc=(mt == 15))
                          tt(hT[:, half * 512:(half + 1) * 512], ps[6 + half][:, :], xt[xi][:, half * 512:(half + 1) * 512],
                             ALU.add, [f"ps{6 + half}", f"xo{xi}"], ["hT"])
                      act(sqo[:, :], hT[:, :], AF.Square, ["hT"], ["sqo"])
                      red(sso[:, 0:1], sqo[:, :], ["sqo"], ["sso"])
                      act(sso[:, 1:2], sso[:, 0:1], AF.Ln, ["sso"], ["sso1"], scale=1.0 / D, bias=RMS_EPS)
                      act(sso[:, 1:2], sso[:, 1:2], AF.Exp, ["sso1"], ["sso1"], scale=-0.5)
                      stt(oT[xi][:, :], hT[:, :], sso[:, 1:2], fnw[:, :], ALU.mult, ALU.mult, ["hT", "sso1", "fnw"],
                          [f"oT{xi}"])
                      dma(y[b, tsl, :], oT[xi][:, :], [f"oT{xi}"], [f"y{b}_{tti}"])
                  S.barrier()
          except _Stop:
            break
        S.barrier()
        print(f"[kernel] ops={S.nops} waits={S.nwaits} cnt={S.cnt}")
    return nc


def _consts():
    ident = np.eye(128, dtype=np.float32)
    blockones = np.zeros((128, 128), np.float32)
    blockones[0:64, 0:64] = 1.0
    blockones[64:128, 64:128] = 1.0
    ones = np.ones((128, 128), np.float32)
    cst = np.concatenate([ident, blockones, ones], axis=1)
    s = np.arange(64)[:, None]
    t = np.arange(64)[None, :]
    maskA = np.concatenate([(s < t), (s <= t)], axis=1).astype(np.float32)
    maskN = (np.arange(64)[:, None] > np.arange(64)[None, :]).astype(np.float32)
    k = np.arange(128)[:, None]
    q = np.arange(128)[None, :]
    mprev = (k >= q).astype(np.float32)
    mdiag = (k <= q).astype(np.float32)
    msk = np.zeros((128, 448), np.float32)
    msk[0:64, 0:128] = maskA
    msk[0:64, 128:192] = maskN
    msk[:, 192:320] = mprev
    msk[:, 320:448] = mdiag
    rst = np.ones((128, 512), np.float32)
    rst[:, 0::64] = 0.0
    return cst, msk, rst


_NC_CACHE = {}


def kernel(x, norm_w, w_in, mu_shift, w0, w_up, a0, a_up, k_k, k_a, r_k, ln_x_w, ln_x_b, w_out, final_norm_w):
    f = lambda a: np.ascontiguousarray(np.asarray(a, dtype=np.float32))
    x = f(x)
    cst, msk, rst = _consts()
    pc = lambda v: f(v).reshape(8, 128).T
    chv = np.ascontiguousarray(np.stack([pc(w0), pc(a0), pc(k_k), pc(k_a), pc(np.asarray(r_k).reshape(-1)),
                                         pc(ln_x_w), pc(ln_x_b)], axis=1).reshape(128, 56))
    mu = np.ascontiguousarray(f(mu_shift).reshape(25, 128).T)
    normw = np.ascontiguousarray(f(norm_w).reshape(8, 128).T)
    fnw = np.ascontiguousarray(np.broadcast_to(f(final_norm_w)[None, :], (128, D)))
    lora_up = np.ascontiguousarray(np.concatenate([f(w_up), f(a_up)], axis=0))
    shared = {"w_in": f(w_in), "w_out": f(w_out), "lora_up": lora_up, "chv": chv, "mu": mu, "normw": normw,
              "fnw": fnw, "cst": cst, "msk": msk, "rst": rst}
    if "nc" not in _NC_CACHE:
        _NC_CACHE["nc"] = build()
    nc = _NC_CACHE["nc"]
    in_maps = [dict(shared, x=np.ascontiguousarray(x[c * NB:(c + 1) * NB])) for c in range(NCORES)]
    res = run_bass_kernel_spmd(nc, in_maps, core_ids=list(range(NCORES)))
    return np.concatenate([r["y"] for r in res.results], axis=0).astype(np.float32)
```

```python
import math
from contextlib import ExitStack

import numpy as np
import concourse.bass as bass
import concourse.mybir as mybir
from concourse.bass_utils import run_bass_kernel_spmd

F32 = mybir.dt.float32
BF16 = mybir.dt.bfloat16
AF = mybir.ActivationFunctionType
ALU = mybir.AluOpType
AX = mybir.AxisListType

NCORES = 8
NB = 2
SEQ = 2048
D = 1024
NIN = 8320
SHIFT_COLS = 3200
GATE0 = 3200
ATT0 = 4224
ATTG0 = 7296
C0 = math.exp(-0.5)
RMS_EPS = 1e-5
GN_EPS = 64e-5
PATTERNS = (1, 4, 16)


class Sched:
    def __init__(self, nc, es, ndma=12):
        self.nc = nc
        self.eng = {"pe": nc.tensor, "dve": nc.vector, "act": nc.scalar, "pool": nc.gpsimd, "sp": nc.sync}
        self.sem = {e: es.enter_context(nc.semaphore("sem_" + e)) for e in ("pe", "dve", "act", "pool")}
        self.cnt = {e: 0 for e in self.sem}
        self.dsem = [es.enter_context(nc.semaphore(f"dsem{i}")) for i in range(ndma)]
        self.dcnt = [0] * ndma
        self.dnext = 0
        self.waited = {e: {} for e in self.eng}
        self.lastw = {}
        self.readers = {}
        self.children = {}
        self.nwaits = 0
        self.nops = 0

    def _semobj(self, sk):
        return self.sem[sk[1]] if sk[0] == "e" else self.dsem[sk[1]]

    def _related(self, k):
        out = [k]
        if "/" in k:
            out.append(k.split("/")[0])
        else:
            out.extend(self.children.get(k, ()))
        return out

    def _wait(self, e, sk, val):
        if self.waited[e].get(sk, 0) < val:
            self.eng[e].wait_ge(self._semobj(sk), val)
            self.waited[e][sk] = val
            self.nwaits += 1

    stopped = False
    deferred = None

    def emit_deferred(self, lst, k):
        for _ in range(min(k, len(lst))):
            e, fn, reads, writes, dma, inc = lst.pop(0)
            self.op(e, fn, reads, writes, dma=dma, inc=inc)

    def op(self, e, fn, reads=(), writes=(), dma=False, inc=True):
        if self.stopped:
            return None
        if self.deferred is not None:
            self.deferred.append((e, fn, tuple(reads), tuple(writes), dma, inc))
            return None
        deps = {}

        def add(ev):
            sk, val = ev
            if e == "pe" and sk == ("e", "pe"):
                return
            if deps.get(sk, 0) < val:
                deps[sk] = val

        for k0 in reads:
            for k in self._related(k0):
                if k in self.lastw:
                    add(self.lastw[k])
        for k0 in writes:
            for k in self._related(k0):
                if k in self.lastw:
                    add(self.lastw[k])
                for ev in self.readers.get(k, {}).items():
                    add(ev)
        for sk, val in deps.items():
            self._wait(e, sk, val)
        if dma:
            idx = self.dnext
            self.dnext = (self.dnext + 1) % len(self.dsem)
            if self.dcnt[idx] > 0:
                self._wait(e, ("d", idx), self.dcnt[idx])
            inst = fn(self.eng[e])
            inst.then_inc(self.dsem[idx], 16)
            self.dcnt[idx] += 16
            ev = (("d", idx), self.dcnt[idx])
        else:
            inst = fn(self.eng[e])
            if inc:
                inst.then_inc(self.sem[e], 1)
                self.cnt[e] += 1
                ev = (("e", e), self.cnt[e])
            else:
                ev = (("e", e), self.cnt[e] + 1)
        self.nops += 1
        for k in writes:
            if "/" in k:
                self.children.setdefault(k.split("/")[0], set()).add(k)
            self.lastw[k] = ev
            self.readers[k] = {}
        for k in reads:
            if "/" in k:
                self.children.setdefault(k.split("/")[0], set()).add(k)
            r = self.readers.setdefault(k, {})
            if r.get(ev[0], 0) < ev[1]:
                r[ev[0]] = ev[1]
        return inst

    def barrier(self):
        if self.stopped:
            return
        for e in self.eng:
            for x in self.sem:
                if self.cnt[x] > 0:
                    self._wait(e, ("e", x), self.cnt[x])
            for i in range(len(self.dsem)):
                if self.dcnt[i] > 0:
                    self._wait(e, ("d", i), self.dcnt[i])


def bcast(ap, dims):
    return bass.AP(ap.tensor, ap.offset, [list(ap.ap[0])] + [list(d) for d in dims])


class _Stop(Exception):
    pass


def build(stop=None):
    nc = bass.Bass("TRN2", target_bir_lowering=False)
    dt = lambda n, s, k="ExternalInput", d=F32: nc.dram_tensor(n, s, d, kind=k).ap()
    x = dt("x", [NB, SEQ, D])
    w_in = dt("w_in", [D, NIN])
    w_out = dt("w_out", [2 * D, D])
    lora_up = dt("lora_up", [128, 1024])
    chv_d = dt("chv", [128, 7 * 8])
    mu_d = dt("mu", [128, 25])
    normw_d = dt("normw", [128, 8])
    fnw_d = dt("fnw", [128, D])
    cst_d = dt("cst", [128, 3 * 128])
    msk_d = dt("msk", [128, 128 + 64 + 256])
    rst_d = dt("rst", [128, 512])
    y = dt("y", [NB, SEQ, D], "ExternalOutput")
    mixs = dt("mixs", [NB, 16, 128, SEQ], "Internal", BF16)
    dbg = dt("dbg", [128, 1024], "ExternalOutput") if stop else None

    es = ExitStack()
    with es:
        S = Sched(nc, es)
        _names = {}

        def sb(n, s, d=F32, st=es):
            k = _names.get(n, 0)
            _names[n] = k + 1
            return st.enter_context(nc.sbuf_tensor(n if k == 0 else f"{n}_{k}", s, d))
        ps = [es.enter_context(nc.psum_tensor(f"ps{i}", [128, 512], F32)) for i in range(8)]
        psb = [p[:, :].bitcast(BF16) for p in ps]

        xnT = sb("xnT", [128, 8, SEQ], BF16)
        chv = sb("chv_s", [128, 7, 8])
        muT = sb("mu_s", [128, 25])
        omu = sb("omu_s", [128, 25])
        omk = sb("omk_s", [128, 8])
        normw = sb("normw_s", [128, 8])
        normw_bc = sb("normw_bc", [128, 8, 128])
        fnw = sb("fnw_s", [128, D])
        cstf = sb("cstf", [128, 3 * 128])
        cst = sb("cst_s", [128, 3, 128], BF16)
        mskf = sb("mskf", [128, 448])
        maskA = sb("maskA", [64, 128], BF16)
        maskN = sb("maskN", [64, 64], BF16)
        maskT = sb("maskT", [128, 2, 2, 128], BF16)
        identP = sb("identP", [64, 64], BF16)
        rst = sb("rst_s", [128, 512])
        stg = [sb(f"stg{i}", [128, 8, 128]) for i in range(2)]
        Wb = [[sb(f"Wb{u}_{i}", [128, 8, 128], BF16) for i in range(4)] for u in range(2)]
        Wl = sb("Wl", [128, 8, 128], BF16)
        upb = sb("upb", [128, 1024], BF16)
        lora_bf = sb("lora_bf", [128, SEQ], BF16)
        WO = sb("WO", [128, 16, D], BF16)

        ident = cst[:, 0, :]
        blockones = cst[:, 1, :]
        ones = cst[:, 2, :]

        def dma(out, in_, reads, writes):
            return S.op("sp", lambda e: e.dma_start(out=out, in_=in_), reads, writes, dma=True)

        def mm(out, lhsT, rhs, reads, writes, start=True, stop=True, inc=True):
            return S.op("pe", lambda e: e.matmul(out, lhsT=lhsT, rhs=rhs, start=start, stop=stop),
                        reads, writes, inc=inc)

        def tr(out, in_, idn, reads, writes, inc=True):
            return S.op("pe", lambda e: e.transpose(out, in_, idn), reads, writes, inc=inc)

        def act(out, in_, func, reads, writes, scale=1.0, bias=None):
            if bias is None:
                return S.op("act", lambda e: e.activation(out=out, in_=in_, func=func, scale=scale), reads, writes)
            return S.op("act", lambda e: e.activation(out=out, in_=in_, func=func, scale=scale, bias=bias),
                        reads, writes)

        def tt(out, in0, in1, op, reads, writes, eng="dve"):
            return S.op(eng, lambda e: e.tensor_tensor(out=out, in0=in0, in1=in1, op=op), reads, writes)

        def ts(out, in0, s1, op0, reads, writes, s2=None, op1=None, eng="dve"):
            if s2 is None:
                return S.op(eng, lambda e: e.tensor_scalar(out=out, in0=in0, scalar1=s1, scalar2=None, op0=op0),
                            reads, writes)
            return S.op(eng, lambda e: e.tensor_scalar(out=out, in0=in0, scalar1=s1, scalar2=s2, op0=op0, op1=op1),
                        reads, writes)

        def stt(out, in0, scalar, in1, op0, op1, reads, writes):
            return S.op("dve", lambda e: e.scalar_tensor_tensor(out=out, in0=in0, scalar=scalar, in1=in1,
                                                                 op0=op0, op1=op1), reads, writes)

        def cp(out, in_, reads, writes, eng="dve"):
            if eng == "act":
                return act(out, in_, AF.Copy, reads, writes)
            return S.op(eng, lambda e: e.tensor_copy(out=out, in_=in_), reads, writes)

        def memset(ap, val, writes, eng="dve"):
            return S.op(eng, lambda e: e.memset(ap, val), (), writes)

        def c64(t):
            return t[:, :].rearrange("p (c t) -> p c t", t=64)

        def red(out, in_, reads, writes):
            return S.op("dve", lambda e: e.tensor_reduce(out=out, in_=in_, axis=AX.X, op=ALU.add), reads, writes)

        dma(chv[:, :, :], chv_d.rearrange("p (a b) -> p a b", b=8), (), ["chv"])
        dma(muT[:, :], mu_d[:, :], (), ["mu"])
        dma(normw[:, :], normw_d[:, :], (), ["normw"])
        dma(fnw[:, :], fnw_d[:, :], (), ["fnw"])
        dma(cstf[:, :], cst_d[:, :], (), ["cstf"])
        dma(mskf[:, :], msk_d[:, :], (), ["mskf"])
        dma(rst[:, :], rst_d[:, :], (), ["rst"])
        upf = stg[0][:, :, :].rearrange("p a b -> p (a b)")
        dma(upf, lora_up[:, :], (), ["stg0"])
        cp(cst[:, :, :], cstf[:, :].rearrange("p (a b) -> p a b", b=128), ["cstf"], ["cst"])
        cp(maskA[:, :], mskf[0:64, 0:128], ["mskf"], ["maskA"])
        cp(maskN[:, :], mskf[0:64, 128:192], ["mskf"], ["maskN"])
        for j in range(2):
            cp(maskT[:, j, :, :], mskf[:, 192:448].rearrange("p (a b) -> p a b", b=128), ["mskf"], ["maskT"])
        cp(identP[:, :], cstf[0:64, 0:64], ["cstf"], ["identP"])
        cp(upb[:, :], upf, ["stg0"], ["upb"])
        ts(omu[:, :], muT[:, :], -1.0, ALU.mult, ["mu"], ["omu"], s2=1.0, op1=ALU.add)
        ts(omk[:, :], chv[:, 3, :], -1.0, ALU.mult, ["chv"], ["omk"], s2=1.0, op1=ALU.add)
        cp(normw_bc[:, :, :], bcast(normw[:, :], [[1, 8], [0, 128]]), ["normw"], ["normw_bc"])

        W0, A0, KKS, KA, RK, LNW, LNB = range(7)

        stg_i = [0]

        def load_wtile(col0, dst, key):
            i = stg_i[0] % 2
            stg_i[0] += 1
            src = w_in[:, col0:col0 + 128].rearrange("(kc p) c -> p kc c", p=128)
            for h in range(2):
                dma(stg[i][:, h * 4:h * 4 + 4, :], src[:, h * 4:h * 4 + 4, :], (), [f"stg{i}/{h}"])
            tt(dst[:, :, :], stg[i][:, :, :], normw_bc[:, :, :], ALU.mult, [f"stg{i}", "normw_bc"], [key], eng="pool")

        for mt in range(16):
            i = stg_i[0] % 2
            stg_i[0] += 1
            dma(stg[i][:, :, :].rearrange("p a b -> p (a b)"), w_out[mt * 128:(mt + 1) * 128, :], (), [f"stg{i}"])
            cp(WO[:, mt, :], stg[i][:, :, :].rearrange("p a b -> p (a b)"), [f"stg{i}"], ["WO"], eng="pool")

        def proj_block(W, wkey, psi, t0):
            for kc in range(8):
                mm(ps[psi][:, :], W[:, kc, :], xnT[:, kc, t0:t0 + 512], [wkey, "xnT"], [f"ps{psi}"],
                   start=(kc == 0), stop=(kc == 7), inc=(kc == 7))

        def shift_evac(psi, mcol, prevlast, pkey, out, okey, tmp, tkey):
            p = ps[psi]
            act(tmp[:, :], p[:, :], AF.Copy, [f"ps{psi}", "omu"], [tkey], scale=omu[:, mcol:mcol + 1])
            stt(out[:, 1:512], p[:, 0:511], muT[:, mcol:mcol + 1], tmp[:, 1:512], ALU.mult, ALU.add,
                [f"ps{psi}", "mu", tkey], [okey])
            stt(out[:, 0:1], prevlast, muT[:, mcol:mcol + 1], tmp[:, 0:1], ALU.mult, ALU.add,
                [pkey, "mu", tkey], [okey])
            cp(prevlast, p[:, 511:512], [f"ps{psi}"], [pkey], eng="act")

        def checkpoint(name, src_ap, key, n):
            if stop != name:
                return
            dtile = stg[1][:, :, :].rearrange("p a b -> p (a b)")
            memset(dtile[:, :], 0.0, ["stg1"])
            n = min(n, 1024)
            cp(dtile[0:src_ap.shape[0], 0:n], src_ap[:, 0:n], [key], ["stg1"])
            dma(dbg[:, :], dtile[:, :], ["stg1"], ["dbg"])
            S.barrier()
            S.stopped = True

        for b in range(NB):
          try:
              with ExitStack() as s1:
                  xt = [sb(f"xt{i}", [128, D], F32, s1) for i in range(2)]
                  sqt = sb("sqt", [128, D], F32, s1)
                  xnb = sb("xnb", [128, D], BF16, s1)
                  ssA = sb("ssA", [128, 2], F32, s1)
                  for tti in range(16):
                      xi = tti % 2
                      dma(xt[xi][:, :], x[b, tti * 128:(tti + 1) * 128, :], (), [f"xt{xi}"])
                      act(sqt[:, :], xt[xi][:, :], AF.Square, [f"xt{xi}"], ["sqt"])
                      red(ssA[:, 0:1], sqt[:, :], ["sqt"], ["ssA"])
                      act(ssA[:, 1:2], ssA[:, 0:1], AF.Ln, ["ssA"], ["ssA1"], scale=1.0 / D, bias=RMS_EPS)
                      act(ssA[:, 1:2], ssA[:, 1:2], AF.Exp, ["ssA1"], ["ssA1"], scale=-0.5)
                      ts(xnb[:, :], xt[xi][:, :], ssA[:, 1:2], ALU.mult, [f"xt{xi}", "ssA1"], ["xnb"])
                      for kc in range(8):
                          tr(psb[0][:, kc * 128:(kc + 1) * 128], xnb[:, kc * 128:(kc + 1) * 128], ident,
                             ["xnb", "cst"], ["ps0"], inc=(kc == 7))
                      cp(xnT[:, :, tti * 128:(tti + 1) * 128], psb[0][:, :].rearrange("p (a b) -> p a b", b=128),
                         ["ps0"], ["xnT"], eng="act")
                  S.barrier()
              checkpoint("A", xnT[:, 3, :], "xnT", SEQ)

              with ExitStack() as s1:
                  lo = sb("lo", [128, 512], F32, s1)
                  lotmp = sb("lotmp", [128, 512], F32, s1)
                  plast = sb("plastl", [128, 1], F32, s1)
                  load_wtile(3072, Wl, "Wl")
                  memset(plast[:, :], 0.0, ["plastl"])
                  for tb in range(4):
                      t0 = tb * 512
                      proj_block(Wl, "Wl", 0, t0)
                      shift_evac(0, 24, plast[:, 0:1], "plastl", lo, "lo", lotmp, "lotmp")
                      act(lora_bf[0:64, t0:t0 + 512], lo[0:64, :], AF.Tanh, ["lo"], ["lora_bf"])
                      act(lora_bf[64:128, t0:t0 + 512], lo[64:128, :], AF.Copy, ["lo"], ["lora_bf"])
                  S.barrier()
              checkpoint("L", lora_bf[:, :], "lora_bf", SEQ)

              units = [("r", hp) for hp in range(8)] + [("a", hp) for hp in range(8)]

              def unit_cols(u):
                  kind, hp = u
                  if kind == "r":
                      return [hp * 128, 1024 + hp * 128, 2048 + hp * 128, GATE0 + hp * 128]
                  return [ATT0 + hp * 128, ATT0 + 1024 + hp * 128, ATT0 + 2048 + hp * 128, ATTG0 + hp * 128]

              def prefetch(ui):
                  if ui >= len(units):
                      return
                  for i, c in enumerate(unit_cols(units[ui])):
                      load_wtile(c, Wb[ui % 2][i], f"Wb{ui % 2}_{i}")

              prefetch(0)

              with ExitStack() as s1:
                  f = lambda n: sb(n, [128, 512], F32, s1)
                  R, K, TV, SIGW, AA, CS, EW, EWI, KK, RN, Bt, FIN = [f(n) for n in
                      ("R", "K", "TV", "SIGW", "AA", "CS", "EW", "EWI", "KK", "RN", "Bt", "FIN")]
                  BONp = [f(f"BON{i}") for i in range(2)]
                  vTp = [sb(f"vT{i}", [128, 512], BF16, s1) for i in range(2)]
                  sgp = [sb(f"sg{i}", [128, 512], BF16, s1) for i in range(2)]
                  SQ = sb("SQ", [128, 512], BF16, s1)
                  T1 = sb("T1", [128, 512], BF16, s1)
                  mixo = sb("mixo", [128, 512], BF16, s1)
                  WCtp = [sb(f"WCt{i}", [128, 8], F32, s1) for i in range(2)]
                  ARp = [sb(f"AR{i}", [128, 8, 2, 64], BF16, s1) for i in range(2)]
                  BKp = [sb(f"BK{i}", [128, 2, 512], BF16, s1) for i in range(2)]
                  BKhp = [sb(f"BKh{i}", [128, 2, 512], BF16, s1) for i in range(2)]
                  AbT = [sb(f"AbT{j}", [64, 8, 128], BF16, s1) for j in range(2)]
                  AkT = [sb(f"AkT{j}", [64, 8, 128], BF16, s1) for j in range(2)]
                  PM = [sb(f"PM{j}", [64, 8, 128], BF16, s1) for j in range(2)]
                  NN = [sb(f"NN{j}", [64, 8, 64], BF16, s1) for j in range(2)]
                  tokB = sb("tokB", [64, 8, 128], BF16, s1)
                  tokK = sb("tokK", [64, 8, 128], BF16, s1)
                  tokV = sb("tokV", [64, 8, 128], BF16, s1)
                  X2all = sb("X2all", [64, 8, 128], F32, s1)
                  Y2all = sb("Y2all", [64, 8, 128], F32, s1)
                  KV = sb("KV", [128, 8, 64], F32, s1)
                  Ysb = sb("Ysb", [64, 8, 128], F32, s1)
                  ARsp = [sb(f"ARs{i}", [128, 8, 2, 64], BF16, s1) for i in range(2)]
                  HT = sb("HT", [128, 64], F32, s1)
                  Xsb = sb("Xsb", [64, 128], BF16, s1)
                  Usb = sb("Usb", [64, 128], BF16, s1)
                  H32 = sb("H32", [128, 64], F32, s1)
                  Hbf2 = sb("Hbf2", [128, 2, 64], BF16, s1)
                  plr = sb("plr", [128, 3], F32, s1)
                  YSQ = sb("YSQ", [64, 1024], F32, s1)
                  ynb = sb("ynb", [64, 8, 128], BF16, s1)
                  st = sb("st", [64, 6, 16], F32, s1)

                  seq = [(ui, tb) for ui in range(8) for tb in range(4)]

                  def ctx(n):
                      ui, tb = seq[n]
                      hp = units[ui][1]
                      return ui, tb, hp, n % 2, tb * 512

                  def stage1(n):
                      ui, tb, hp, pb, t0 = ctx(n)
                      Wr, Wk, Wv, Wg = Wb[ui % 2]
                      wk = [f"Wb{ui % 2}_{i}" for i in range(4)]
                      cv = lambda v: chv[:, v, hp:hp + 1]
                      AR, ARs, BK, BKh, vT, sg, BON, WCt = (ARp[pb], ARsp[pb], BKp[pb], BKhp[pb], vTp[pb], sgp[pb],
                                                            BONp[pb], WCtp[pb])
                      if tb == 0:
                          memset(plr[:, :], 0.0, ["plr0", "plr1", "plr2"])
                      proj_block(Wr, wk[0], 0, t0)
                      proj_block(Wk, wk[1], 1, t0)
                      proj_block(Wv, wk[2], 2, t0)
                      shift_evac(0, hp, plr[:, 0:1], "plr0", R, "R", R, "R")
                      shift_evac(1, 8 + hp, plr[:, 1:2], "plr1", K, "K", K, "K")
                      shift_evac(2, 16 + hp, plr[:, 2:3], "plr2", vT, f"vT{pb}", TV, "TV")
                      proj_block(Wg, wk[3], 0, t0)
                      mm(ps[1][:, :], upb[0:64, hp * 128:(hp + 1) * 128], lora_bf[0:64, t0:t0 + 512],
                         ["upb", "lora_bf"], ["ps1"])
                      mm(ps[2][:, :], upb[64:128, hp * 128:(hp + 1) * 128], lora_bf[64:128, t0:t0 + 512],
                         ["upb", "lora_bf"], ["ps2"])
                      act(sg[:, :], ps[0][:, :], AF.Silu, ["ps0"], [f"sg{pb}"])
                      act(SIGW[:, :], ps[1][:, :], AF.Sigmoid, ["ps1", "chv"], ["SIGW"], bias=cv(W0))
                      act(AA[:, :], ps[2][:, :], AF.Sigmoid, ["ps2", "chv"], ["AA"], bias=cv(A0))
                      S.op("dve", lambda e: e.tensor_tensor_scan(out=CS[:, :], data0=rst[:, :], data1=SIGW[:, :],
                                                                 initial=0.0, op0=ALU.mult, op1=ALU.add),
                           ["rst", "SIGW"], ["CS"])
                      tt(SIGW[:, :], CS[:, :], SIGW[:, :], ALU.subtract, ["CS", "SIGW"], ["SIGW"])
                      act(EW[:, :], CS[:, :], AF.Exp, ["CS"], ["EW"], scale=-C0)
                      act(EWI[:, :], CS[:, :], AF.Exp, ["CS"], ["EWI"], scale=C0)
                      act(SIGW[:, :], SIGW[:, :], AF.Exp, ["SIGW"], ["SIGW"], scale=-C0)
                      cp(WCt[:, :], EW[:, 63:512:64], ["EW"], [f"WCt{pb}"])
                      tt(CS[:, :].rearrange("p (c t) -> p c t", t=64), EWI[:, :].rearrange("p (c t) -> p c t", t=64),
                         bcast(WCt[:, :], [[1, 8], [0, 64]]), ALU.mult, ["EWI", f"WCt{pb}"], ["CS"])
                      ts(KK[:, :], K[:, :], cv(KKS), ALU.mult, ["K", "chv"], ["KK"])
                      act(SQ[:, :], KK[:, :], AF.Square, ["KK"], ["SQ"])
                      mm(ps[0][:, :], blockones, SQ[:, :], ["cst", "SQ"], ["ps0"])
                      act(RN[:, :], ps[0][:, :], AF.Ln, ["ps0"], ["RN"])
                      act(RN[:, :], RN[:, :], AF.Exp, ["RN"], ["RN"], scale=-0.5)
                      tt(KK[:, :], KK[:, :], RN[:, :], ALU.mult, ["KK", "RN"], ["KK"])
                      tt(Bt[:, :], KK[:, :], AA[:, :], ALU.mult, ["KK", "AA"], ["Bt"])
                      ts(AA[:, :], AA[:, :], cv(KA), ALU.mult, ["AA", "chv", "omk"], ["AA"],
                         s2=omk[:, hp:hp + 1], op1=ALU.add)
                      tt(K[:, :], K[:, :], AA[:, :], ALU.mult, ["K", "AA"], ["K"])
                      stt(T1[:, :], R[:, :], cv(RK), K[:, :], ALU.mult, ALU.mult, ["R", "K", "chv"], ["T1"])
                      mm(ps[1][:, :], blockones, T1[:, :], ["cst", "T1"], ["ps1"])
                      tt(BON[:, :], ps[1][:, :], vT[:, :], ALU.mult, ["ps1", f"vT{pb}"], [f"BON{pb}"])
                      tt(AR[:, :, 1, :], c64(R), c64(EW), ALU.mult, ["R", "EW"], [f"AR{pb}"])
                      stt(AR[:, :, 0, :], c64(KK), -1.0, c64(SIGW), ALU.mult, ALU.mult, ["KK", "SIGW"], [f"AR{pb}"])
                      cp(ARs[:, :, 0, :], AR[:, :, 1, :], [f"AR{pb}"], [f"ARs{pb}"], eng="pool")
                      cp(ARs[:, :, 1, :], AR[:, :, 0, :], [f"AR{pb}"], [f"ARs{pb}"], eng="pool")
                      tt(BK[:, 0, :], Bt[:, :], EWI[:, :], ALU.mult, ["Bt", "EWI"], [f"BK{pb}"], eng="pool")
                      tt(BK[:, 1, :], K[:, :], EWI[:, :], ALU.mult, ["K", "EWI"], [f"BK{pb}"], eng="pool")
                      tt(BKh[:, 0, :], Bt[:, :], CS[:, :], ALU.mult, ["Bt", "CS"], [f"BKh{pb}"], eng="pool")
                      tt(BKh[:, 1, :], K[:, :], CS[:, :], ALU.mult, ["K", "CS"], [f"BKh{pb}"], eng="pool")


                  def stage2(n):
                      ui, tb, hp, pb, t0 = ctx(n)
                      AR, ARs, BK, BKh, vT, sg, BON, WCt = (ARp[pb], ARsp[pb], BKp[pb], BKhp[pb], vTp[pb], sgp[pb],
                                                            BONp[pb], WCtp[pb])
                      if tb == 0:
                          prefetch(ui + 1)
                          memset(H32[:, :], 0.0, ["H32"])
                          memset(Hbf2[:, :, :], 0.0, ["Hbf2"], eng="pool")
                      for j in range(2):
                          kp = slice(j * 64, j * 64 + 64)
                          for c in range(8):
                              cs_ = slice(c * 64, c * 64 + 64)
                              bk = c // 4
                              col = (c % 4) * 128
                              mm(ps[0 + bk][0:64, col:col + 128], BK[kp, 0, cs_], AR[kp, c, :, :].rearrange("p a b -> p (a b)"),
                                 [f"BK{pb}", f"AR{pb}"], [f"ps{bk}"], inc=(c % 4 == 3))
                          for c in range(8):
                              cs_ = slice(c * 64, c * 64 + 64)
                              bk = c // 4
                              col = (c % 4) * 128
                              mm(ps[2 + bk][0:64, col:col + 128], BK[kp, 1, cs_], AR[kp, c, :, :].rearrange("p a b -> p (a b)"),
                                 [f"BK{pb}", f"AR{pb}"], [f"ps{2 + bk}"], inc=(c % 4 == 3))
                          for c in range(8):
                              cs_ = slice(c * 64, c * 64 + 64)
                              mm(ps[4][0:64, cs_], AR[kp, c, 0, :], BK[kp, 0, cs_], [f"BK{pb}", f"AR{pb}"], ["ps4"],
                                 inc=(c == 7))
                          mA = bcast(maskA[:, :], [[0, 4], [1, 128]])
                          for bk in range(2):
                              tt(AbT[j][:, bk * 4:bk * 4 + 4, :], ps[bk][0:64, :].rearrange("p (c t) -> p c t", t=128),
                                 mA, ALU.mult, [f"ps{bk}", "maskA"], [f"AbT{j}"])
                              tt(AkT[j][:, bk * 4:bk * 4 + 4, :],
                                 ps[2 + bk][0:64, :].rearrange("p (c t) -> p c t", t=128),
                                 mA, ALU.mult, [f"ps{2 + bk}", "maskA"], [f"AkT{j}"])
                          tt(NN[j][:, :, :], ps[4][0:64, :].rearrange("p (c t) -> p c t", t=64),
                             bcast(maskN[:, :], [[0, 8], [1, 64]]), ALU.mult, ["ps4", "maskN"], [f"NN{j}"])
                          cp(PM[j][:, :, 0:64], bcast(identP[:, :], [[0, 8], [1, 64]]), ["identP"], [f"PM{j}"],
                             eng="pool")
                          cp(PM[j][:, :, 64:128], AbT[j][:, :, 0:64], [f"AbT{j}"], [f"PM{j}"], eng="pool")
                      for lvl in range(6):
                          last = lvl == 5
                          for j in range(2):
                              pq0, nnb = (5, 7) if j == 0 else (0, 2)
                              for c in range(8):
                                  bk = c // 4
                                  col = (c % 4) * 128
                                  if last:
                                      mm(ps[pq0 + bk][0:64, col:col + 64], NN[j][:, c, :], PM[j][:, c, 0:64],
                                         [f"NN{j}", f"PM{j}"], [f"ps{pq0 + bk}"], inc=(c % 4 == 3))
                                  else:
                                      mm(ps[pq0 + bk][0:64, col:col + 128], NN[j][:, c, :], PM[j][:, c, :],
                                         [f"NN{j}", f"PM{j}"], [f"ps{pq0 + bk}"], inc=(c % 4 == 3))
                              if not last:
                                  for c in range(8):
                                      mm(ps[nnb][0:64, c * 64:c * 64 + 64], PM[j][:, c, 64:128], NN[j][:, c, :],
                                         [f"NN{j}", f"PM{j}"], [f"ps{nnb}"], inc=(c == 7))
                              for bk in range(2):
                                  pv = ps[pq0 + bk][0:64, :].rearrange("p (c t) -> p c t", t=128)
                                  tt(PM[j][:, bk * 4:bk * 4 + 4, 0:64], pv[:, :, 0:64],
                                     PM[j][:, bk * 4:bk * 4 + 4, 0:64], ALU.add,
                                     [f"ps{pq0 + bk}", f"PM{j}"], [f"PM{j}"])
                                  if not last:
                                      cp(PM[j][:, bk * 4:bk * 4 + 4, 64:128], pv[:, :, 64:128],
                                         [f"ps{pq0 + bk}"], [f"PM{j}"], eng="act")
                              if not last:
                                  cp(NN[j][:, :, :], ps[nnb][0:64, :].rearrange("p (c t) -> p c t", t=64),
                                     [f"ps{nnb}"], [f"NN{j}"], eng="act")
                      for qi, (src, dst, dkey, skey) in enumerate(((BKh[:, 0, :], tokB, "tokB", f"BKh{pb}"),
                                                                    (BKh[:, 1, :], tokK, "tokK", f"BKh{pb}"),
                                                                    (vT[:, :], tokV, "tokV", f"vT{pb}"))):
                          for c in range(8):
                              tr(psb[qi][0:64, c * 128:(c + 1) * 128], src[:, c * 64:(c + 1) * 64], ident,
                                 [skey, "cst"], [f"ps{qi}"], inc=(c == 7))
                          cp(dst[:, :, :], psb[qi][0:64, :].rearrange("p (c t) -> p c t", t=128), [f"ps{qi}"], [dkey],
                             eng=("act" if qi != 1 else "dve"))


                  def chain(n, lst):
                      ui, tb, hp, pb, t0 = ctx(n)
                      AR, ARs, BK, BKh, vT, sg, BON, WCt = (ARp[pb], ARsp[pb], BKp[pb], BKhp[pb], vTp[pb], sgp[pb],
                                                            BONp[pb], WCtp[pb])
                      for off, dst, dkey, banks in ((0, X2all, "X2all", (0, 1)), (64, Y2all, "Y2all", (2, 3))):
                          for c in range(8):
                              bk = banks[c // 4]
                              col = (c % 4) * 128
                              for j in range(2):
                                  jc = slice(j * 64, j * 64 + 64)
                                  mm(ps[bk][0:64, col + j * 64:col + j * 64 + 64], AkT[j][:, c, off:off + 64],
                                     tokV[:, c, jc], [f"AkT{j}", "tokV"], [f"ps{bk}"], inc=(c % 4 == 3 and j == 1))
                          for h in range(2):
                              cp(dst[:, h * 4:h * 4 + 4, :], ps[banks[h]][0:64, :].rearrange("p (c t) -> p c t", t=128),
                                 [f"ps{banks[h]}"], [dkey], eng=("act" if h else "dve"))
                      for c in range(8):
                          cs_ = slice(c * 64, c * 64 + 64)
                          mm(ps[4][0:64, cs_], tokK[:, c, 0:64], tokV[:, c, 0:64], ["tokK", "tokV"], ["ps4"], inc=False)
                          mm(ps[4][64:128, cs_], tokK[:, c, 64:128], tokV[:, c, 64:128], ["tokK", "tokV"], ["ps4"],
                             inc=(c == 7))
                      cp(KV[:, :, :], ps[4][:, :].rearrange("p (c v) -> p c v", v=64), ["ps4"], ["KV"], eng="act")

                      H2 = Hbf2[:, :, :].rearrange("p a b -> p (a b)")
                      for c in range(8):
                          cs_ = slice(c * 64, c * 64 + 64)
                          mm(ps[3][:, 0:128], AR[:, c, :, :].rearrange("p a b -> p (a b)"), H2, [f"AR{pb}", "Hbf2"], ["ps3"])
                          mm(ps[5][:, 0:128], ARs[:, c, :, :].rearrange("p a b -> p (a b)"), H2, [f"ARs{pb}", "Hbf2"], ["ps5"])
                          tt(Xsb[:, :], ps[3][0:64, 0:128], X2all[:, c, :], ALU.add, ["ps3", "X2all"], ["Xsb"])
                          for j in range(2):
                              jc = slice(j * 64, j * 64 + 64)
                              mm(ps[7][0:64, jc], PM[j][:, c, 0:64], Xsb[:, jc], [f"PM{j}", "Xsb"], ["ps7"], inc=(j == 1))
                          cp(Usb[:, :], ps[7][0:64, 0:128], ["ps7"], ["Usb"], eng="act")
                          stt(HT[:, :], H32[:, :], WCt[:, c:c + 1], KV[:, c, :], ALU.mult, ALU.add,
                              ["H32", f"WCt{pb}", "KV"], ["HT"])
                          mm(ps[4][0:64, 0:64], tokB[:, c, 0:64], Usb[:, 0:64], ["tokB", "Usb"], ["ps4"], inc=False)
                          mm(ps[4][64:128, 0:64], tokB[:, c, 64:128], Usb[:, 64:128], ["tokB", "Usb"], ["ps4"])
                          tt(H32[:, :], ps[4][:, 0:64], HT[:, :], ALU.add, ["ps4", "HT"], ["H32"])
                          cp(Hbf2[0:64, 0, :], H32[0:64, :], ["H32"], ["Hbf2"], eng="act")
                          cp(Hbf2[64:128, 1, :], H32[64:128, :], ["H32"], ["Hbf2"], eng="pool")
                          tt(Ysb[:, c, :], ps[5][0:64, 0:128], Y2all[:, c, :], ALU.add, ["ps5", "Y2all"], ["Ysb"])
                          for j in range(2):
                              jc = slice(j * 64, j * 64 + 64)
                              mm(ps[6][0:64, jc], AbT[j][:, c, 64:128], Usb[:, jc], [f"AbT{j}", "Usb"], ["ps6"],
                                 inc=(j == 1))
                          tt(Ysb[:, c, :], ps[6][0:64, 0:128], Ysb[:, c, :], ALU.add, ["ps6", "Ysb"], ["Ysb"])
                          S.emit_deferred(lst, (len(lst) + (7 - c)) // (8 - c))


                  def stage4(n):
                      ui, tb, hp, pb, t0 = ctx(n)
                      cv = lambda v: chv[:, v, hp:hp + 1]
                      AR, ARs, BK, BKh, vT, sg, BON, WCt = (ARp[pb], ARsp[pb], BKp[pb], BKhp[pb], vTp[pb], sgp[pb],
                                                            BONp[pb], WCtp[pb])
                      Yv = Ysb[:, :, :].rearrange("p c (j v) -> p (c j) v", v=64)
                      red(st[:, 0, :], Yv, ["Ysb"], ["st0"])
                      act(YSQ[:, :], Ysb[:, :, :].rearrange("p c t -> p (c t)"), AF.Square, ["Ysb"], ["YSQ"])
                      red(st[:, 1, :], YSQ[:, :].rearrange("p (g v) -> p g v", v=64), ["YSQ"], ["st1"])
                      ts(st[:, 2, :], st[:, 0, :], 1.0 / 64, ALU.mult, ["st0"], ["st2"])
                      tt(st[:, 3, :], st[:, 2, :], st[:, 2, :], ALU.mult, ["st2"], ["st3"])
                      stt(st[:, 4, :], st[:, 1, :], 1.0 / 64, st[:, 3, :], ALU.mult, ALU.subtract, ["st1", "st3"], ["st4"])
                      act(st[:, 5, :], st[:, 4, :], AF.Ln, ["st4"], ["st5"], bias=GN_EPS)
                      act(st[:, 5, :], st[:, 5, :], AF.Exp, ["st5"], ["st5"], scale=-0.5)
                      tt(Yv, Yv, bcast(st[:, 2, :], [[1, 16], [0, 64]]), ALU.subtract, ["Ysb", "st2"], ["Ysb"])
                      tt(ynb[:, :, :].rearrange("p c (j v) -> p (c j) v", v=64), Yv,
                         bcast(st[:, 5, :], [[1, 16], [0, 64]]), ALU.mult, ["Ysb", "st5"], ["ynb"])
                      for c in range(8):
                          for j in range(2):
                              jc = slice(j * 64, j * 64 + 64)
                              mm(ps[0][jc, c * 64:c * 64 + 64], ynb[:, c, jc], identP[:, :], ["ynb", "identP"], ["ps0"],
                                 inc=(c == 7 and j == 1))
                      ts(FIN[:, :], ps[0][:, :], cv(LNW), ALU.mult, ["ps0", "chv"], ["FIN"], s2=cv(LNB), op1=ALU.add)
                      tt(FIN[:, :], FIN[:, :], BON[:, :], ALU.add, ["FIN", f"BON{pb}"], ["FIN"])
                      tt(mixo[:, :], FIN[:, :], sg[:, :], ALU.mult, ["FIN", f"sg{pb}"], ["mixo"])
                      dma(mixs[b, hp, :, t0:t0 + 512], mixo[:, :], ["mixo"], [f"mixs{hp}"])

                  stage1(0)
                  for n in range(len(seq)):
                      stage2(n)
                      lst = []
                      if n + 1 < len(seq):
                          S.deferred = lst
                          stage1(n + 1)
                          S.deferred = None
                      chain(n, lst)
                      S.emit_deferred(lst, len(lst))
                      stage4(n)
                  S.barrier()

              with ExitStack() as s1:
                  qz = sb("qz", [128, 2, SEQ], BF16, s1)
                  memset(qz[:, :, :], 0.0, ["qz"], eng="pool")
                  kT = sb("kT", [128, SEQ], BF16, s1)
                  vTa = sb("vTa", [128, SEQ], BF16, s1)
                  sgT = sb("sgT", [128, SEQ], BF16, s1)
                  Vp = [sb(f"Vp{p}", [128, 16, 128], BF16, s1) for p in range(3)]
                  acc = sb("acc", [128, 2, SEQ], F32, s1)
                  RL = sb("RL", [128, SEQ], F32, s1)
                  mixa = sb("mixa", [128, SEQ], BF16, s1)
                  PT = [sb(f"PT{i}", [128, 2, 2, 128], BF16, s1) for i in range(3)]

                  def toks(d, i):
                      L = SEQ // d
                      g = i * 128
                      r, l0 = g // L, g % L
                      s0 = r + d * l0
                      return slice(s0, s0 + d * 127 + 1, d)

                  for ui in range(8, 16):
                      hp = units[ui][1]
                      prefetch(ui + 1)
                      Wq, Wk_, Wv_, Wg_ = Wb[ui % 2]
                      wk = [f"Wb{ui % 2}_{i}" for i in range(4)]
                      for tb in range(4):
                          t0 = tb * 512
                          proj_block(Wq, wk[0], 0, t0)
                          act(qz[0:64, 0, t0:t0 + 512], ps[0][0:64, :], AF.Copy, ["ps0"], ["qz"], scale=0.125)
                          act(qz[64:128, 1, t0:t0 + 512], ps[0][64:128, :], AF.Copy, ["ps0"], ["qz"], scale=0.125)
                          proj_block(Wk_, wk[1], 1, t0)
                          cp(kT[:, t0:t0 + 512], ps[1][:, :], ["ps1"], ["kT"])
                          proj_block(Wv_, wk[2], 2, t0)
                          act(vTa[:, t0:t0 + 512], ps[2][:, :], AF.Copy, ["ps2"], ["vTa"])
                          proj_block(Wg_, wk[3], 3, t0)
                          act(sgT[:, t0:t0 + 512], ps[3][:, :], AF.Silu, ["ps3"], ["sgT"])
                      for p, d in enumerate(PATTERNS):
                          for half in range(2):
                              pi = 4 + (p * 2 + half) % 2
                              for ii in range(8):
                                  i = half * 8 + ii
                                  tr(psb[pi][:, ii * 128:(ii + 1) * 128], vTa[:, toks(d, i)], ident, ["vTa", "cst"],
                                     [f"ps{pi}"], inc=(ii == 7))
                              cp(Vp[p][:, half * 8:half * 8 + 8, :], psb[pi][:, :].rearrange("p (a b) -> p a b", b=128),
                                 [f"ps{pi}"], [f"Vp{p}"], eng=("act" if half else "dve"))
                      blocks = [(p, d, i) for p, d in enumerate(PATTERNS) for i in range(16)]
                      SB_, OB_ = (0, 1, 6), (2, 3, 7)
                      LOOK = 2

                      def kbs_of(d, i):
                          has_prev = ((i * 128) % (SEQ // d)) != 0
                          return ([(0, i - 1)] if has_prev else []) + [(1, i)]

                      def emit_scores(bi):
                          p, d, i = blocks[bi]
                          kbs = kbs_of(d, i)
                          sbk = SB_[bi % 3]
                          sv = ps[sbk][:, :].rearrange("p (j k q) -> p j k q", j=2, k=2)
                          pt = PT[bi % 3]
                          for j in range(2):
                              for (kbi, kt) in kbs:
                                  mm(sv[:, j, kbi, :], kT[:, toks(d, kt)], qz[:, j, toks(d, i)], ["kT", "qz"],
                                     [f"ps{sbk}"], inc=(j == 1 and kbi == 1))
                          k0 = kbs[0][0]
                          act(pt[:, :, k0:2, :], sv[:, :, k0:2, :], AF.Exp, [f"ps{sbk}"], [f"PT{bi % 3}"])
                          tt(pt[:, :, k0:2, :], pt[:, :, k0:2, :], maskT[:, :, k0:2, :], ALU.mult,
                             [f"PT{bi % 3}", "maskT"], [f"PT{bi % 3}"], eng="pool")

                      def emit_pv(bi):
                          p, d, i = blocks[bi]
                          kbs = kbs_of(d, i)
                          obk = OB_[bi % 3]
                          pt = PT[bi % 3]
                          ov = ps[obk][:, :].rearrange("p (r q) -> p r q", q=128)
                          for rgn in range(4):
                              j = rgn % 2
                              for n_, (kbi, kt) in enumerate(kbs):
                                  lhs = Vp[p][:, kt, :] if rgn < 2 else ones
                                  mm(ov[:, rgn, :], lhs, pt[:, j, kbi, :], [f"Vp{p}", "cst", f"PT{bi % 3}"], [f"ps{obk}"],
                                     start=(n_ == 0), stop=(n_ == len(kbs) - 1),
                                     inc=(rgn == 3 and n_ == len(kbs) - 1))
                          for j in range(2):
                              kp = slice(j * 64, j * 64 + 64)
                              src = ov[kp, j:4:2, :]
                              dst = acc[kp, :, toks(d, i)]
                              if p == 0:
                                  cp(dst, src, [f"ps{obk}"], [f"acc{j}"], eng=("act" if j else "dve"))
                              else:
                                  tt(dst, src, dst, ALU.add, [f"ps{obk}", f"acc{j}"], [f"acc{j}"])

                      for bi in range(min(LOOK, len(blocks))):
                          emit_scores(bi)
                      for bi in range(len(blocks)):
                          if bi + LOOK < len(blocks):
                              emit_scores(bi + LOOK)
                          emit_pv(bi)
                      act(RL[:, :], acc[:, 1, :], AF.Ln, ["acc0", "acc1"], ["RL"])
                      act(RL[:, :], RL[:, :], AF.Exp, ["RL"], ["RL"], scale=-1.0)
                      tt(acc[:, 0, :], acc[:, 0, :], RL[:, :], ALU.mult, ["acc0", "acc1", "RL"], ["acc0", "acc1"])
                      tt(mixa[:, :], acc[:, 0, :], sgT[:, :], ALU.mult, ["acc0", "acc1", "sgT"], ["mixa"])
                      dma(mixs[b, 8 + hp, :, :], mixa[:, :], ["mixa"], [f"mixs{8 + hp}"])
                      if ui == 8:
                          checkpoint("T1", mixa[:, :], "mixa", SEQ)
                  S.barrier()

              with ExitStack() as s1:
                  xt = [sb(f"xo{i}", [128, D], F32, s1) for i in range(2)]
                  MT = [sb(f"MT{i}", [128, 16, 128], BF16, s1) for i in range(2)]
                  hT = sb("hT", [128, D], F32, s1)
                  sqo = sb("sqo", [128, D], F32, s1)
                  oT = [sb(f"oT{i}", [128, D], F32, s1) for i in range(2)]
                  sso = sb("sso", [128, 2], F32, s1)
                  allmix = [f"mixs{m}" for m in range(16)]
                  for tti in range(16):
                      xi = tti % 2
                      tsl = slice(tti * 128, (tti + 1) * 128)
                      dma(xt[xi][:, :], x[b, tsl, :], (), [f"xo{xi}"])
                      for q4 in range(4):
                          dma(MT[xi][:, q4 * 4:q4 * 4 + 4, :], mixs[b, q4 * 4:q4 * 4 + 4, :, tsl].rearrange("m p t -> p m t"),
                              allmix[q4 * 4:q4 * 4 + 4], [f"MT{xi}/{q4}"])
                      for half in range(2):
                          for mt in range(16):
                              mm(ps[6 + half][:, :], MT[xi][:, mt, :], WO[:, mt, half * 512:(half + 1) * 512],
                                 [f"MT{xi}", "WO"], [f"ps{6 + half}"], start=(mt == 0), stop=(mt == 15), inc=(mt == 15))
                          tt(hT[:, half * 512:(half + 1) * 512], ps[6 + half][:, :], xt[xi][:, half * 512:(half + 1) * 512],
                             ALU.add, [f"ps{6 + half}", f"xo{xi}"], ["hT"])
                      act(sqo[:, :], hT[:, :], AF.Square, ["hT"], ["sqo"])
                      red(sso[:, 0:1], sqo[:, :], ["sqo"], ["sso"])
                      act(sso[:, 1:2], sso[:, 0:1], AF.Ln, ["sso"], ["sso1"], scale=1.0 / D, bias=RMS_EPS)
                      act(sso[:, 1:2], sso[:, 1:2], AF.Exp, ["sso1"], ["sso1"], scale=-0.5)
                      stt(oT[xi][:, :], hT[:, :], sso[:, 1:2], fnw[:, :], ALU.mult, ALU.mult, ["hT", "sso1", "fnw"],
                          [f"oT{xi}"])
                      dma(y[b, tsl, :], oT[xi][:, :], [f"oT{xi}"], [f"y{b}_{tti}"])
                  S.barrier()
          except _Stop:
            break
        S.barrier()
        print(f"[kernel] ops={S.nops} waits={S.nwaits} cnt={S.cnt}")
    return nc


def _consts():
    ident = np.eye(128, dtype=np.float32)
    blockones = np.zeros((128, 128), np.float32)
    blockones[0:64, 0:64] = 1.0
    blockones[64:128, 64:128] = 1.0
    ones = np.ones((128, 128), np.float32)
    cst = np.concatenate([ident, blockones, ones], axis=1)
    s = np.arange(64)[:, None]
    t = np.arange(64)[None, :]
    maskA = np.concatenate([(s < t), (s <= t)], axis=1).astype(np.float32)
    maskN = (np.arange(64)[:, None] > np.arange(64)[None, :]).astype(np.float32)
    k = np.arange(128)[:, None]
    q = np.arange(128)[None, :]
    mprev = (k >= q).astype(np.float32)
    mdiag = (k <= q).astype(np.float32)
    msk = np.zeros((128, 448), np.float32)
    msk[0:64, 0:128] = maskA
    msk[0:64, 128:192] = maskN
    msk[:, 192:320] = mprev
    msk[:, 320:448] = mdiag
    rst = np.ones((128, 512), np.float32)
    rst[:, 0::64] = 0.0
    return cst, msk, rst


_NC_CACHE = {}


def kernel(x, norm_w, w_in, mu_shift, w0, w_up, a0, a_up, k_k, k_a, r_k, ln_x_w, ln_x_b, w_out, final_norm_w):
    f = lambda a: np.ascontiguousarray(np.asarray(a, dtype=np.float32))
    x = f(x)
    cst, msk, rst = _consts()
    pc = lambda v: f(v).reshape(8, 128).T
    chv = np.ascontiguousarray(np.stack([pc(w0), pc(a0), pc(k_k), pc(k_a), pc(np.asarray(r_k).reshape(-1)),
                                         pc(ln_x_w), pc(ln_x_b)], axis=1).reshape(128, 56))
    mu = np.ascontiguousarray(f(mu_shift).reshape(25, 128).T)
    normw = np.ascontiguousarray(f(norm_w).reshape(8, 128).T)
    fnw = np.ascontiguousarray(np.broadcast_to(f(final_norm_w)[None, :], (128, D)))
    lora_up = np.ascontiguousarray(np.concatenate([f(w_up), f(a_up)], axis=0))
    shared = {"w_in": f(w_in), "w_out": f(w_out), "lora_up": lora_up, "chv": chv, "mu": mu, "normw": normw,
              "fnw": fnw, "cst": cst, "msk": msk, "rst": rst}
    if "nc" not in _NC_CACHE:
        _NC_CACHE["nc"] = build()
    nc = _NC_CACHE["nc"]
    in_maps = [dict(shared, x=np.ascontiguousarray(x[c * NB:(c + 1) * NB])) for c in range(NCORES)]
    res = run_bass_kernel_spmd(nc, in_maps, core_ids=list(range(NCORES)))
    return np.concatenate([r["y"] for r in res.results], axis=0).astype(np.float32)
```

```python
import math
from contextlib import ExitStack

import numpy as np
import concourse.bass as bass
import concourse.mybir as mybir
from concourse.bass_utils import run_bass_kernel_spmd

F32 = mybir.dt.float32
BF16 = mybir.dt.bfloat16
AF = mybir.ActivationFunctionType
ALU = mybir.AluOpType
AX = mybir.AxisListType

NCORES = 8
NB = 2
SEQ = 2048
D = 1024
NIN = 8320
SHIFT_COLS = 3200
GATE0 = 3200
ATT0 = 4224
ATTG0 = 7296
C0 = math.exp(-0.5)
RMS_EPS = 1e-5
GN_EPS = 64e-5
PATTERNS = (1, 4, 16)


class Sched:
    def __init__(self, nc, es, ndma=12):
        self.nc = nc
        self.eng = {"pe": nc.tensor, "dve": nc.vector, "act": nc.scalar, "pool": nc.gpsimd, "sp": nc.sync}
        self.sem = {e: es.enter_context(nc.semaphore("sem_" + e)) for e in ("pe", "dve", "act", "pool")}
        self.cnt = {e: 0 for e in self.sem}
        self.dsem = [es.enter_context(nc.semaphore(f"dsem{i}")) for i in range(ndma)]
        self.dcnt = [0] * ndma
        self.dnext = 0
        self.waited = {e: {} for e in self.eng}
        self.lastw = {}
        self.readers = {}
        self.children = {}
        self.nwaits = 0
        self.nops = 0

    def _semobj(self, sk):
        return self.sem[sk[1]] if sk[0] == "e" else self.dsem[sk[1]]

    def _related(self, k):
        out = [k]
        if "/" in k:
            out.append(k.split("/")[0])
        else:
            out.extend(self.children.get(k, ()))
        return out

    def _wait(self, e, sk, val):
        if self.waited[e].get(sk, 0) < val:
            self.eng[e].wait_ge(self._semobj(sk), val)
            self.waited[e][sk] = val
            self.nwaits += 1

    stopped = False
    deferred = None

    def emit_deferred(self, lst, k):
        for _ in range(min(k, len(lst))):
            e, fn, reads, writes, dma, inc = lst.pop(0)
            self.op(e, fn, reads, writes, dma=dma, inc=inc)

    def op(self, e, fn, reads=(), writes=(), dma=False, inc=True):
        if self.stopped:
            return None
        if self.deferred is not None:
            self.deferred.append((e, fn, tuple(reads), tuple(writes), dma, inc))
            return None
        deps = {}

        def add(ev):
            sk, val = ev
            if e == "pe" and sk == ("e", "pe"):
                return
            if deps.get(sk, 0) < val:
                deps[sk] = val

        for k0 in reads:
            for k in self._related(k0):
                if k in self.lastw:
                    add(self.lastw[k])
        for k0 in writes:
            for k in self._related(k0):
                if k in self.lastw:
                    add(self.lastw[k])
                for ev in self.readers.get(k, {}).items():
                    add(ev)
        for sk, val in deps.items():
            self._wait(e, sk, val)
        if dma:
            idx = self.dnext
            self.dnext = (self.dnext + 1) % len(self.dsem)
            if self.dcnt[idx] > 0:
                self._wait(e, ("d", idx), self.dcnt[idx])
            inst = fn(self.eng[e])
            inst.then_inc(self.dsem[idx], 16)
            self.dcnt[idx] += 16
            ev = (("d", idx), self.dcnt[idx])
        else:
            inst = fn(self.eng[e])
            if inc:
                inst.then_inc(self.sem[e], 1)
                self.cnt[e] += 1
                ev = (("e", e), self.cnt[e])
            else:
                ev = (("e", e), self.cnt[e] + 1)
        self.nops += 1
        for k in writes:
            if "/" in k:
                self.children.setdefault(k.split("/")[0], set()).add(k)
            self.lastw[k] = ev
            self.readers[k] = {}
        for k in reads:
            if "/" in k:
                self.children.setdefault(k.split("/")[0], set()).add(k)
            r = self.readers.setdefault(k, {})
            if r.get(ev[0], 0) < ev[1]:
                r[ev[0]] = ev[1]
        return inst

    def barrier(self):
        if self.stopped:
            return
        for e in self.eng:
            for x in self.sem:
                if self.cnt[x] > 0:
                    self._wait(e, ("e", x), self.cnt[x])
            for i in range(len(self.dsem)):
                if self.dcnt[i] > 0:
                    self._wait(e, ("d", i), self.dcnt[i])


def bcast(ap, dims):
    return bass.AP(ap.tensor, ap.offset, [list(ap.ap[0])] + [list(d) for d in dims])


class _Stop(Exception):
    pass


def build(stop=None):
    nc = bass.Bass("TRN2", target_bir_lowering=False)
    dt = lambda n, s, k="ExternalInput", d=F32: nc.dram_tensor(n, s, d, kind=k).ap()
    x = dt("x", [NB, SEQ, D])
    w_in = dt("w_in", [D, NIN])
    w_out = dt("w_out", [2 * D, D])
    lora_up = dt("lora_up", [128, 1024])
    chv_d = dt("chv", [128, 7 * 8])
    mu_d = dt("mu", [128, 25])
    normw_d = dt("normw", [128, 8])
    fnw_d = dt("fnw", [128, D])
    cst_d = dt("cst", [128, 3 * 128])
    msk_d = dt("msk", [128, 128 + 64 + 256])
    rst_d = dt("rst", [128, 512])
    y = dt("y", [NB, SEQ, D], "ExternalOutput")
    mixs = dt("mixs", [NB, 16, 128, SEQ], "Internal", BF16)
    dbg = dt("dbg", [128, 1024], "ExternalOutput") if stop else None

    es = ExitStack()
    with es:
        S = Sched(nc, es)
        _names = {}

        def sb(n, s, d=F32, st=es):
            k = _names.get(n, 0)
            _names[n] = k + 1
            return st.enter_context(nc.sbuf_tensor(n if k == 0 else f"{n}_{k}", s, d))
        ps = [es.enter_context(nc.psum_tensor(f"ps{i}", [128, 512], F32)) for i in range(8)]
        psb = [p[:, :].bitcast(BF16) for p in ps]

        xnT = sb("xnT", [128, 8, SEQ], BF16)
        chv = sb("chv_s", [128, 7, 8])
        muT = sb("mu_s", [128, 25])
        omu = sb("omu_s", [128, 25])
        omk = sb("omk_s", [128, 8])
        normw = sb("normw_s", [128, 8])
        normw_bc = sb("normw_bc", [128, 8, 128])
        fnw = sb("fnw_s", [128, D])
        cstf = sb("cstf", [128, 3 * 128])
        cst = sb("cst_s", [128, 3, 128], BF16)
        mskf = sb("mskf", [128, 448])
        maskA = sb("maskA", [64, 128], BF16)
        maskN = sb("maskN", [64, 64], BF16)
        maskT = sb("maskT", [128, 2, 2, 128], BF16)
        identP = sb("identP", [64, 64], BF16)
        rst = sb("rst_s", [128, 512])
        stg = [sb(f"stg{i}", [128, 8, 128]) for i in range(2)]
        Wb = [[sb(f"Wb{u}_{i}", [128, 8, 128], BF16) for i in range(4)] for u in range(2)]
        Wl = sb("Wl", [128, 8, 128], BF16)
        upb = sb("upb", [128, 1024], BF16)
        lora_bf = sb("lora_bf", [128, SEQ], BF16)
        WO = sb("WO", [128, 16, D], BF16)

        ident = cst[:, 0, :]
        blockones = cst[:, 1, :]
        ones = cst[:, 2, :]

        def dma(out, in_, reads, writes):
            return S.op("sp", lambda e: e.dma_start(out=out, in_=in_), reads, writes, dma=True)

        def mm(out, lhsT, rhs, reads, writes, start=True, stop=True, inc=True):
            return S.op("pe", lambda e: e.matmul(out, lhsT=lhsT, rhs=rhs, start=start, stop=stop),
                        reads, writes, inc=inc)

        def tr(out, in_, idn, reads, writes, inc=True):
            return S.op("pe", lambda e: e.transpose(out, in_, idn), reads, writes, inc=inc)

        def act(out, in_, func, reads, writes, scale=1.0, bias=None):
            if bias is None:
                return S.op("act", lambda e: e.activation(out=out, in_=in_, func=func, scale=scale), reads, writes)
            return S.op("act", lambda e: e.activation(out=out, in_=in_, func=func, scale=scale, bias=bias),
                        reads, writes)

        def tt(out, in0, in1, op, reads, writes, eng="dve"):
            return S.op(eng, lambda e: e.tensor_tensor(out=out, in0=in0, in1=in1, op=op), reads, writes)

        def ts(out, in0, s1, op0, reads, writes, s2=None, op1=None, eng="dve"):
            if s2 is None:
                return S.op(eng, lambda e: e.tensor_scalar(out=out, in0=in0, scalar1=s1, scalar2=None, op0=op0),
                            reads, writes)
            return S.op(eng, lambda e: e.tensor_scalar(out=out, in0=in0, scalar1=s1, scalar2=s2, op0=op0, op1=op1),
                        reads, writes)

        def stt(out, in0, scalar, in1, op0, op1, reads, writes):
            return S.op("dve", lambda e: e.scalar_tensor_tensor(out=out, in0=in0, scalar=scalar, in1=in1,
                                                                 op0=op0, op1=op1), reads, writes)

        def cp(out, in_, reads, writes, eng="dve"):
            if eng == "act":
                return act(out, in_, AF.Copy, reads, writes)
            return S.op(eng, lambda e: e.tensor_copy(out=out, in_=in_), reads, writes)

        def memset(ap, val, writes, eng="dve"):
            return S.op(eng, lambda e: e.memset(ap, val), (), writes)

        def c64(t):
            return t[:, :].rearrange("p (c t) -> p c t", t=64)

        def red(out, in_, reads, writes):
            return S.op("dve", lambda e: e.tensor_reduce(out=out, in_=in_, axis=AX.X, op=ALU.add), reads, writes)

        dma(chv[:, :, :], chv_d.rearrange("p (a b) -> p a b", b=8), (), ["chv"])
        dma(muT[:, :], mu_d[:, :], (), ["mu"])
        dma(normw[:, :], normw_d[:, :], (), ["normw"])
        dma(fnw[:, :], fnw_d[:, :], (), ["fnw"])
        dma(cstf[:, :], cst_d[:, :], (), ["cstf"])
        dma(mskf[:, :], msk_d[:, :], (), ["mskf"])
        dma(rst[:, :], rst_d[:, :], (), ["rst"])
        upf = stg[0][:, :, :].rearrange("p a b -> p (a b)")
        dma(upf, lora_up[:, :], (), ["stg0"])
        cp(cst[:, :, :], cstf[:, :].rearrange("p (a b) -> p a b", b=128), ["cstf"], ["cst"])
        cp(maskA[:, :], mskf[0:64, 0:128], ["mskf"], ["maskA"])
        cp(maskN[:, :], mskf[0:64, 128:192], ["mskf"], ["maskN"])
        for j in range(2):
            cp(maskT[:, j, :, :], mskf[:, 192:448].rearrange("p (a b) -> p a b", b=128), ["mskf"], ["maskT"])
        cp(identP[:, :], cstf[0:64, 0:64], ["cstf"], ["identP"])
        cp(upb[:, :], upf, ["stg0"], ["upb"])
        ts(omu[:, :], muT[:, :], -1.0, ALU.mult, ["mu"], ["omu"], s2=1.0, op1=ALU.add)
        ts(omk[:, :], chv[:, 3, :], -1.0, ALU.mult, ["chv"], ["omk"], s2=1.0, op1=ALU.add)
        cp(normw_bc[:, :, :], bcast(normw[:, :], [[1, 8], [0, 128]]), ["normw"], ["normw_bc"])

        W0, A0, KKS, KA, RK, LNW, LNB = range(7)

        stg_i = [0]

        def load_wtile(col0, dst, key):
            i = stg_i[0] % 2
            stg_i[0] += 1
            src = w_in[:, col0:col0 + 128].rearrange("(kc p) c -> p kc c", p=128)
            for h in range(2):
                dma(stg[i][:, h * 4:h * 4 + 4, :], src[:, h * 4:h * 4 + 4, :], (), [f"stg{i}/{h}"])
            tt(dst[:, :, :], stg[i][:, :, :], normw_bc[:, :, :], ALU.mult, [f"stg{i}", "normw_bc"], [key], eng="pool")

        for mt in range(16):
            i = stg_i[0] % 2
            stg_i[0] += 1
            dma(stg[i][:, :, :].rearrange("p a b -> p (a b)"), w_out[mt * 128:(mt + 1) * 128, :], (), [f"stg{i}"])
            cp(WO[:, mt, :], stg[i][:, :, :].rearrange("p a b -> p (a b)"), [f"stg{i}"], ["WO"], eng="pool")

        def proj_block(W, wkey, psi, t0):
            for kc in range(8):
                mm(ps[psi][:, :], W[:, kc, :], xnT[:, kc, t0:t0 + 512], [wkey, "xnT"], [f"ps{psi}"],
                   start=(kc == 0), stop=(kc == 7), inc=(kc == 7))

        def shift_evac(psi, mcol, prevlast, pkey, out, okey, tmp, tkey):
            p = ps[psi]
            act(tmp[:, :], p[:, :], AF.Copy, [f"ps{psi}", "omu"], [tkey], scale=omu[:, mcol:mcol + 1])
            stt(out[:, 1:512], p[:, 0:511], muT[:, mcol:mcol + 1], tmp[:, 1:512], ALU.mult, ALU.add,
                [f"ps{psi}", "mu", tkey], [okey])
            stt(out[:, 0:1], prevlast, muT[:, mcol:mcol + 1], tmp[:, 0:1], ALU.mult, ALU.add,
                [pkey, "mu", tkey], [okey])
            cp(prevlast, p[:, 511:512], [f"ps{psi}"], [pkey], eng="act")

        def checkpoint(name, src_ap, key, n):
            if stop != name:
                return
            dtile = stg[1][:, :, :].rearrange("p a b -> p (a b)")
            memset(dtile[:, :], 0.0, ["stg1"])
            n = min(n, 1024)
            cp(dtile[0:src_ap.shape[0], 0:n], src_ap[:, 0:n], [key], ["stg1"])
            dma(dbg[:, :], dtile[:, :], ["stg1"], ["dbg"])
            S.barrier()
            S.stopped = True

        for b in range(NB):
          try:
              with ExitStack() as s1:
                  xt = [sb(f"xt{i}", [128, D], F32, s1) for i in range(2)]
                  sqt = sb("sqt", [128, D], F32, s1)
                  xnb = sb("xnb", [128, D], BF16, s1)
                  ssA = sb("ssA", [128, 2], F32, s1)
                  for tti in range(16):
                      xi = tti % 2
                      dma(xt[xi][:, :], x[b, tti * 128:(tti + 1) * 128, :], (), [f"xt{xi}"])
                      act(sqt[:, :], xt[xi][:, :], AF.Square, [f"xt{xi}"], ["sqt"])
                      red(ssA[:, 0:1], sqt[:, :], ["sqt"], ["ssA"])
                      act(ssA[:, 1:2], ssA[:, 0:1], AF.Ln, ["ssA"], ["ssA1"], scale=1.0 / D, bias=RMS_EPS)
                      act(ssA[:, 1:2], ssA[:, 1:2], AF.Exp, ["ssA1"], ["ssA1"], scale=-0.5)
                      ts(xnb[:, :], xt[xi][:, :], ssA[:, 1:2], ALU.mult, [f"xt{xi}", "ssA1"], ["xnb"])
                      for kc in range(8):
                          tr(psb[0][:, kc * 128:(kc + 1) * 128], xnb[:, kc * 128:(kc + 1) * 128], ident,
                             ["xnb", "cst"], ["ps0"], inc=(kc == 7))
                      cp(xnT[:, :, tti * 128:(tti + 1) * 128], psb[0][:, :].rearrange("p (a b) -> p a b", b=128),
                         ["ps0"], ["xnT"], eng="act")
                  S.barrier()
              checkpoint("A", xnT[:, 3, :], "xnT", SEQ)

              with ExitStack() as s1:
                  lo = sb("lo", [128, 512], F32, s1)
                  lotmp = sb("lotmp", [128, 512], F32, s1)
                  plast = sb("plastl", [128, 1], F32, s1)
                  load_wtile(3072, Wl, "Wl")
                  memset(plast[:, :], 0.0, ["plastl"])
                  for tb in range(4):
                      t0 = tb * 512
                      proj_block(Wl, "Wl", 0, t0)
                      shift_evac(0, 24, plast[:, 0:1], "plastl", lo, "lo", lotmp, "lotmp")
                      act(lora_bf[0:64, t0:t0 + 512], lo[0:64, :], AF.Tanh, ["lo"], ["lora_bf"])
                      act(lora_bf[64:128, t0:t0 + 512], lo[64:128, :], AF.Copy, ["lo"], ["lora_bf"])
                  S.barrier()
              checkpoint("L", lora_bf[:, :], "lora_bf", SEQ)

              units = [("r", hp) for hp in range(8)] + [("a", hp) for hp in range(8)]

              def unit_cols(u):
                  kind, hp = u
                  if kind == "r":
                      return [hp * 128, 1024 + hp * 128, 2048 + hp * 128, GATE0 + hp * 128]
                  return [ATT0 + hp * 128, ATT0 + 1024 + hp * 128, ATT0 + 2048 + hp * 128, ATTG0 + hp * 128]

              def prefetch(ui):
                  if ui >= len(units):
                      return
                  for i, c in enumerate(unit_cols(units[ui])):
                      load_wtile(c, Wb[ui % 2][i], f"Wb{ui % 2}_{i}")

              prefetch(0)

              with ExitStack() as s1:
                  f = lambda n: sb(n, [128, 512], F32, s1)
                  R, K, TV, SIGW, AA, CS, EW, EWI, KK, RN, Bt, FIN = [f(n) for n in
                      ("R", "K", "TV", "SIGW", "AA", "CS", "EW", "EWI", "KK", "RN", "Bt", "FIN")]
                  BONp = [f(f"BON{i}") for i in range(2)]
                  vTp = [sb(f"vT{i}", [128, 512], BF16, s1) for i in range(2)]
                  sgp = [sb(f"sg{i}", [128, 512], BF16, s1) for i in range(2)]
                  SQ = sb("SQ", [128, 512], BF16, s1)
                  T1 = sb("T1", [128, 512], BF16, s1)
                  mixo = sb("mixo", [128, 512], BF16, s1)
                  WCtp = [sb(f"WCt{i}", [128, 8], F32, s1) for i in range(2)]
                  ARp = [sb(f"AR{i}", [128, 8, 2, 64], BF16, s1) for i in range(2)]
                  BKp = [sb(f"BK{i}", [128, 2, 512], BF16, s1) for i in range(2)]
                  BKhp = [sb(f"BKh{i}", [128, 2, 512], BF16, s1) for i in range(2)]
                  AbT = [sb(f"AbT{j}", [64, 8, 128], BF16, s1) for j in range(2)]
                  AkT = [sb(f"AkT{j}", [64, 8, 128], BF16, s1) for j in range(2)]
                  PM = [sb(f"PM{j}", [64, 8, 128], BF16, s1) for j in range(2)]
                  NN = [sb(f"NN{j}", [64, 8, 64], BF16, s1) for j in range(2)]
                  tokB = sb("tokB", [64, 8, 128], BF16, s1)
                  tokK = sb("tokK", [64, 8, 128], BF16, s1)
                  tokV = sb("tokV", [64, 8, 128], BF16, s1)
                  X2all = sb("X2all", [64, 8, 128], F32, s1)
                  Y2all = sb("Y2all", [64, 8, 128], F32, s1)
                  KV = sb("KV", [128, 8, 64], F32, s1)
                  Ysb = sb("Ysb", [64, 8, 128], F32, s1)
                  ARsp = [sb(f"ARs{i}", [128, 8, 2, 64], BF16, s1) for i in range(2)]
                  HT = sb("HT", [128, 64], F32, s1)
                  Xsb = sb("Xsb", [64, 128], BF16, s1)
                  Usb = sb("Usb", [64, 128], BF16, s1)
                  H32 = sb("H32", [128, 64], F32, s1)
                  Hbf2 = sb("Hbf2", [128, 2, 64], BF16, s1)
                  plr = sb("plr", [128, 3], F32, s1)
                  YSQ = sb("YSQ", [64, 1024], F32, s1)
                  ynb = sb("ynb", [64, 8, 128], BF16, s1)
                  st = sb("st", [64, 6, 16], F32, s1)

                  seq = [(ui, tb) for ui in range(8) for tb in range(4)]

                  def ctx(n):
                      ui, tb = seq[n]
                      hp = units[ui][1]
                      return ui, tb, hp, n % 2, tb * 512

                  def stage1(n):
                      ui, tb, hp, pb, t0 = ctx(n)
                      Wr, Wk, Wv, Wg = Wb[ui % 2]
                      wk = [f"Wb{ui % 2}_{i}" for i in range(4)]
                      cv = lambda v: chv[:, v, hp:hp + 1]
                      AR, ARs, BK, BKh, vT, sg, BON, WCt = (ARp[pb], ARsp[pb], BKp[pb], BKhp[pb], vTp[pb], sgp[pb],
                                                            BONp[pb], WCtp[pb])
                      if tb == 0:
                          memset(plr[:, :], 0.0, ["plr0", "plr1", "plr2"])
                      proj_block(Wr, wk[0], 0, t0)
                      proj_block(Wk, wk[1], 1, t0)
                      proj_block(Wv, wk[2], 2, t0)
                      shift_evac(0, hp, plr[:, 0:1], "plr0", R, "R", R, "R")
                      shift_evac(1, 8 + hp, plr[:, 1:2], "plr1", K, "K", K, "K")
                      shift_evac(2, 16 + hp, plr[:, 2:3], "plr2", vT, f"vT{pb}", TV, "TV")
                      proj_block(Wg, wk[3], 0, t0)
                      mm(ps[1][:, :], upb[0:64, hp * 128:(hp + 1) * 128], lora_bf[0:64, t0:t0 + 512],
                         ["upb", "lora_bf"], ["ps1"])
                      mm(ps[2][:, :], upb[64:128, hp * 128:(hp + 1) * 128], lora_bf[64:128, t0:t0 + 512],
                         ["upb", "lora_bf"], ["ps2"])
                      act(sg[:, :], ps[0][:, :], AF.Silu, ["ps0"], [f"sg{pb}"])
                      act(SIGW[:, :], ps[1][:, :], AF.Sigmoid, ["ps1", "chv"], ["SIGW"], bias=cv(W0))
                      act(AA[:, :], ps[2][:, :], AF.Sigmoid, ["ps2", "chv"], ["AA"], bias=cv(A0))
                      S.op("dve", lambda e: e.tensor_tensor_scan(out=CS[:, :], data0=rst[:, :], data1=SIGW[:, :],
                                                                 initial=0.0, op0=ALU.mult, op1=ALU.add),
                           ["rst", "SIGW"], ["CS"])
                      tt(SIGW[:, :], CS[:, :], SIGW[:, :], ALU.subtract, ["CS", "SIGW"], ["SIGW"])
                      act(EW[:, :], CS[:, :], AF.Exp, ["CS"], ["EW"], scale=-C0)
                      act(EWI[:, :], CS[:, :], AF.Exp, ["CS"], ["EWI"], scale=C0)
                      act(SIGW[:, :], SIGW[:, :], AF.Exp, ["SIGW"], ["SIGW"], scale=-C0)
                      cp(WCt[:, :], EW[:, 63:512:64], ["EW"], [f"WCt{pb}"])
                      tt(CS[:, :].rearrange("p (c t) -> p c t", t=64), EWI[:, :].rearrange("p (c t) -> p c t", t=64),
                         bcast(WCt[:, :], [[1, 8], [0, 64]]), ALU.mult, ["EWI", f"WCt{pb}"], ["CS"])
                      ts(KK[:, :], K[:, :], cv(KKS), ALU.mult, ["K", "chv"], ["KK"])
                      act(SQ[:, :], KK[:, :], AF.Square, ["KK"], ["SQ"])
                      mm(ps[0][:, :], blockones, SQ[:, :], ["cst", "SQ"], ["ps0"])
                      act(RN[:, :], ps[0][:, :], AF.Ln, ["ps0"], ["RN"])
                      act(RN[:, :], RN[:, :], AF.Exp, ["RN"], ["RN"], scale=-0.5)
                      tt(KK[:, :], KK[:, :], RN[:, :], ALU.mult, ["KK", "RN"], ["KK"])
                      tt(Bt[:, :], KK[:, :], AA[:, :], ALU.mult, ["KK", "AA"], ["Bt"])
                      ts(AA[:, :], AA[:, :], cv(KA), ALU.mult, ["AA", "chv", "omk"], ["AA"],
                         s2=omk[:, hp:hp + 1], op1=ALU.add)
                      tt(K[:, :], K[:, :], AA[:, :], ALU.mult, ["K", "AA"], ["K"])
                      stt(T1[:, :], R[:, :], cv(RK), K[:, :], ALU.mult, ALU.mult, ["R", "K", "chv"], ["T1"])
                      mm(ps[1][:, :], blockones, T1[:, :], ["cst", "T1"], ["ps1"])
                      tt(BON[:, :], ps[1][:, :], vT[:, :], ALU.mult, ["ps1", f"vT{pb}"], [f"BON{pb}"])
                      tt(AR[:, :, 1, :], c64(R), c64(EW), ALU.mult, ["R", "EW"], [f"AR{pb}"])
                      stt(AR[:, :, 0, :], c64(KK), -1.0, c64(SIGW), ALU.mult, ALU.mult, ["KK", "SIGW"], [f"AR{pb}"])
                      cp(ARs[:, :, 0, :], AR[:, :, 1, :], [f"AR{pb}"], [f"ARs{pb}"], eng="pool")
                      cp(ARs[:, :, 1, :], AR[:, :, 0, :], [f"AR{pb}"], [f"ARs{pb}"], eng="pool")
                      tt(BK[:, 0, :], Bt[:, :], EWI[:, :], ALU.mult, ["Bt", "EWI"], [f"BK{pb}"], eng="pool")
                      tt(BK[:, 1, :], K[:, :], EWI[:, :], ALU.mult, ["K", "EWI"], [f"BK{pb}"], eng="pool")
                      tt(BKh[:, 0, :], Bt[:, :], CS[:, :], ALU.mult, ["Bt", "CS"], [f"BKh{pb}"], eng="pool")
                      tt(BKh[:, 1, :], K[:, :], CS[:, :], ALU.mult, ["K", "CS"], [f"BKh{pb}"], eng="pool")


                  def stage2(n):
                      ui, tb, hp, pb, t0 = ctx(n)
                      AR, ARs, BK, BKh, vT, sg, BON, WCt = (ARp[pb], ARsp[pb], BKp[pb], BKhp[pb], vTp[pb], sgp[pb],
                                                            BONp[pb], WCtp[pb])
                      if tb == 0:
                          prefetch(ui + 1)
                          memset(H32[:, :], 0.0, ["H32"])
                          memset(Hbf2[:, :, :], 0.0, ["Hbf2"], eng="pool")
                      for j in range(2):
                          kp = slice(j * 64, j * 64 + 64)
                          for c in range(8):
                              cs_ = slice(c * 64, c * 64 + 64)
                              bk = c // 4
                              col = (c % 4) * 128
                              mm(ps[0 + bk][0:64, col:col + 128], BK[kp, 0, cs_], AR[kp, c, :, :].rearrange("p a b -> p (a b)"),
                                 [f"BK{pb}", f"AR{pb}"], [f"ps{bk}"], inc=(c % 4 == 3))
                          for c in range(8):
                              cs_ = slice(c * 64, c * 64 + 64)
                              bk = c // 4
                              col = (c % 4) * 128
                              mm(ps[2 + bk][0:64, col:col + 128], BK[kp, 1, cs_], AR[kp, c, :, :].rearrange("p a b -> p (a b)"),
                                 [f"BK{pb}", f"AR{pb}"], [f"ps{2 + bk}"], inc=(c % 4 == 3))
                          for c in range(8):
                              cs_ = slice(c * 64, c * 64 + 64)
                              mm(ps[4][0:64, cs_], AR[kp, c, 0, :], BK[kp, 0, cs_], [f"BK{pb}", f"AR{pb}"], ["ps4"],
                                 inc=(c == 7))
                          mA = bcast(maskA[:, :], [[0, 4], [1, 128]])
                          for bk in range(2):
                              tt(AbT[j][:, bk * 4:bk * 4 + 4, :], ps[bk][0:64, :].rearrange("p (c t) -> p c t", t=128),
                                 mA, ALU.mult, [f"ps{bk}", "maskA"], [f"AbT{j}"])
                              tt(AkT[j][:, bk * 4:bk * 4 + 4, :],
                                 ps[2 + bk][0:64, :].rearrange("p (c t) -> p c t", t=128),
                                 mA, ALU.mult, [f"ps{2 + bk}", "maskA"], [f"AkT{j}"])
                          tt(NN[j][:, :, :], ps[4][0:64, :].rearrange("p (c t) -> p c t", t=64),
                             bcast(maskN[:, :], [[0, 8], [1, 64]]), ALU.mult, ["ps4", "maskN"], [f"NN{j}"])
                          cp(PM[j][:, :, 0:64], bcast(identP[:, :], [[0, 8], [1, 64]]), ["identP"], [f"PM{j}"],
                             eng="pool")
                          cp(PM[j][:, :, 64:128], AbT[j][:, :, 0:64], [f"AbT{j}"], [f"PM{j}"], eng="pool")
                      for lvl in range(6):
                          last = lvl == 5
                          for j in range(2):
                              pq0, nnb = (5, 4) if j == 0 else (0, 2)
                              for c in range(8):
                                  bk = c // 4
                                  col = (c % 4) * 128
                                  if last:
                                      mm(ps[pq0 + bk][0:64, col:col + 64], NN[j][:, c, :], PM[j][:, c, 0:64],
                                         [f"NN{j}", f"PM{j}"], [f"ps{pq0 + bk}"], inc=(c % 4 == 3))
                                  else:
                                      mm(ps[pq0 + bk][0:64, col:col + 128], NN[j][:, c, :], PM[j][:, c, :],
                                         [f"NN{j}", f"PM{j}"], [f"ps{pq0 + bk}"], inc=(c % 4 == 3))
                              if not last:
                                  for c in range(8):
                                      mm(ps[nnb][0:64, c * 64:c * 64 + 64], PM[j][:, c, 64:128], NN[j][:, c, :],
                                         [f"NN{j}", f"PM{j}"], [f"ps{nnb}"], inc=(c == 7))
                              for bk in range(2):
                                  pv = ps[pq0 + bk][0:64, :].rearrange("p (c t) -> p c t", t=128)
                                  tt(PM[j][:, bk * 4:bk * 4 + 4, 0:64], pv[:, :, 0:64],
                                     PM[j][:, bk * 4:bk * 4 + 4, 0:64], ALU.add,
                                     [f"ps{pq0 + bk}", f"PM{j}"], [f"PM{j}"])
                                  if not last:
                                      cp(PM[j][:, bk * 4:bk * 4 + 4, 64:128], pv[:, :, 64:128],
                                         [f"ps{pq0 + bk}"], [f"PM{j}"], eng="act")
                              if not last:
                                  cp(NN[j][:, :, :], ps[nnb][0:64, :].rearrange("p (c t) -> p c t", t=64),
                                     [f"ps{nnb}"], [f"NN{j}"], eng="act")
                      for qi, (src, dst, dkey, skey) in enumerate(((BKh[:, 0, :], tokB, "tokB", f"BKh{pb}"),
                                                                    (BKh[:, 1, :], tokK, "tokK", f"BKh{pb}"),
                                                                    (vT[:, :], tokV, "tokV", f"vT{pb}"))):
                          for c in range(8):
                              tr(psb[qi][0:64, c * 128:(c + 1) * 128], src[:, c * 64:(c + 1) * 64], ident,
                                 [skey, "cst"], [f"ps{qi}"], inc=(c == 7))
                          cp(dst[:, :, :], psb[qi][0:64, :].rearrange("p (c t) -> p c t", t=128), [f"ps{qi}"], [dkey],
                             eng=("act" if qi != 1 else "dve"))


                  def chain(n, lst):
                      ui, tb, hp, pb, t0 = ctx(n)
                      AR, ARs, BK, BKh, vT, sg, BON, WCt = (ARp[pb], ARsp[pb], BKp[pb], BKhp[pb], vTp[pb], sgp[pb],
                                                            BONp[pb], WCtp[pb])
                      for off, dst, dkey, banks in ((0, X2all, "X2all", (0, 1)), (64, Y2all, "Y2all", (2, 3))):
                          for c in range(8):
                              bk = banks[c // 4]
                              col = (c % 4) * 128
                              for j in range(2):
                                  jc = slice(j * 64, j * 64 + 64)
                                  mm(ps[bk][0:64, col + j * 64:col + j * 64 + 64], AkT[j][:, c, off:off + 64],
                                     tokV[:, c, jc], [f"AkT{j}", "tokV"], [f"ps{bk}"], inc=(c % 4 == 3 and j == 1))
                          for h in range(2):
                              cp(dst[:, h * 4:h * 4 + 4, :], ps[banks[h]][0:64, :].rearrange("p (c t) -> p c t", t=128),
                                 [f"ps{banks[h]}"], [dkey], eng=("act" if h else "dve"))
                      for c in range(8):
                          cs_ = slice(c * 64, c * 64 + 64)
                          mm(ps[4][0:64, cs_], tokK[:, c, 0:64], tokV[:, c, 0:64], ["tokK", "tokV"], ["ps4"], inc=False)
                          mm(ps[4][64:128, cs_], tokK[:, c, 64:128], tokV[:, c, 64:128], ["tokK", "tokV"], ["ps4"],
                             inc=(c == 7))
                      cp(KV[:, :, :], ps[4][:, :].rearrange("p (c v) -> p c v", v=64), ["ps4"], ["KV"], eng="act")

                      H2 = Hbf2[:, :, :].rearrange("p a b -> p (a b)")
                      for c in range(8):
                          cs_ = slice(c * 64, c * 64 + 64)
                          mm(ps[3][:, 0:128], AR[:, c, :, :].rearrange("p a b -> p (a b)"), H2, [f"AR{pb}", "Hbf2"], ["ps3"])
                          mm(ps[5][:, 0:128], ARs[:, c, :, :].rearrange("p a b -> p (a b)"), H2, [f"ARs{pb}", "Hbf2"], ["ps5"])
                          tt(Xsb[:, :], ps[3][0:64, 0:128], X2all[:, c, :], ALU.add, ["ps3", "X2all"], ["Xsb"])
                          for j in range(2):
                              jc = slice(j * 64, j * 64 + 64)
                              mm(ps[7][0:64, jc], PM[j][:, c, 0:64], Xsb[:, jc], [f"PM{j}", "Xsb"], ["ps7"], inc=(j == 1))
                          cp(Usb[:, :], ps[7][0:64, 0:128], ["ps7"], ["Usb"], eng="act")
                          stt(HT[:, :], H32[:, :], WCt[:, c:c + 1], KV[:, c, :], ALU.mult, ALU.add,
                              ["H32", f"WCt{pb}", "KV"], ["HT"])
                          mm(ps[4][0:64, 0:64], tokB[:, c, 0:64], Usb[:, 0:64], ["tokB", "Usb"], ["ps4"], inc=False)
                          mm(ps[4][64:128, 0:64], tokB[:, c, 64:128], Usb[:, 64:128], ["tokB", "Usb"], ["ps4"])
                          tt(H32[:, :], ps[4][:, 0:64], HT[:, :], ALU.add, ["ps4", "HT"], ["H32"])
                          cp(Hbf2[0:64, 0, :], H32[0:64, :], ["H32"], ["Hbf2"], eng="act")
                          cp(Hbf2[64:128, 1, :], H32[64:128, :], ["H32"], ["Hbf2"], eng="pool")
                          tt(Ysb[:, c, :], ps[5][0:64, 0:128], Y2all[:, c, :], ALU.add, ["ps5", "Y2all"], ["Ysb"])
                          for j in range(2):
                              jc = slice(j * 64, j * 64 + 64)
                              mm(ps[6][0:64, jc], AbT[j][:, c, 64:128], Usb[:, jc], [f"AbT{j}", "Usb"], ["ps6"],
                                 inc=(j == 1))
                          tt(Ysb[:, c, :], ps[6][0:64, 0:128], Ysb[:, c, :], ALU.add, ["ps6", "Ysb"], ["Ysb"])
                          S.emit_deferred(lst, (len(lst) + (7 - c)) // (8 - c))


                  def stage4(n):
                      ui, tb, hp, pb, t0 = ctx(n)
                      cv = lambda v: chv[:, v, hp:hp + 1]
                      AR, ARs, BK, BKh, vT, sg, BON, WCt = (ARp[pb], ARsp[pb], BKp[pb], BKhp[pb], vTp[pb], sgp[pb],
                                                            BONp[pb], WCtp[pb])
                      Yv = Ysb[:, :, :].rearrange("p c (j v) -> p (c j) v", v=64)
                      red(st[:, 0, :], Yv, ["Ysb"], ["st0"])
                      act(YSQ[:, :], Ysb[:, :, :].rearrange("p c t -> p (c t)"), AF.Square, ["Ysb"], ["YSQ"])
                      red(st[:, 1, :], YSQ[:, :].rearrange("p (g v) -> p g v", v=64), ["YSQ"], ["st1"])
                      ts(st[:, 2, :], st[:, 0, :], 1.0 / 64, ALU.mult, ["st0"], ["st2"])
                      tt(st[:, 3, :], st[:, 2, :], st[:, 2, :], ALU.mult, ["st2"], ["st3"])
                      stt(st[:, 4, :], st[:, 1, :], 1.0 / 64, st[:, 3, :], ALU.mult, ALU.subtract, ["st1", "st3"], ["st4"])
                      act(st[:, 5, :], st[:, 4, :], AF.Ln, ["st4"], ["st5"], bias=GN_EPS)
                      act(st[:, 5, :], st[:, 5, :], AF.Exp, ["st5"], ["st5"], scale=-0.5)
                      tt(Yv, Yv, bcast(st[:, 2, :], [[1, 16], [0, 64]]), ALU.subtract, ["Ysb", "st2"], ["Ysb"])
                      tt(ynb[:, :, :].rearrange("p c (j v) -> p (c j) v", v=64), Yv,
                         bcast(st[:, 5, :], [[1, 16], [0, 64]]), ALU.mult, ["Ysb", "st5"], ["ynb"])
                      for c in range(8):
                          for j in range(2):
                              jc = slice(j * 64, j * 64 + 64)
                              mm(ps[7][jc, c * 64:c * 64 + 64], ynb[:, c, jc], identP[:, :], ["ynb", "identP"], ["ps7"],
                                 inc=(c == 7 and j == 1))
                      ts(FIN[:, :], ps[7][:, :], cv(LNW), ALU.mult, ["ps7", "chv"], ["FIN"], s2=cv(LNB), op1=ALU.add)
                      tt(FIN[:, :], FIN[:, :], BON[:, :], ALU.add, ["FIN", f"BON{pb}"], ["FIN"])
                      tt(mixo[:, :], FIN[:, :], sg[:, :], ALU.mult, ["FIN", f"sg{pb}"], ["mixo"])
                      dma(mixs[b, hp, :, t0:t0 + 512], mixo[:, :], ["mixo"], [f"mixs{hp}"])

                  def record(fn, n):
                      lst = []
                      S.deferred = lst
                      fn(n)
                      S.deferred = None
                      return lst

                  def emit_merged(la, lb):
                      na, nb_ = max(len(la), 1), max(len(lb), 1)
                      while la or lb:
                          if la and (not lb or len(la) * nb_ >= len(lb) * na):
                              S.emit_deferred(la, 1)
                          else:
                              S.emit_deferred(lb, 1)

                  stage1(0)
                  pend4 = []
                  for n in range(len(seq)):
                      emit_merged(record(stage2, n), pend4)
                      lst = record(stage1, n + 1) if n + 1 < len(seq) else []
                      chain(n, lst)
                      S.emit_deferred(lst, len(lst))
                      pend4 = record(stage4, n)
                  S.emit_deferred(pend4, len(pend4))
                  S.barrier()

              with ExitStack() as s1:
                  qz = sb("qz", [128, 2, SEQ], BF16, s1)
                  memset(qz[:, :, :], 0.0, ["qz"], eng="pool")
                  kT = sb("kT", [128, SEQ], BF16, s1)
                  vTa = sb("vTa", [128, SEQ], BF16, s1)
                  sgT = sb("sgT", [128, SEQ], BF16, s1)
                  Vp = [sb(f"Vp{p}", [128, 16, 128], BF16, s1) for p in range(3)]
                  acc = sb("acc", [128, 2, SEQ], F32, s1)
                  RL = sb("RL", [128, SEQ], F32, s1)
                  mixa = sb("mixa", [128, SEQ], BF16, s1)
                  PT = [sb(f"PT{i}", [128, 2, 2, 128], BF16, s1) for i in range(3)]

                  def toks(d, i):
                      L = SEQ // d
                      g = i * 128
                      r, l0 = g // L, g % L
                      s0 = r + d * l0
                      return slice(s0, s0 + d * 127 + 1, d)

                  for ui in range(8, 16):
                      hp = units[ui][1]
                      prefetch(ui + 1)
                      Wq, Wk_, Wv_, Wg_ = Wb[ui % 2]
                      wk = [f"Wb{ui % 2}_{i}" for i in range(4)]
                      for tb in range(4):
                          t0 = tb * 512
                          proj_block(Wq, wk[0], 0, t0)
                          act(qz[0:64, 0, t0:t0 + 512], ps[0][0:64, :], AF.Copy, ["ps0"], ["qz"], scale=0.125)
                          act(qz[64:128, 1, t0:t0 + 512], ps[0][64:128, :], AF.Copy, ["ps0"], ["qz"], scale=0.125)
                          proj_block(Wk_, wk[1], 1, t0)
                          cp(kT[:, t0:t0 + 512], ps[1][:, :], ["ps1"], ["kT"])
                          proj_block(Wv_, wk[2], 2, t0)
                          act(vTa[:, t0:t0 + 512], ps[2][:, :], AF.Copy, ["ps2"], ["vTa"])
                          proj_block(Wg_, wk[3], 3, t0)
                          act(sgT[:, t0:t0 + 512], ps[3][:, :], AF.Silu, ["ps3"], ["sgT"])
                      for p, d in enumerate(PATTERNS):
                          for half in range(2):
                              pi = 4 + (p * 2 + half) % 2
                              for ii in range(8):
                                  i = half * 8 + ii
                                  tr(psb[pi][:, ii * 128:(ii + 1) * 128], vTa[:, toks(d, i)], ident, ["vTa", "cst"],
                                     [f"ps{pi}"], inc=(ii == 7))
                              cp(Vp[p][:, half * 8:half * 8 + 8, :], psb[pi][:, :].rearrange("p (a b) -> p a b", b=128),
                                 [f"ps{pi}"], [f"Vp{p}"], eng=("act" if half else "dve"))
                      blocks = [(p, d, i) for p, d in enumerate(PATTERNS) for i in range(16)]
                      SB_, OB_ = (0, 1, 6), (2, 3, 7)
                      LOOK = 2

                      def kbs_of(d, i):
                          has_prev = ((i * 128) % (SEQ // d)) != 0
                          return ([(0, i - 1)] if has_prev else []) + [(1, i)]

                      def emit_scores(bi):
                          p, d, i = blocks[bi]
                          kbs = kbs_of(d, i)
                          sbk = SB_[bi % 3]
                          sv = ps[sbk][:, :].rearrange("p (j k q) -> p j k q", j=2, k=2)
                          pt = PT[bi % 3]
                          for j in range(2):
                              for (kbi, kt) in kbs:
                                  mm(sv[:, j, kbi, :], kT[:, toks(d, kt)], qz[:, j, toks(d, i)], ["kT", "qz"],
                                     [f"ps{sbk}"], inc=(j == 1 and kbi == 1))
                          k0 = kbs[0][0]
                          act(pt[:, :, k0:2, :], sv[:, :, k0:2, :], AF.Exp, [f"ps{sbk}"], [f"PT{bi % 3}"])
                          tt(pt[:, :, k0:2, :], pt[:, :, k0:2, :], maskT[:, :, k0:2, :], ALU.mult,
                             [f"PT{bi % 3}", "maskT"], [f"PT{bi % 3}"], eng="pool")

                      def emit_pv(bi):
                          p, d, i = blocks[bi]
                          kbs = kbs_of(d, i)
                          obk = OB_[bi % 3]
                          pt = PT[bi % 3]
                          ov = ps[obk][:, :].rearrange("p (r q) -> p r q", q=128)
                          for rgn in range(4):
                              j = rgn % 2
                              for n_, (kbi, kt) in enumerate(kbs):
                                  lhs = Vp[p][:, kt, :] if rgn < 2 else ones
                                  mm(ov[:, rgn, :], lhs, pt[:, j, kbi, :], [f"Vp{p}", "cst", f"PT{bi % 3}"], [f"ps{obk}"],
                                     start=(n_ == 0), stop=(n_ == len(kbs) - 1),
                                     inc=(rgn == 3 and n_ == len(kbs) - 1))
                          for j in range(2):
                              kp = slice(j * 64, j * 64 + 64)
                              src = ov[kp, j:4:2, :]
                              dst = acc[kp, :, toks(d, i)]
                              if p == 0:
                                  cp(dst, src, [f"ps{obk}"], [f"acc{j}"], eng=("act" if j else "dve"))
                              else:
                                  tt(dst, src, dst, ALU.add, [f"ps{obk}", f"acc{j}"], [f"acc{j}"])

                      for bi in range(min(LOOK, len(blocks))):
                          emit_scores(bi)
                      for bi in range(len(blocks)):
                          if bi + LOOK < len(blocks):
                              emit_scores(bi + LOOK)
                          emit_pv(bi)
                      act(RL[:, :], acc[:, 1, :], AF.Ln, ["acc0", "acc1"], ["RL"])
                      act(RL[:, :], RL[:, :], AF.Exp, ["RL"], ["RL"], scale=-1.0)
                      tt(acc[:, 0, :], acc[:, 0, :], RL[:, :], ALU.mult, ["acc0", "acc1", "RL"], ["acc0", "acc1"])
                      tt(mixa[:, :], acc[:, 0, :], sgT[:, :], ALU.mult, ["acc0", "acc1", "sgT"], ["mixa"])
                      dma(mixs[b, 8 + hp, :, :], mixa[:, :], ["mixa"], [f"mixs{8 + hp}"])
                      if ui == 8:
                          checkpoint("T1", mixa[:, :], "mixa", SEQ)
                  S.barrier()

              with ExitStack() as s1:
                  xt = [sb(f"xo{i}", [128, D], F32, s1) for i in range(2)]
                  MT = [sb(f"MT{i}", [128, 16, 128], BF16, s1) for i in range(2)]
                  hT = sb("hT", [128, D], F32, s1)
                  sqo = sb("sqo", [128, D], F32, s1)
                  oT = [sb(f"oT{i}", [128, D], F32, s1) for i in range(2)]
                  sso = sb("sso", [128, 2], F32, s1)
                  allmix = [f"mixs{m}" for m in range(16)]
                  for tti in range(16):
                      xi = tti % 2
                      tsl = slice(tti * 128, (tti + 1) * 128)
                      dma(xt[xi][:, :], x[b, tsl, :], (), [f"xo{xi}"])
                      for q4 in range(4):
                          dma(MT[xi][:, q4 * 4:q4 * 4 + 4, :], mixs[b, q4 * 4:q4 * 4 + 4, :, tsl].rearrange("m p t -> p m t"),
                              allmix[q4 * 4:q4 * 4 + 4], [f"MT{xi}/{q4}"])
                      for half in range(2):
                          for mt in range(16):
                              mm(ps[6 + half][:, :], MT[xi][:, mt, :], WO[:, mt, half * 512:(half + 1) * 512],
                                 [f"MT{xi}", "WO"], [f"ps{6 + half}"], start=(mt == 0), stop=(mt == 15), inc=(mt == 15))
                          tt(hT[:, half * 512:(half + 1) * 512], ps[6 + half][:, :], xt[xi][:, half * 512:(half + 1) * 512],
                             ALU.add, [f"ps{6 + half}", f"xo{xi}"], ["hT"])
                      act(sqo[:, :], hT[:, :], AF.Square, ["hT"], ["sqo"])
                      red(sso[:, 0:1], sqo[:, :], ["sqo"], ["sso"])
                      act(sso[:, 1:2], sso[:, 0:1], AF.Ln, ["sso"], ["sso1"], scale=1.0 / D, bias=RMS_EPS)
                      act(sso[:, 1:2], sso[:, 1:2], AF.Exp, ["sso1"], ["sso1"], scale=-0.5)
                      stt(oT[xi][:, :], hT[:, :], sso[:, 1:2], fnw[:, :], ALU.mult, ALU.mult, ["hT", "sso1", "fnw"],
                          [f"oT{xi}"])
                      dma(y[b, tsl, :], oT[xi][:, :], [f"oT{xi}"], [f"y{b}_{tti}"])
                  S.barrier()
          except _Stop:
            break
        S.barrier()
        print(f"[kernel] ops={S.nops} waits={S.nwaits} cnt={S.cnt}")
    return nc


def _consts():
    ident = np.eye(128, dtype=np.float32)
    blockones = np.zeros((128, 128), np.float32)
    blockones[0:64, 0:64] = 1.0
    blockones[64:128, 64:128] = 1.0
    ones = np.ones((128, 128), np.float32)
    cst = np.concatenate([ident, blockones, ones], axis=1)
    s = np.arange(64)[:, None]
    t = np.arange(64)[None, :]
    maskA = np.concatenate([(s < t), (s <= t)], axis=1).astype(np.float32)
    maskN = (np.arange(64)[:, None] > np.arange(64)[None, :]).astype(np.float32)
    k = np.arange(128)[:, None]
    q = np.arange(128)[None, :]
    mprev = (k >= q).astype(np.float32)
    mdiag = (k <= q).astype(np.float32)
    msk = np.zeros((128, 448), np.float32)
    msk[0:64, 0:128] = maskA
    msk[0:64, 128:192] = maskN
    msk[:, 192:320] = mprev
    msk[:, 320:448] = mdiag
    rst = np.ones((128, 512), np.float32)
    rst[:, 0::64] = 0.0
    return cst, msk, rst


_NC_CACHE = {}


def kernel(x, norm_w, w_in, mu_shift, w0, w_up, a0, a_up, k_k, k_a, r_k, ln_x_w, ln_x_b, w_out, final_norm_w):
    f = lambda a: np.ascontiguousarray(np.asarray(a, dtype=np.float32))
    x = f(x)
    cst, msk, rst = _consts()
    pc = lambda v: f(v).reshape(8, 128).T
    chv = np.ascontiguousarray(np.stack([pc(w0), pc(a0), pc(k_k), pc(k_a), pc(np.asarray(r_k).reshape(-1)),
                                         pc(ln_x_w), pc(ln_x_b)], axis=1).reshape(128, 56))
    mu = np.ascontiguousarray(f(mu_shift).reshape(25, 128).T)
    normw = np.ascontiguousarray(f(norm_w).reshape(8, 128).T)
    fnw = np.ascontiguousarray(np.broadcast_to(f(final_norm_w)[None, :], (128, D)))
    lora_up = np.ascontiguousarray(np.concatenate([f(w_up), f(a_up)], axis=0))
    shared = {"w_in": f(w_in), "w_out": f(w_out), "lora_up": lora_up, "chv": chv, "mu": mu, "normw": normw,
              "fnw": fnw, "cst": cst, "msk": msk, "rst": rst}
    if "nc" not in _NC_CACHE:
        _NC_CACHE["nc"] = build()
    nc = _NC_CACHE["nc"]
    in_maps = [dict(shared, x=np.ascontiguousarray(x[c * NB:(c + 1) * NB])) for c in range(NCORES)]
    res = run_bass_kernel_spmd(nc, in_maps, core_ids=list(range(NCORES)))
    return np.concatenate([r["y"] for r in res.results], axis=0).astype(np.float32)
```

```python
import math
from contextlib import ExitStack

import numpy as np
import concourse.bass as bass
import concourse.mybir as mybir
from concourse.bass_utils import run_bass_kernel_spmd

F32 = mybir.dt.float32
BF16 = mybir.dt.bfloat16
AF = mybir.ActivationFunctionType
ALU = mybir.AluOpType
AX = mybir.AxisListType

NCORES = 8
NB = 2
SEQ = 2048
D = 1024
NIN = 8320
SHIFT_COLS = 3200
GATE0 = 3200
ATT0 = 4224
ATTG0 = 7296
C0 = math.exp(-0.5)
RMS_EPS = 1e-5
GN_EPS = 64e-5
PATTERNS = (1, 4, 16)


class Sched:
    def __init__(self, nc, es, ndma=12):
        self.nc = nc
        self.eng = {"pe": nc.tensor, "dve": nc.vector, "act": nc.scalar, "pool": nc.gpsimd, "sp": nc.sync}
        self.sem = {e: es.enter_context(nc.semaphore("sem_" + e)) for e in ("pe", "dve", "act", "pool")}
        self.cnt = {e: 0 for e in self.sem}
        self.dsem = [es.enter_context(nc.semaphore(f"dsem{i}")) for i in range(ndma)]
        self.dcnt = [0] * ndma
        self.dnext = 0
        self.waited = {e: {} for e in self.eng}
        self.lastw = {}
        self.readers = {}
        self.children = {}
        self.nwaits = 0
        self.nops = 0

    def _semobj(self, sk):
        return self.sem[sk[1]] if sk[0] == "e" else self.dsem[sk[1]]

    def _related(self, k):
        out = [k]
        if "/" in k:
            out.append(k.split("/")[0])
        else:
            out.extend(self.children.get(k, ()))
        return out

    def _wait(self, e, sk, val):
        if self.waited[e].get(sk, 0) < val:
            self.eng[e].wait_ge(self._semobj(sk), val)
            self.waited[e][sk] = val
            self.nwaits += 1

    stopped = False
    deferred = None

    def emit_deferred(self, lst, k):
        for _ in range(min(k, len(lst))):
            e, fn, reads, writes, dma, inc = lst.pop(0)
            self.op(e, fn, reads, writes, dma=dma, inc=inc)

    def op(self, e, fn, reads=(), writes=(), dma=False, inc=True):
        if self.stopped:
            return None
        if self.deferred is not None:
            self.deferred.append((e, fn, tuple(reads), tuple(writes), dma, inc))
            return None
        deps = {}

        def add(ev):
            sk, val = ev
            if e == "pe" and sk == ("e", "pe"):
                return
            if deps.get(sk, 0) < val:
                deps[sk] = val

        for k0 in reads:
            for k in self._related(k0):
                if k in self.lastw:
                    add(self.lastw[k])
        for k0 in writes:
            for k in self._related(k0):
                if k in self.lastw:
                    add(self.lastw[k])
                for ev in self.readers.get(k, {}).items():
                    add(ev)
        for sk, val in deps.items():
            self._wait(e, sk, val)
        if dma:
            idx = self.dnext
            self.dnext = (self.dnext + 1) % len(self.dsem)
            if self.dcnt[idx] > 0:
                self._wait(e, ("d", idx), self.dcnt[idx])
            inst = fn(self.eng[e])
            inst.then_inc(self.dsem[idx], 16)
            self.dcnt[idx] += 16
            ev = (("d", idx), self.dcnt[idx])
        else:
            inst = fn(self.eng[e])
            if inc:
                inst.then_inc(self.sem[e], 1)
                self.cnt[e] += 1
                ev = (("e", e), self.cnt[e])
            else:
                ev = (("e", e), self.cnt[e] + 1)
        self.nops += 1
        for k in writes:
            if "/" in k:
                self.children.setdefault(k.split("/")[0], set()).add(k)
            self.lastw[k] = ev
            self.readers[k] = {}
        for k in reads:
            if "/" in k:
                self.children.setdefault(k.split("/")[0], set()).add(k)
            r = self.readers.setdefault(k, {})
            if r.get(ev[0], 0) < ev[1]:
                r[ev[0]] = ev[1]
        return inst

    def barrier(self):
        if self.stopped:
            return
        for e in self.eng:
            for x in self.sem:
                if self.cnt[x] > 0:
                    self._wait(e, ("e", x), self.cnt[x])
            for i in range(len(self.dsem)):
                if self.dcnt[i] > 0:
                    self._wait(e, ("d", i), self.dcnt[i])


def bcast(ap, dims):
    return bass.AP(ap.tensor, ap.offset, [list(ap.ap[0])] + [list(d) for d in dims])


class _Stop(Exception):
    pass


def build(stop=None):
    nc = bass.Bass("TRN2", target_bir_lowering=False)
    dt = lambda n, s, k="ExternalInput", d=F32: nc.dram_tensor(n, s, d, kind=k).ap()
    x = dt("x", [NB, SEQ, D])
    w_in = dt("w_in", [D, NIN])
    w_out = dt("w_out", [2 * D, D])
    lora_up = dt("lora_up", [128, 1024])
    chv_d = dt("chv", [128, 7 * 8])
    mu_d = dt("mu", [128, 25])
    normw_d = dt("normw", [128, 8])
    fnw_d = dt("fnw", [128, D])
    cst_d = dt("cst", [128, 3 * 128])
    msk_d = dt("msk", [128, 128 + 64 + 256])
    rst_d = dt("rst", [128, 512])
    y = dt("y", [NB, SEQ, D], "ExternalOutput")
    mixs = dt("mixs", [NB, 16, 128, SEQ], "Internal", BF16)
    dbg = dt("dbg", [128, 1024], "ExternalOutput") if stop else None

    es = ExitStack()
    with es:
        S = Sched(nc, es)
        _names = {}

        def sb(n, s, d=F32, st=es):
            k = _names.get(n, 0)
            _names[n] = k + 1
            return st.enter_context(nc.sbuf_tensor(n if k == 0 else f"{n}_{k}", s, d))
        ps = [es.enter_context(nc.psum_tensor(f"ps{i}", [128, 512], F32)) for i in range(8)]
        psb = [p[:, :].bitcast(BF16) for p in ps]

        xnT = sb("xnT", [128, 8, SEQ], BF16)
        chv = sb("chv_s", [128, 7, 8])
        muT = sb("mu_s", [128, 25])
        omu = sb("omu_s", [128, 25])
        omk = sb("omk_s", [128, 8])
        normw = sb("normw_s", [128, 8])
        normw_bc = sb("normw_bc", [128, 8, 128])
        fnw = sb("fnw_s", [128, D])
        cstf = sb("cstf", [128, 3 * 128])
        cst = sb("cst_s", [128, 3, 128], BF16)
        mskf = sb("mskf", [128, 448])
        maskA = sb("maskA", [64, 128], BF16)
        maskN = sb("maskN", [64, 64], BF16)
        maskT = sb("maskT", [128, 2, 2, 128], BF16)
        identP = sb("identP", [64, 64], BF16)
        rst = sb("rst_s", [128, 512])
        stg = [sb(f"stg{i}", [128, 8, 128]) for i in range(2)]
        Wb = [[sb(f"Wb{u}_{i}", [128, 8, 128], BF16) for i in range(4)] for u in range(2)]
        Wl = sb("Wl", [128, 8, 128], BF16)
        upb = sb("upb", [128, 1024], BF16)
        lora_bf = sb("lora_bf", [128, SEQ], BF16)
        WO = sb("WO", [128, 16, D], BF16)

        ident = cst[:, 0, :]
        blockones = cst[:, 1, :]
        ones = cst[:, 2, :]

        def dma(out, in_, reads, writes):
            return S.op("sp", lambda e: e.dma_start(out=out, in_=in_), reads, writes, dma=True)

        def mm(out, lhsT, rhs, reads, writes, start=True, stop=True, inc=True):
            return S.op("pe", lambda e: e.matmul(out, lhsT=lhsT, rhs=rhs, start=start, stop=stop),
                        reads, writes, inc=inc)

        def tr(out, in_, idn, reads, writes, inc=True):
            return S.op("pe", lambda e: e.transpose(out, in_, idn), reads, writes, inc=inc)

        def act(out, in_, func, reads, writes, scale=1.0, bias=None):
            if bias is None:
                return S.op("act", lambda e: e.activation(out=out, in_=in_, func=func, scale=scale), reads, writes)
            return S.op("act", lambda e: e.activation(out=out, in_=in_, func=func, scale=scale, bias=bias),
                        reads, writes)

        def tt(out, in0, in1, op, reads, writes, eng="dve"):
            return S.op(eng, lambda e: e.tensor_tensor(out=out, in0=in0, in1=in1, op=op), reads, writes)

        def ts(out, in0, s1, op0, reads, writes, s2=None, op1=None, eng="dve"):
            if s2 is None:
                return S.op(eng, lambda e: e.tensor_scalar(out=out, in0=in0, scalar1=s1, scalar2=None, op0=op0),
                            reads, writes)
            return S.op(eng, lambda e: e.tensor_scalar(out=out, in0=in0, scalar1=s1, scalar2=s2, op0=op0, op1=op1),
                        reads, writes)

        def stt(out, in0, scalar, in1, op0, op1, reads, writes):
            return S.op("dve", lambda e: e.scalar_tensor_tensor(out=out, in0=in0, scalar=scalar, in1=in1,
                                                                 op0=op0, op1=op1), reads, writes)

        def cp(out, in_, reads, writes, eng="dve"):
            if eng == "act":
                return act(out, in_, AF.Copy, reads, writes)
            return S.op(eng, lambda e: e.tensor_copy(out=out, in_=in_), reads, writes)

        def memset(ap, val, writes, eng="dve"):
            return S.op(eng, lambda e: e.memset(ap, val), (), writes)

        def c64(t):
            return t[:, :].rearrange("p (c t) -> p c t", t=64)

        def rec(fn, *args):
            lst = []
            S.deferred = lst
            fn(*args)
            S.deferred = None
            return lst

        def merge2(la, lb):
            na, nb_ = max(len(la), 1), max(len(lb), 1)
            while la or lb:
                if la and (not lb or len(la) * nb_ >= len(lb) * na):
                    S.emit_deferred(la, 1)
                else:
                    S.emit_deferred(lb, 1)

        def red(out, in_, reads, writes):
            return S.op("dve", lambda e: e.tensor_reduce(out=out, in_=in_, axis=AX.X, op=ALU.add), reads, writes)

        dma(chv[:, :, :], chv_d.rearrange("p (a b) -> p a b", b=8), (), ["chv"])
        dma(muT[:, :], mu_d[:, :], (), ["mu"])
        dma(normw[:, :], normw_d[:, :], (), ["normw"])
        dma(fnw[:, :], fnw_d[:, :], (), ["fnw"])
        dma(cstf[:, :], cst_d[:, :], (), ["cstf"])
        dma(mskf[:, :], msk_d[:, :], (), ["mskf"])
        dma(rst[:, :], rst_d[:, :], (), ["rst"])
        upf = stg[0][:, :, :].rearrange("p a b -> p (a b)")
        dma(upf, lora_up[:, :], (), ["stg0"])
        cp(cst[:, :, :], cstf[:, :].rearrange("p (a b) -> p a b", b=128), ["cstf"], ["cst"])
        cp(maskA[:, :], mskf[0:64, 0:128], ["mskf"], ["maskA"])
        cp(maskN[:, :], mskf[0:64, 128:192], ["mskf"], ["maskN"])
        for j in range(2):
            cp(maskT[:, j, :, :], mskf[:, 192:448].rearrange("p (a b) -> p a b", b=128), ["mskf"], ["maskT"])
        cp(identP[:, :], cstf[0:64, 0:64], ["cstf"], ["identP"])
        cp(upb[:, :], upf, ["stg0"], ["upb"])
        ts(omu[:, :], muT[:, :], -1.0, ALU.mult, ["mu"], ["omu"], s2=1.0, op1=ALU.add)
        ts(omk[:, :], chv[:, 3, :], -1.0, ALU.mult, ["chv"], ["omk"], s2=1.0, op1=ALU.add)
        cp(normw_bc[:, :, :], bcast(normw[:, :], [[1, 8], [0, 128]]), ["normw"], ["normw_bc"])

        W0, A0, KKS, KA, RK, LNW, LNB = range(7)

        stg_i = [0]

        def load_wtile(col0, dst, key):
            i = stg_i[0] % 2
            stg_i[0] += 1
            src = w_in[:, col0:col0 + 128].rearrange("(kc p) c -> p kc c", p=128)
            for h in range(2):
                dma(stg[i][:, h * 4:h * 4 + 4, :], src[:, h * 4:h * 4 + 4, :], (), [f"stg{i}/{h}"])
            tt(dst[:, :, :], stg[i][:, :, :], normw_bc[:, :, :], ALU.mult, [f"stg{i}", "normw_bc"], [key], eng="pool")

        for mt in range(16):
            i = stg_i[0] % 2
            stg_i[0] += 1
            dma(stg[i][:, :, :].rearrange("p a b -> p (a b)"), w_out[mt * 128:(mt + 1) * 128, :], (), [f"stg{i}"])
            cp(WO[:, mt, :], stg[i][:, :, :].rearrange("p a b -> p (a b)"), [f"stg{i}"], ["WO"], eng="pool")

        def proj_block(W, wkey, psi, t0):
            for kc in range(8):
                mm(ps[psi][:, :], W[:, kc, :], xnT[:, kc, t0:t0 + 512], [wkey, "xnT"], [f"ps{psi}"],
                   start=(kc == 0), stop=(kc == 7), inc=(kc == 7))

        def shift_evac(psi, mcol, prevlast, pkey, out, okey, tmp, tkey):
            p = ps[psi]
            act(tmp[:, :], p[:, :], AF.Copy, [f"ps{psi}", "omu"], [tkey], scale=omu[:, mcol:mcol + 1])
            stt(out[:, 1:512], p[:, 0:511], muT[:, mcol:mcol + 1], tmp[:, 1:512], ALU.mult, ALU.add,
                [f"ps{psi}", "mu", tkey], [okey])
            stt(out[:, 0:1], prevlast, muT[:, mcol:mcol + 1], tmp[:, 0:1], ALU.mult, ALU.add,
                [pkey, "mu", tkey], [okey])
            cp(prevlast, p[:, 511:512], [f"ps{psi}"], [pkey], eng="act")

        def checkpoint(name, src_ap, key, n):
            if stop != name:
                return
            dtile = stg[1][:, :, :].rearrange("p a b -> p (a b)")
            memset(dtile[:, :], 0.0, ["stg1"])
            n = min(n, 1024)
            cp(dtile[0:src_ap.shape[0], 0:n], src_ap[:, 0:n], [key], ["stg1"])
            dma(dbg[:, :], dtile[:, :], ["stg1"], ["dbg"])
            S.barrier()
            S.stopped = True

        for b in range(NB):
          try:
              with ExitStack() as s1:
                  xt = [sb(f"xt{i}", [128, D], F32, s1) for i in range(2)]
                  sqt = [sb(f"sqt{i}", [128, D], F32, s1) for i in range(2)]
                  xnb = [sb(f"xnb{i}", [128, D], BF16, s1) for i in range(2)]
                  ssA = [sb(f"ssA{i}", [128, 2], F32, s1) for i in range(2)]

                  def tileA(tti):
                      xi = tti % 2
                      dma(xt[xi][:, :], x[b, tti * 128:(tti + 1) * 128, :], (), [f"xt{xi}"])
                      act(sqt[xi][:, :], xt[xi][:, :], AF.Square, [f"xt{xi}"], [f"sqt{xi}"])
                      red(ssA[xi][:, 0:1], sqt[xi][:, :], [f"sqt{xi}"], [f"ssA{xi}"])
                      act(ssA[xi][:, 1:2], ssA[xi][:, 0:1], AF.Ln, [f"ssA{xi}"], [f"ssB{xi}"], scale=1.0 / D, bias=RMS_EPS)
                      act(ssA[xi][:, 1:2], ssA[xi][:, 1:2], AF.Exp, [f"ssB{xi}"], [f"ssB{xi}"], scale=-0.5)
                      ts(xnb[xi][:, :], xt[xi][:, :], ssA[xi][:, 1:2], ALU.mult, [f"xt{xi}", f"ssB{xi}"], [f"xnb{xi}"])
                      for kc in range(8):
                          tr(psb[xi][:, kc * 128:(kc + 1) * 128], xnb[xi][:, kc * 128:(kc + 1) * 128], ident,
                             [f"xnb{xi}", "cst"], [f"ps{xi}"], inc=(kc == 7))
                      cp(xnT[:, :, tti * 128:(tti + 1) * 128], psb[xi][:, :].rearrange("p (a b) -> p a b", b=128),
                         [f"ps{xi}"], [f"xnT/{tti}"], eng=("act" if xi else "dve"))

                  for tti in range(0, 16, 2):
                      merge2(rec(tileA, tti), rec(tileA, tti + 1))
                  S.barrier()
              checkpoint("A", xnT[:, 3, :], "xnT", SEQ)

              with ExitStack() as s1:
                  lo = sb("lo", [128, 512], F32, s1)
                  lotmp = sb("lotmp", [128, 512], F32, s1)
                  plast = sb("plastl", [128, 1], F32, s1)
                  load_wtile(3072, Wl, "Wl")
                  memset(plast[:, :], 0.0, ["plastl"])
                  for tb in range(4):
                      t0 = tb * 512
                      proj_block(Wl, "Wl", 0, t0)
                      shift_evac(0, 24, plast[:, 0:1], "plastl", lo, "lo", lotmp, "lotmp")
                      act(lora_bf[0:64, t0:t0 + 512], lo[0:64, :], AF.Tanh, ["lo"], ["lora_bf"])
                      act(lora_bf[64:128, t0:t0 + 512], lo[64:128, :], AF.Copy, ["lo"], ["lora_bf"])
                  S.barrier()
              checkpoint("L", lora_bf[:, :], "lora_bf", SEQ)

              units = [("r", hp) for hp in range(8)] + [("a", hp) for hp in range(8)]

              def unit_cols(u):
                  kind, hp = u
                  if kind == "r":
                      return [hp * 128, 1024 + hp * 128, 2048 + hp * 128, GATE0 + hp * 128]
                  return [ATT0 + hp * 128, ATT0 + 1024 + hp * 128, ATT0 + 2048 + hp * 128, ATTG0 + hp * 128]

              def prefetch(ui):
                  if ui >= len(units):
                      return
                  for i, c in enumerate(unit_cols(units[ui])):
                      load_wtile(c, Wb[ui % 2][i], f"Wb{ui % 2}_{i}")

              prefetch(0)

              with ExitStack() as s1:
                  f = lambda n: sb(n, [128, 512], F32, s1)
                  R, K, TV, SIGW, AA, CS, EW, EWI, KK, RN, Bt, FIN = [f(n) for n in
                      ("R", "K", "TV", "SIGW", "AA", "CS", "EW", "EWI", "KK", "RN", "Bt", "FIN")]
                  BONp = [f(f"BON{i}") for i in range(2)]
                  vTp = [sb(f"vT{i}", [128, 512], BF16, s1) for i in range(2)]
                  sgp = [sb(f"sg{i}", [128, 512], BF16, s1) for i in range(2)]
                  SQ = sb("SQ", [128, 512], BF16, s1)
                  T1 = sb("T1", [128, 512], BF16, s1)
                  mixo = sb("mixo", [128, 512], BF16, s1)
                  WCtp = [sb(f"WCt{i}", [128, 8], F32, s1) for i in range(2)]
                  ARp = [sb(f"AR{i}", [128, 8, 2, 64], BF16, s1) for i in range(2)]
                  BKp = [sb(f"BK{i}", [128, 2, 512], BF16, s1) for i in range(2)]
                  BKhp = [sb(f"BKh{i}", [128, 2, 512], BF16, s1) for i in range(2)]
                  AbT = [sb(f"AbT{j}", [64, 8, 128], BF16, s1) for j in range(2)]
                  AkT = [sb(f"AkT{j}", [64, 8, 128], BF16, s1) for j in range(2)]
                  PM = [sb(f"PM{j}", [64, 8, 128], BF16, s1) for j in range(2)]
                  NN = [sb(f"NN{j}", [64, 8, 64], BF16, s1) for j in range(2)]
                  tokB = sb("tokB", [64, 8, 128], BF16, s1)
                  tokK = sb("tokK", [64, 8, 128], BF16, s1)
                  tokV = sb("tokV", [64, 8, 128], BF16, s1)
                  X2all = sb("X2all", [64, 8, 128], F32, s1)
                  Y2all = sb("Y2all", [64, 8, 128], F32, s1)
                  KV = sb("KV", [128, 8, 64], F32, s1)
                  Ysb = sb("Ysb", [64, 8, 128], F32, s1)
                  ARsp = [sb(f"ARs{i}", [128, 8, 2, 64], BF16, s1) for i in range(2)]
                  HT = sb("HT", [128, 64], F32, s1)
                  Xsb = sb("Xsb", [64, 128], BF16, s1)
                  Usb = sb("Usb", [64, 128], BF16, s1)
                  H32 = sb("H32", [128, 64], F32, s1)
                  Hbf2 = sb("Hbf2", [128, 2, 64], BF16, s1)
                  plr = sb("plr", [128, 3], F32, s1)
                  YSQ = sb("YSQ", [64, 1024], F32, s1)
                  ynb = sb("ynb", [64, 8, 128], BF16, s1)
                  st = sb("st", [64, 6, 16], F32, s1)

                  seq = [(ui, tb) for ui in range(8) for tb in range(4)]

                  def ctx(n):
                      ui, tb = seq[n]
                      hp = units[ui][1]
                      return ui, tb, hp, n % 2, tb * 512

                  def stage1(n):
                      ui, tb, hp, pb, t0 = ctx(n)
                      Wr, Wk, Wv, Wg = Wb[ui % 2]
                      wk = [f"Wb{ui % 2}_{i}" for i in range(4)]
                      cv = lambda v: chv[:, v, hp:hp + 1]
                      AR, ARs, BK, BKh, vT, sg, BON, WCt = (ARp[pb], ARsp[pb], BKp[pb], BKhp[pb], vTp[pb], sgp[pb],
                                                            BONp[pb], WCtp[pb])
                      if tb == 0:
                          memset(plr[:, :], 0.0, ["plr0", "plr1", "plr2"])
                      proj_block(Wr, wk[0], 0, t0)
                      proj_block(Wk, wk[1], 1, t0)
                      proj_block(Wv, wk[2], 2, t0)
                      shift_evac(0, hp, plr[:, 0:1], "plr0", R, "R", R, "R")
                      shift_evac(1, 8 + hp, plr[:, 1:2], "plr1", K, "K", K, "K")
                      shift_evac(2, 16 + hp, plr[:, 2:3], "plr2", vT, f"vT{pb}", TV, "TV")
                      proj_block(Wg, wk[3], 0, t0)
                      mm(ps[1][:, :], upb[0:64, hp * 128:(hp + 1) * 128], lora_bf[0:64, t0:t0 + 512],
                         ["upb", "lora_bf"], ["ps1"])
                      mm(ps[2][:, :], upb[64:128, hp * 128:(hp + 1) * 128], lora_bf[64:128, t0:t0 + 512],
                         ["upb", "lora_bf"], ["ps2"])
                      act(sg[:, :], ps[0][:, :], AF.Silu, ["ps0"], [f"sg{pb}"])
                      act(SIGW[:, :], ps[1][:, :], AF.Sigmoid, ["ps1", "chv"], ["SIGW"], bias=cv(W0))
                      act(AA[:, :], ps[2][:, :], AF.Sigmoid, ["ps2", "chv"], ["AA"], bias=cv(A0))
                      S.op("dve", lambda e: e.tensor_tensor_scan(out=CS[:, :], data0=rst[:, :], data1=SIGW[:, :],
                                                                 initial=0.0, op0=ALU.mult, op1=ALU.add),
                           ["rst", "SIGW"], ["CS"])
                      tt(SIGW[:, :], CS[:, :], SIGW[:, :], ALU.subtract, ["CS", "SIGW"], ["SIGW"])
                      act(EW[:, :], CS[:, :], AF.Exp, ["CS"], ["EW"], scale=-C0)
                      act(EWI[:, :], CS[:, :], AF.Exp, ["CS"], ["EWI"], scale=C0)
                      act(SIGW[:, :], SIGW[:, :], AF.Exp, ["SIGW"], ["SIGW"], scale=-C0)
                      cp(WCt[:, :], EW[:, 63:512:64], ["EW"], [f"WCt{pb}"])
                      tt(CS[:, :].rearrange("p (c t) -> p c t", t=64), EWI[:, :].rearrange("p (c t) -> p c t", t=64),
                         bcast(WCt[:, :], [[1, 8], [0, 64]]), ALU.mult, ["EWI", f"WCt{pb}"], ["CS"])
                      ts(KK[:, :], K[:, :], cv(KKS), ALU.mult, ["K", "chv"], ["KK"])
                      act(SQ[:, :], KK[:, :], AF.Square, ["KK"], ["SQ"])
                      mm(ps[0][:, :], blockones, SQ[:, :], ["cst", "SQ"], ["ps0"])
                      act(RN[:, :], ps[0][:, :], AF.Ln, ["ps0"], ["RN"])
                      act(RN[:, :], RN[:, :], AF.Exp, ["RN"], ["RN"], scale=-0.5)
                      tt(KK[:, :], KK[:, :], RN[:, :], ALU.mult, ["KK", "RN"], ["KK"])
                      tt(Bt[:, :], KK[:, :], AA[:, :], ALU.mult, ["KK", "AA"], ["Bt"])
                      ts(AA[:, :], AA[:, :], cv(KA), ALU.mult, ["AA", "chv", "omk"], ["AA"],
                         s2=omk[:, hp:hp + 1], op1=ALU.add)
                      tt(K[:, :], K[:, :], AA[:, :], ALU.mult, ["K", "AA"], ["K"])
                      stt(T1[:, :], R[:, :], cv(RK), K[:, :], ALU.mult, ALU.mult, ["R", "K", "chv"], ["T1"])
                      mm(ps[1][:, :], blockones, T1[:, :], ["cst", "T1"], ["ps1"])
                      tt(BON[:, :], ps[1][:, :], vT[:, :], ALU.mult, ["ps1", f"vT{pb}"], [f"BON{pb}"])
                      tt(AR[:, :, 1, :], c64(R), c64(EW), ALU.mult, ["R", "EW"], [f"AR{pb}"])
                      stt(AR[:, :, 0, :], c64(KK), -1.0, c64(SIGW), ALU.mult, ALU.mult, ["KK", "SIGW"], [f"AR{pb}"])
                      cp(ARs[:, :, 0, :], AR[:, :, 1, :], [f"AR{pb}"], [f"ARs{pb}"], eng="pool")
                      cp(ARs[:, :, 1, :], AR[:, :, 0, :], [f"AR{pb}"], [f"ARs{pb}"], eng="pool")
                      tt(BK[:, 0, :], Bt[:, :], EWI[:, :], ALU.mult, ["Bt", "EWI"], [f"BK{pb}"], eng="pool")
                      tt(BK[:, 1, :], K[:, :], EWI[:, :], ALU.mult, ["K", "EWI"], [f"BK{pb}"], eng="pool")
                      tt(BKh[:, 0, :], Bt[:, :], CS[:, :], ALU.mult, ["Bt", "CS"], [f"BKh{pb}"], eng="pool")
                      tt(BKh[:, 1, :], K[:, :], CS[:, :], ALU.mult, ["K", "CS"], [f"BKh{pb}"], eng="pool")


                  def stage2(n):
                      ui, tb, hp, pb, t0 = ctx(n)
                      AR, ARs, BK, BKh, vT, sg, BON, WCt = (ARp[pb], ARsp[pb], BKp[pb], BKhp[pb], vTp[pb], sgp[pb],
                                                            BONp[pb], WCtp[pb])
                      if tb == 0:
                          prefetch(ui + 1)
                          memset(H32[:, :], 0.0, ["H32"])
                          memset(Hbf2[:, :, :], 0.0, ["Hbf2"], eng="pool")
                      for j in range(2):
                          kp = slice(j * 64, j * 64 + 64)
                          for c in range(8):
                              cs_ = slice(c * 64, c * 64 + 64)
                              bk = c // 4
                              col = (c % 4) * 128
                              mm(ps[0 + bk][0:64, col:col + 128], BK[kp, 0, cs_], AR[kp, c, :, :].rearrange("p a b -> p (a b)"),
                                 [f"BK{pb}", f"AR{pb}"], [f"ps{bk}"], inc=(c % 4 == 3))
                          for c in range(8):
                              cs_ = slice(c * 64, c * 64 + 64)
                              bk = c // 4
                              col = (c % 4) * 128
                              mm(ps[2 + bk][0:64, col:col + 128], BK[kp, 1, cs_], AR[kp, c, :, :].rearrange("p a b -> p (a b)"),
                                 [f"BK{pb}", f"AR{pb}"], [f"ps{2 + bk}"], inc=(c % 4 == 3))
                          for c in range(8):
                              cs_ = slice(c * 64, c * 64 + 64)
                              mm(ps[4][0:64, cs_], AR[kp, c, 0, :], BK[kp, 0, cs_], [f"BK{pb}", f"AR{pb}"], ["ps4"],
                                 inc=(c == 7))
                          mA = bcast(maskA[:, :], [[0, 4], [1, 128]])
                          for bk in range(2):
                              tt(AbT[j][:, bk * 4:bk * 4 + 4, :], ps[bk][0:64, :].rearrange("p (c t) -> p c t", t=128),
                                 mA, ALU.mult, [f"ps{bk}", "maskA"], [f"AbT{j}"])
                              tt(AkT[j][:, bk * 4:bk * 4 + 4, :],
                                 ps[2 + bk][0:64, :].rearrange("p (c t) -> p c t", t=128),
                                 mA, ALU.mult, [f"ps{2 + bk}", "maskA"], [f"AkT{j}"])
                          tt(NN[j][:, :, :], ps[4][0:64, :].rearrange("p (c t) -> p c t", t=64),
                             bcast(maskN[:, :], [[0, 8], [1, 64]]), ALU.mult, ["ps4", "maskN"], [f"NN{j}"])
                          cp(PM[j][:, :, 0:64], bcast(identP[:, :], [[0, 8], [1, 64]]), ["identP"], [f"PM{j}"],
                             eng="pool")
                          cp(PM[j][:, :, 64:128], AbT[j][:, :, 0:64], [f"AbT{j}"], [f"PM{j}"], eng="pool")
                      for lvl in range(6):
                          last = lvl == 5
                          for j in range(2):
                              pq0, nnb = (5, 4) if j == 0 else (0, 2)
                              for c in range(8):
                                  bk = c // 4
                                  col = (c % 4) * 128
                                  if last:
                                      mm(ps[pq0 + bk][0:64, col:col + 64], NN[j][:, c, :], PM[j][:, c, 0:64],
                                         [f"NN{j}", f"PM{j}"], [f"ps{pq0 + bk}"], inc=(c % 4 == 3))
                                  else:
                                      mm(ps[pq0 + bk][0:64, col:col + 128], NN[j][:, c, :], PM[j][:, c, :],
                                         [f"NN{j}", f"PM{j}"], [f"ps{pq0 + bk}"], inc=(c % 4 == 3))
                              if not last:
                                  for c in range(8):
                                      mm(ps[nnb][0:64, c * 64:c * 64 + 64], PM[j][:, c, 64:128], NN[j][:, c, :],
                                         [f"NN{j}", f"PM{j}"], [f"ps{nnb}"], inc=(c == 7))
                              for bk in range(2):
                                  pv = ps[pq0 + bk][0:64, :].rearrange("p (c t) -> p c t", t=128)
                                  tt(PM[j][:, bk * 4:bk * 4 + 4, 0:64], pv[:, :, 0:64],
                                     PM[j][:, bk * 4:bk * 4 + 4, 0:64], ALU.add,
                                     [f"ps{pq0 + bk}", f"PM{j}"], [f"PM{j}"])
                                  if not last:
                                      cp(PM[j][:, bk * 4:bk * 4 + 4, 64:128], pv[:, :, 64:128],
                                         [f"ps{pq0 + bk}"], [f"PM{j}"], eng="act")
                              if not last:
                                  cp(NN[j][:, :, :], ps[nnb][0:64, :].rearrange("p (c t) -> p c t", t=64),
                                     [f"ps{nnb}"], [f"NN{j}"], eng="act")
                      for qi, (src, dst, dkey, skey) in enumerate(((BKh[:, 0, :], tokB, "tokB", f"BKh{pb}"),
                                                                    (BKh[:, 1, :], tokK, "tokK", f"BKh{pb}"),
                                                                    (vT[:, :], tokV, "tokV", f"vT{pb}"))):
                          for c in range(8):
                              tr(psb[qi][0:64, c * 128:(c + 1) * 128], src[:, c * 64:(c + 1) * 64], ident,
                                 [skey, "cst"], [f"ps{qi}"], inc=(c == 7))
                          cp(dst[:, :, :], psb[qi][0:64, :].rearrange("p (c t) -> p c t", t=128), [f"ps{qi}"], [dkey],
                             eng=("act" if qi != 1 else "dve"))


                  def chain(n, lst):
                      ui, tb, hp, pb, t0 = ctx(n)
                      AR, ARs, BK, BKh, vT, sg, BON, WCt = (ARp[pb], ARsp[pb], BKp[pb], BKhp[pb], vTp[pb], sgp[pb],
                                                            BONp[pb], WCtp[pb])
                      for off, dst, dkey, banks in ((0, X2all, "X2all", (0, 1)), (64, Y2all, "Y2all", (2, 3))):
                          for c in range(8):
                              bk = banks[c // 4]
                              col = (c % 4) * 128
                              for j in range(2):
                                  jc = slice(j * 64, j * 64 + 64)
                                  mm(ps[bk][0:64, col + j * 64:col + j * 64 + 64], AkT[j][:, c, off:off + 64],
                                     tokV[:, c, jc], [f"AkT{j}", "tokV"], [f"ps{bk}"], inc=(c % 4 == 3 and j == 1))
                          for h in range(2):
                              cp(dst[:, h * 4:h * 4 + 4, :], ps[banks[h]][0:64, :].rearrange("p (c t) -> p c t", t=128),
                                 [f"ps{banks[h]}"], [dkey], eng=("act" if h else "dve"))
                      for c in range(8):
                          cs_ = slice(c * 64, c * 64 + 64)
                          mm(ps[4][0:64, cs_], tokK[:, c, 0:64], tokV[:, c, 0:64], ["tokK", "tokV"], ["ps4"], inc=False)
                          mm(ps[4][64:128, cs_], tokK[:, c, 64:128], tokV[:, c, 64:128], ["tokK", "tokV"], ["ps4"],
                             inc=(c == 7))
                      cp(KV[:, :, :], ps[4][:, :].rearrange("p (c v) -> p c v", v=64), ["ps4"], ["KV"], eng="act")

                      H2 = Hbf2[:, :, :].rearrange("p a b -> p (a b)")
                      for c in range(8):
                          cs_ = slice(c * 64, c * 64 + 64)
                          mm(ps[3][:, 0:128], AR[:, c, :, :].rearrange("p a b -> p (a b)"), H2, [f"AR{pb}", "Hbf2"], ["ps3"])
                          mm(ps[5][:, 0:128], ARs[:, c, :, :].rearrange("p a b -> p (a b)"), H2, [f"ARs{pb}", "Hbf2"], ["ps5"])
                          tt(Xsb[:, :], ps[3][0:64, 0:128], X2all[:, c, :], ALU.add, ["ps3", "X2all"], ["Xsb"])
                          for j in range(2):
                              jc = slice(j * 64, j * 64 + 64)
                              mm(ps[7][0:64, jc], PM[j][:, c, 0:64], Xsb[:, jc], [f"PM{j}", "Xsb"], ["ps7"], inc=(j == 1))
                          cp(Usb[:, :], ps[7][0:64, 0:128], ["ps7"], ["Usb"], eng="act")
                          stt(HT[:, :], H32[:, :], WCt[:, c:c + 1], KV[:, c, :], ALU.mult, ALU.add,
                              ["H32", f"WCt{pb}", "KV"], ["HT"])
                          mm(ps[4][0:64, 0:64], tokB[:, c, 0:64], Usb[:, 0:64], ["tokB", "Usb"], ["ps4"], inc=False)
                          mm(ps[4][64:128, 0:64], tokB[:, c, 64:128], Usb[:, 64:128], ["tokB", "Usb"], ["ps4"])
                          tt(H32[:, :], ps[4][:, 0:64], HT[:, :], ALU.add, ["ps4", "HT"], ["H32"])
                          cp(Hbf2[0:64, 0, :], H32[0:64, :], ["H32"], ["Hbf2"], eng="act")
                          cp(Hbf2[64:128, 1, :], H32[64:128, :], ["H32"], ["Hbf2"], eng="pool")
                          tt(Ysb[:, c, :], ps[5][0:64, 0:128], Y2all[:, c, :], ALU.add, ["ps5", "Y2all"], ["Ysb"])
                          for j in range(2):
                              jc = slice(j * 64, j * 64 + 64)
                              mm(ps[6][0:64, jc], AbT[j][:, c, 64:128], Usb[:, jc], [f"AbT{j}", "Usb"], ["ps6"],
                                 inc=(j == 1))
                          tt(Ysb[:, c, :], ps[6][0:64, 0:128], Ysb[:, c, :], ALU.add, ["ps6", "Ysb"], ["Ysb"])
                          S.emit_deferred(lst, (len(lst) + (7 - c)) // (8 - c))


                  def stage4(n):
                      ui, tb, hp, pb, t0 = ctx(n)
                      cv = lambda v: chv[:, v, hp:hp + 1]
                      AR, ARs, BK, BKh, vT, sg, BON, WCt = (ARp[pb], ARsp[pb], BKp[pb], BKhp[pb], vTp[pb], sgp[pb],
                                                            BONp[pb], WCtp[pb])
                      Yv = Ysb[:, :, :].rearrange("p c (j v) -> p (c j) v", v=64)
                      red(st[:, 0, :], Yv, ["Ysb"], ["st0"])
                      act(YSQ[:, :], Ysb[:, :, :].rearrange("p c t -> p (c t)"), AF.Square, ["Ysb"], ["YSQ"])
                      red(st[:, 1, :], YSQ[:, :].rearrange("p (g v) -> p g v", v=64), ["YSQ"], ["st1"])
                      ts(st[:, 2, :], st[:, 0, :], 1.0 / 64, ALU.mult, ["st0"], ["st2"])
                      tt(st[:, 3, :], st[:, 2, :], st[:, 2, :], ALU.mult, ["st2"], ["st3"])
                      stt(st[:, 4, :], st[:, 1, :], 1.0 / 64, st[:, 3, :], ALU.mult, ALU.subtract, ["st1", "st3"], ["st4"])
                      act(st[:, 5, :], st[:, 4, :], AF.Ln, ["st4"], ["st5"], bias=GN_EPS)
                      act(st[:, 5, :], st[:, 5, :], AF.Exp, ["st5"], ["st5"], scale=-0.5)
                      tt(Yv, Yv, bcast(st[:, 2, :], [[1, 16], [0, 64]]), ALU.subtract, ["Ysb", "st2"], ["Ysb"])
                      tt(ynb[:, :, :].rearrange("p c (j v) -> p (c j) v", v=64), Yv,
                         bcast(st[:, 5, :], [[1, 16], [0, 64]]), ALU.mult, ["Ysb", "st5"], ["ynb"])
                      for c in range(8):
                          for j in range(2):
                              jc = slice(j * 64, j * 64 + 64)
                              mm(ps[7][jc, c * 64:c * 64 + 64], ynb[:, c, jc], identP[:, :], ["ynb", "identP"], ["ps7"],
                                 inc=(c == 7 and j == 1))
                      ts(FIN[:, :], ps[7][:, :], cv(LNW), ALU.mult, ["ps7", "chv"], ["FIN"], s2=cv(LNB), op1=ALU.add)
                      tt(FIN[:, :], FIN[:, :], BON[:, :], ALU.add, ["FIN", f"BON{pb}"], ["FIN"])
                      tt(mixo[:, :], FIN[:, :], sg[:, :], ALU.mult, ["FIN", f"sg{pb}"], ["mixo"])
                      dma(mixs[b, hp, :, t0:t0 + 512], mixo[:, :], ["mixo"], [f"mixs{hp}"])

                  def record(fn, n):
                      lst = []
                      S.deferred = lst
                      fn(n)
                      S.deferred = None
                      return lst

                  def emit_merged(la, lb):
                      na, nb_ = max(len(la), 1), max(len(lb), 1)
                      while la or lb:
                          if la and (not lb or len(la) * nb_ >= len(lb) * na):
                              S.emit_deferred(la, 1)
                          else:
                              S.emit_deferred(lb, 1)

                  stage1(0)
                  pend4 = []
                  for n in range(len(seq)):
                      emit_merged(record(stage2, n), pend4)
                      lst = record(stage1, n + 1) if n + 1 < len(seq) else []
                      chain(n, lst)
                      S.emit_deferred(lst, len(lst))
                      pend4 = record(stage4, n)
                  S.emit_deferred(pend4, len(pend4))
                  S.barrier()

              with ExitStack() as s1:
                  qz = sb("qz", [128, 2, SEQ], BF16, s1)
                  memset(qz[:, :, :], 0.0, ["qz"], eng="pool")
                  kT = sb("kT", [128, SEQ], BF16, s1)
                  vTa = sb("vTa", [128, SEQ], BF16, s1)
                  sgT = sb("sgT", [128, SEQ], BF16, s1)
                  Vp = [sb(f"Vp{p}", [128, 16, 128], BF16, s1) for p in range(3)]
                  acc = sb("acc", [128, 2, SEQ], F32, s1)
                  RL = sb("RL", [128, SEQ], F32, s1)
                  mixa = sb("mixa", [128, SEQ], BF16, s1)
                  PT = [sb(f"PT{i}", [128, 2, 2, 128], BF16, s1) for i in range(3)]

                  def toks(d, i):
                      L = SEQ // d
                      g = i * 128
                      r, l0 = g // L, g % L
                      s0 = r + d * l0
                      return slice(s0, s0 + d * 127 + 1, d)

                  for ui in range(8, 16):
                      hp = units[ui][1]
                      prefetch(ui + 1)
                      Wq, Wk_, Wv_, Wg_ = Wb[ui % 2]
                      wk = [f"Wb{ui % 2}_{i}" for i in range(4)]
                      for tb in range(4):
                          t0 = tb * 512
                          proj_block(Wq, wk[0], 0, t0)
                          act(qz[0:64, 0, t0:t0 + 512], ps[0][0:64, :], AF.Copy, ["ps0"], ["qz"], scale=0.125)
                          act(qz[64:128, 1, t0:t0 + 512], ps[0][64:128, :], AF.Copy, ["ps0"], ["qz"], scale=0.125)
                          proj_block(Wk_, wk[1], 1, t0)
                          cp(kT[:, t0:t0 + 512], ps[1][:, :], ["ps1"], ["kT"])
                          proj_block(Wv_, wk[2], 2, t0)
                          act(vTa[:, t0:t0 + 512], ps[2][:, :], AF.Copy, ["ps2"], ["vTa"])
                          proj_block(Wg_, wk[3], 3, t0)
                          act(sgT[:, t0:t0 + 512], ps[3][:, :], AF.Silu, ["ps3"], ["sgT"])
                      for p, d in enumerate(PATTERNS):
                          for half in range(2):
                              pi = 4 + (p * 2 + half) % 2
                              for ii in range(8):
                                  i = half * 8 + ii
                                  tr(psb[pi][:, ii * 128:(ii + 1) * 128], vTa[:, toks(d, i)], ident, ["vTa", "cst"],
                                     [f"ps{pi}"], inc=(ii == 7))
                              cp(Vp[p][:, half * 8:half * 8 + 8, :], psb[pi][:, :].rearrange("p (a b) -> p a b", b=128),
                                 [f"ps{pi}"], [f"Vp{p}"], eng=("act" if half else "dve"))
                      blocks = [(p, d, i) for p, d in enumerate(PATTERNS) for i in range(16)]
                      SB_, OB_ = (0, 1, 6), (2, 3, 7)
                      LOOK = 2

                      def kbs_of(d, i):
                          has_prev = ((i * 128) % (SEQ // d)) != 0
                          return ([(0, i - 1)] if has_prev else []) + [(1, i)]

                      def emit_scores(bi):
                          p, d, i = blocks[bi]
                          kbs = kbs_of(d, i)
                          sbk = SB_[bi % 3]
                          sv = ps[sbk][:, :].rearrange("p (j k q) -> p j k q", j=2, k=2)
                          pt = PT[bi % 3]
                          for j in range(2):
                              for (kbi, kt) in kbs:
                                  mm(sv[:, j, kbi, :], kT[:, toks(d, kt)], qz[:, j, toks(d, i)], ["kT", "qz"],
                                     [f"ps{sbk}"], inc=(j == 1 and kbi == 1))
                          k0 = kbs[0][0]
                          act(pt[:, :, k0:2, :], sv[:, :, k0:2, :], AF.Exp, [f"ps{sbk}"], [f"PT{bi % 3}"])
                          tt(pt[:, :, k0:2, :], pt[:, :, k0:2, :], maskT[:, :, k0:2, :], ALU.mult,
                             [f"PT{bi % 3}", "maskT"], [f"PT{bi % 3}"], eng="pool")

                      def emit_pv(bi):
                          p, d, i = blocks[bi]
                          kbs = kbs_of(d, i)
                          obk = OB_[bi % 3]
                          pt = PT[bi % 3]
                          ov = ps[obk][:, :].rearrange("p (r q) -> p r q", q=128)
                          for rgn in range(4):
                              j = rgn % 2
                              for n_, (kbi, kt) in enumerate(kbs):
                                  lhs = Vp[p][:, kt, :] if rgn < 2 else ones
                                  mm(ov[:, rgn, :], lhs, pt[:, j, kbi, :], [f"Vp{p}", "cst", f"PT{bi % 3}"], [f"ps{obk}"],
                                     start=(n_ == 0), stop=(n_ == len(kbs) - 1),
                                     inc=(rgn == 3 and n_ == len(kbs) - 1))
                          for j in range(2):
                              kp = slice(j * 64, j * 64 + 64)
                              src = ov[kp, j:4:2, :]
                              dst = acc[kp, :, toks(d, i)]
                              if p == 0:
                                  cp(dst, src, [f"ps{obk}"], [f"acc{j}"], eng=("act" if j else "dve"))
                              else:
                                  tt(dst, src, dst, ALU.add, [f"ps{obk}", f"acc{j}"], [f"acc{j}"])

                      for bi in range(min(LOOK, len(blocks))):
                          emit_scores(bi)
                      for bi in range(len(blocks)):
                          if bi + LOOK < len(blocks):
                              emit_scores(bi + LOOK)
                          emit_pv(bi)
                      act(RL[:, :], acc[:, 1, :], AF.Ln, ["acc0", "acc1"], ["RL"])
                      act(RL[:, :], RL[:, :], AF.Exp, ["RL"], ["RL"], scale=-1.0)
                      tt(acc[:, 0, :], acc[:, 0, :], RL[:, :], ALU.mult, ["acc0", "acc1", "RL"], ["acc0", "acc1"])
                      tt(mixa[:, :], acc[:, 0, :], sgT[:, :], ALU.mult, ["acc0", "acc1", "sgT"], ["mixa"])
                      dma(mixs[b, 8 + hp, :, :], mixa[:, :], ["mixa"], [f"mixs{8 + hp}"])
                      if ui == 8:
                          checkpoint("T1", mixa[:, :], "mixa", SEQ)
                  S.barrier()

              with ExitStack() as s1:
                  xt = [sb(f"xo{i}", [128, D], F32, s1) for i in range(2)]
                  MT = [sb(f"MT{i}", [128, 16, 128], BF16, s1) for i in range(2)]
                  hT = [sb(f"hT{i}", [128, D], F32, s1) for i in range(2)]
                  sqo = [sb(f"sqo{i}", [128, D], F32, s1) for i in range(2)]
                  oT = [sb(f"oT{i}", [128, D], F32, s1) for i in range(2)]
                  sso = [sb(f"sso{i}", [128, 2], F32, s1) for i in range(2)]
                  allmix = [f"mixs{m}" for m in range(16)]

                  def tileO(tti):
                      xi = tti % 2
                      pb0 = 4 + 2 * xi
                      tsl = slice(tti * 128, (tti + 1) * 128)
                      dma(xt[xi][:, :], x[b, tsl, :], (), [f"xo{xi}"])
                      for q4 in range(4):
                          dma(MT[xi][:, q4 * 4:q4 * 4 + 4, :], mixs[b, q4 * 4:q4 * 4 + 4, :, tsl].rearrange("m p t -> p m t"),
                              allmix[q4 * 4:q4 * 4 + 4], [f"MT{xi}/{q4}"])
                      for half in range(2):
                          for mt in range(16):
                              mm(ps[pb0 + half][:, :], MT[xi][:, mt, :], WO[:, mt, half * 512:(half + 1) * 512],
                                 [f"MT{xi}", "WO"], [f"ps{pb0 + half}"], start=(mt == 0), stop=(mt == 15), inc=(mt == 15))
                          tt(hT[xi][:, half * 512:(half + 1) * 512], ps[pb0 + half][:, :],
                             xt[xi][:, half * 512:(half + 1) * 512], ALU.add, [f"ps{pb0 + half}", f"xo{xi}"], [f"hT{xi}"])
                      act(sqo[xi][:, :], hT[xi][:, :], AF.Square, [f"hT{xi}"], [f"sqo{xi}"])
                      red(sso[xi][:, 0:1], sqo[xi][:, :], [f"sqo{xi}"], [f"sso{xi}"])
                      act(sso[xi][:, 1:2], sso[xi][:, 0:1], AF.Ln, [f"sso{xi}"], [f"ssp{xi}"], scale=1.0 / D, bias=RMS_EPS)
                      act(sso[xi][:, 1:2], sso[xi][:, 1:2], AF.Exp, [f"ssp{xi}"], [f"ssp{xi}"], scale=-0.5)
                      stt(oT[xi][:, :], hT[xi][:, :], sso[xi][:, 1:2], fnw[:, :], ALU.mult, ALU.mult,
                          [f"hT{xi}", f"ssp{xi}", "fnw"], [f"oT{xi}"])
                      dma(y[b, tsl, :], oT[xi][:, :], [f"oT{xi}"], [f"y{b}_{tti}"])

                  for tti in range(0, 16, 2):
                      merge2(rec(tileO, tti), rec(tileO, tti + 1))
                  S.barrier()
          except _Stop:
            break
        S.barrier()
        print(f"[kernel] ops={S.nops} waits={S.nwaits} cnt={S.cnt}")
    return nc


def _consts():
    ident = np.eye(128, dtype=np.float32)
    blockones = np.zeros((128, 128), np.float32)
    blockones[0:64, 0:64] = 1.0
    blockones[64:128, 64:128] = 1.0
    ones = np.ones((128, 128), np.float32)
    cst = np.concatenate([ident, blockones, ones], axis=1)
    s = np.arange(64)[:, None]
    t = np.arange(64)[None, :]
    maskA = np.concatenate([(s < t), (s <= t)], axis=1).astype(np.float32)
    maskN = (np.arange(64)[:, None] > np.arange(64)[None, :]).astype(np.float32)
    k = np.arange(128)[:, None]
    q = np.arange(128)[None, :]
    mprev = (k >= q).astype(np.float32)
    mdiag = (k <= q).astype(np.float32)
    msk = np.zeros((128, 448), np.float32)
    msk[0:64, 0:128] = maskA
    msk[0:64, 128:192] = maskN
    msk[:, 192:320] = mprev
    msk[:, 320:448] = mdiag
    rst = np.ones((128, 512), np.float32)
    rst[:, 0::64] = 0.0
    return cst, msk, rst


_NC_CACHE = {}


def kernel(x, norm_w, w_in, mu_shift, w0, w_up, a0, a_up, k_k, k_a, r_k, ln_x_w, ln_x_b, w_out, final_norm_w):
    f = lambda a: np.ascontiguousarray(np.asarray(a, dtype=np.float32))
    x = f(x)
    cst, msk, rst = _consts()
    pc = lambda v: f(v).reshape(8, 128).T
    chv = np.ascontiguousarray(np.stack([pc(w0), pc(a0), pc(k_k), pc(k_a), pc(np.asarray(r_k).reshape(-1)),
                                         pc(ln_x_w), pc(ln_x_b)], axis=1).reshape(128, 56))
    mu = np.ascontiguousarray(f(mu_shift).reshape(25, 128).T)
    normw = np.ascontiguousarray(f(norm_w).reshape(8, 128).T)
    fnw = np.ascontiguousarray(np.broadcast_to(f(final_norm_w)[None, :], (128, D)))
    lora_up = np.ascontiguousarray(np.concatenate([f(w_up), f(a_up)], axis=0))
    shared = {"w_in": f(w_in), "w_out": f(w_out), "lora_up": lora_up, "chv": chv, "mu": mu, "normw": normw,
              "fnw": fnw, "cst": cst, "msk": msk, "rst": rst}
    if "nc" not in _NC_CACHE:
        _NC_CACHE["nc"] = build()
    nc = _NC_CACHE["nc"]
    in_maps = [dict(shared, x=np.ascontiguousarray(x[c * NB:(c + 1) * NB])) for c in range(NCORES)]
    res = run_bass_kernel_spmd(nc, in_maps, core_ids=list(range(NCORES)))
    return np.concatenate([r["y"] for r in res.results], axis=0).astype(np.float32)
```
